# Optimizing a Trainium2 kernel written in Bass

```python
import math
import jax
import jax.numpy as jnp
from jax import lax
import numpy as np

D_MODEL = 1024
BATCH = 16
SEQ = 2048
DEPTH = 1

PLE_DIM = 256
NSA_HEADS = 8
NSA_KV_GROUPS = 2
NSA_REP = NSA_HEADS // NSA_KV_GROUPS
NSA_HEAD_DIM = 64
NSA_WIDTH = NSA_HEADS * NSA_HEAD_DIM
CMP_BLOCK = 32
CMP_STRIDE = 16
SEL_BLOCK = 64
N_SEL = 16
WINDOW = 512
SEL_Q_CHUNK = 16
MLA_HEADS = 8
Q_LORA = 256
KV_LORA = 128
QK_NOPE = 64
QK_ROPE = 32
V_DIM = 64
MLA_WIDTH = MLA_HEADS * V_DIM
ROPE_THETA = 10000.0
MAX_POS_OFFSET = 4096
D_FF = 2816
CONV_WIDTH = 3
Q_BLOCK = 128
EPS = 1e-6
NEG = -1e30
BIG = 1e9
IN_SIZES = (NSA_WIDTH, 6 * NSA_KV_GROUPS * NSA_HEAD_DIM, 3 * NSA_HEADS, Q_LORA, KV_LORA, QK_ROPE, 2 * D_MODEL)
N_IN = NSA_WIDTH + 6 * NSA_KV_GROUPS * NSA_HEAD_DIM + 3 * NSA_HEADS + Q_LORA + KV_LORA + QK_ROPE + 2 * D_MODEL

kernel_name = 'hybrid_nsa_mla_gated_convffn_ple'


def _rmsnorm(x, g):
    xf = x.astype(jnp.float32)
    y = xf * lax.rsqrt(jnp.mean(xf * xf, axis=-1, keepdims=True) + EPS)
    return (y * g.astype(jnp.float32)).astype(x.dtype)


def _masked_softmax(s, mask):
    p = jax.nn.softmax(jnp.where(mask, s, NEG), axis=-1)
    return p * mask.astype(p.dtype)


def _alibi_slopes(n):
    return jnp.asarray(np.array([2.0 ** (-8.0 * (i + 1) / n) for i in range(n)], dtype=np.float32))


def _rope(x, cos, sin):
    half = x.shape[-1] // 2
    x1, x2 = x[..., :half], x[..., half:]
    return jnp.concatenate([x1 * cos - x2 * sin, x2 * cos + x1 * sin], axis=-1)


def _causal_dwconv(u, w, b):
    c = u.shape[-1]
    y = lax.conv_general_dilated(u, w[:, None, :].astype(u.dtype), window_strides=(1,), padding=[(CONV_WIDTH - 1, 0)], dimension_numbers=('NWC', 'WIO', 'NWC'), feature_group_count=c)
    return y + b


def _nsa_mixer(q, kc, vc, ks, vs, kw, vw, gates, pos_k, pos_v, ck_w1, ck_b1, ck_w2, cv_w1, cv_b1, cv_w2):
    B, T = q.shape[0], q.shape[1]
    G, R, dh = NSA_KV_GROUPS, NSA_REP, NSA_HEAD_DIM
    scale = dh ** -0.5
    slopes = _alibi_slopes(NSA_HEADS).reshape(G, R)
    t_idx = jnp.arange(T)

    n_cmp = (T - CMP_BLOCK) // CMP_STRIDE + 1
    blk = jnp.arange(n_cmp)[:, None] * CMP_STRIDE + jnp.arange(CMP_BLOCK)[None, :]

    def compress(k, pos, w1, b1, w2):
        kb = k[:, blk] + pos[None, None, :, None, :]
        kb = kb.transpose(0, 1, 3, 2, 4).reshape(B, n_cmp, G, CMP_BLOCK * dh)
        return jax.nn.gelu(kb @ w1 + b1) @ w2

    k_cmp = compress(kc, pos_k, ck_w1, ck_b1, ck_w2)
    v_cmp = compress(vc, pos_v, cv_w1, cv_b1, cv_w2)
    blk_end = jnp.arange(n_cmp) * CMP_STRIDE + CMP_BLOCK - 1
    dist_c = (t_idx[:, None] - blk_end[None, :]).astype(jnp.float32)
    s_c = jnp.einsum('btgrd,bjgd->bgrtj', q, k_cmp, preferred_element_type=jnp.float32) * scale
    s_c = s_c - slopes[:, :, None, None] * dist_c
    p_cmp = _masked_softmax(s_c, dist_c >= 0)
    o_cmp = jnp.einsum('bgrtj,bjgd->btgrd', p_cmp.astype(v_cmp.dtype), v_cmp)

    n_sb = T // SEL_BLOCK
    per = SEL_BLOCK // CMP_STRIDE
    imp = p_cmp.sum(axis=2)
    imp = jnp.pad(imp, ((0, 0), (0, 0), (0, 0), (0, n_sb * per - n_cmp)))
    imp = imp.reshape(B, G, T, n_sb, per).sum(axis=-1)
    cur = t_idx // SEL_BLOCK
    sb = jnp.arange(n_sb)
    forced = (sb[None, :] == 0) | (sb[None, :] == cur[:, None]) | (sb[None, :] == cur[:, None] - 1)
    future = sb[None, :] > cur[:, None]
    score = jnp.where(forced, BIG, jnp.where(future, -BIG, imp))
    k_sel = min(N_SEL, n_sb)
    _, sel_idx = lax.top_k(score, k_sel)

    ksb = ks.reshape(B, n_sb, SEL_BLOCK, G, dh).transpose(0, 3, 1, 2, 4)
    vsb = vs.reshape(B, n_sb, SEL_BLOCK, G, dh).transpose(0, 3, 1, 2, 4)
    C = SEL_Q_CHUNK
    n_ch = T // C
    q_ch = q.reshape(B, n_ch, C, G, R, dh).transpose(1, 0, 2, 3, 4, 5)
    idx_ch = sel_idx.reshape(B, G, n_ch, C, k_sel).transpose(2, 0, 1, 3, 4)
    t_ch = t_idx.reshape(n_ch, C)
    gather = jax.vmap(jax.vmap(lambda kb, ix: kb[ix]))

    def sel_step(args):
        qc, ic, tc = args
        kg = gather(ksb, ic)
        vg = gather(vsb, ic)
        s = jnp.einsum('bcgrd,bgckld->bgrckl', qc, kg, preferred_element_type=jnp.float32) * scale
        spos = ic[..., None] * SEL_BLOCK + jnp.arange(SEL_BLOCK)
        dist = (tc[None, None, :, None, None] - spos).astype(jnp.float32)
        s = s - slopes[None, :, :, None, None, None] * dist[:, :, None]
        s = s.reshape(B, G, R, C, k_sel * SEL_BLOCK)
        mask = (dist >= 0).reshape(B, G, 1, C, k_sel * SEL_BLOCK)
        pr = _masked_softmax(s, mask).reshape(B, G, R, C, k_sel, SEL_BLOCK)
        return jnp.einsum('bgrckl,bgckld->bcgrd', pr.astype(vg.dtype), vg)

    o_sel = lax.map(sel_step, (q_ch, idx_ch, t_ch))
    o_sel = o_sel.transpose(1, 0, 2, 3, 4, 5).reshape(B, T, G, R, dh)

    n_qb = T // Q_BLOCK
    span = WINDOW + Q_BLOCK
    kw_p = jnp.pad(kw, ((0, 0), (WINDOW, 0), (0, 0), (0, 0)))
    vw_p = jnp.pad(vw, ((0, 0), (WINDOW, 0), (0, 0), (0, 0)))
    q_blk = q.reshape(B, n_qb, Q_BLOCK, G, R, dh).transpose(1, 0, 2, 3, 4, 5)

    def win_step(args):
        qb, i = args
        start = i * Q_BLOCK
        kblk = lax.dynamic_slice_in_dim(kw_p, start, span, axis=1)
        vblk = lax.dynamic_slice_in_dim(vw_p, start, span, axis=1)
        tq = start + jnp.arange(Q_BLOCK)
        sk = start - WINDOW + jnp.arange(span)
        dist = (tq[:, None] - sk[None, :]).astype(jnp.float32)
        mask = (dist >= 0) & (dist < WINDOW) & (sk[None, :] >= 0)
        s = jnp.einsum('bqgrd,bsgd->bgrqs', qb, kblk, preferred_element_type=jnp.float32) * scale
        s = s - slopes[:, :, None, None] * dist
        pr = _masked_softmax(s, mask)
        return jnp.einsum('bgrqs,bsgd->bqgrd', pr.astype(vblk.dtype), vblk)

    o_win = lax.map(win_step, (q_blk, jnp.arange(n_qb, dtype=jnp.int32)))
    o_win = o_win.transpose(1, 0, 2, 3, 4, 5).reshape(B, T, G, R, dh)

    g = jax.nn.sigmoid(gates.astype(jnp.float32)).astype(q.dtype).reshape(B, T, 3, G, R, 1)
    o = g[:, :, 0] * o_cmp + g[:, :, 1] * o_sel + g[:, :, 2] * o_win
    return o.reshape(B, T, NSA_WIDTH)


def _mla_mixer(c_q, c_kv, k_rope, cos, sin, g_q, w_uq, g_kv, w_ukv):
    B, T = c_q.shape[0], c_q.shape[1]
    H = MLA_HEADS
    q = (_rmsnorm(c_q, g_q) @ w_uq).reshape(B, T, H, QK_NOPE + QK_ROPE)
    q_nope = q[..., :QK_NOPE]
    q_rope = _rope(q[..., QK_NOPE:], cos[:, :, None, :], sin[:, :, None, :])
    kv = (_rmsnorm(c_kv, g_kv) @ w_ukv).reshape(B, T, H, QK_NOPE + V_DIM)
    k_nope = kv[..., :QK_NOPE]
    v = kv[..., QK_NOPE:]
    k_r = _rope(k_rope, cos, sin)
    scale = (QK_NOPE + QK_ROPE) ** -0.5
    n_qb = T // Q_BLOCK
    qn_b = q_nope.reshape(B, n_qb, Q_BLOCK, H, QK_NOPE).transpose(1, 0, 2, 3, 4)
    qr_b = q_rope.reshape(B, n_qb, Q_BLOCK, H, QK_ROPE).transpose(1, 0, 2, 3, 4)
    s_idx = jnp.arange(T)

    def step(args):
        qn, qr, i = args
        s = (jnp.einsum('bqhd,bshd->bhqs', qn, k_nope, preferred_element_type=jnp.float32) + jnp.einsum('bqhd,bsd->bhqs', qr, k_r, preferred_element_type=jnp.float32)) * scale
        tq = i * Q_BLOCK + jnp.arange(Q_BLOCK)
        pr = _masked_softmax(s, s_idx[None, :] <= tq[:, None])
        return jnp.einsum('bhqs,bshd->bqhd', pr.astype(v.dtype), v)

    o = lax.map(step, (qn_b, qr_b, jnp.arange(n_qb, dtype=jnp.int32)))
    return o.transpose(1, 0, 2, 3, 4).reshape(B, T, MLA_WIDTH)


def setup_inputs(seed: int = 0) -> dict:
    key = jax.random.key(seed)
    ks = jax.random.split(key, 40)
    f32 = jnp.float32

    def nrm(k, shape, fan_in):
        return jax.random.normal(k, shape, f32) * (fan_in ** -0.5)

    def gain(k, n):
        return 1.0 + 0.05 * jax.random.normal(k, (DEPTH, n), f32)

    def small(k, shape, s):
        return s * jax.random.normal(k, shape, f32)

    L = DEPTH
    dh = NSA_HEAD_DIM
    offs = jax.random.randint(ks[2], (BATCH, 1), 0, MAX_POS_OFFSET, dtype=jnp.int32)
    return {
        'x': jax.random.normal(ks[0], (BATCH, SEQ, D_MODEL), f32),
        'p': jax.random.normal(ks[1], (DEPTH, BATCH, SEQ, PLE_DIM), f32),
        'positions': offs + jnp.arange(SEQ, dtype=jnp.int32)[None, :],
        'g_pre_mix': gain(ks[3], D_MODEL),
        'w_in': nrm(ks[4], (L, D_MODEL, N_IN), D_MODEL),
        'nsa_pos_k': small(ks[5], (L, CMP_BLOCK, dh), 0.1),
        'nsa_pos_v': small(ks[6], (L, CMP_BLOCK, dh), 0.1),
        'nsa_ck_w1': nrm(ks[7], (L, CMP_BLOCK * dh, dh), CMP_BLOCK * dh),
        'nsa_ck_b1': small(ks[8], (L, dh), 0.02),
        'nsa_ck_w2': nrm(ks[9], (L, dh, dh), dh),
        'nsa_cv_w1': nrm(ks[10], (L, CMP_BLOCK * dh, dh), CMP_BLOCK * dh),
        'nsa_cv_b1': small(ks[11], (L, dh), 0.02),
        'nsa_cv_w2': nrm(ks[12], (L, dh, dh), dh),
        'mla_g_q': gain(ks[13], Q_LORA),
        'mla_w_uq': nrm(ks[14], (L, Q_LORA, MLA_HEADS * (QK_NOPE + QK_ROPE)), Q_LORA),
        'mla_g_kv': gain(ks[15], KV_LORA),
        'mla_w_ukv': nrm(ks[16], (L, KV_LORA, MLA_HEADS * (QK_NOPE + V_DIM)), KV_LORA),
        'w_br_nsa': nrm(ks[17], (L, NSA_WIDTH, D_MODEL), NSA_WIDTH),
        'w_br_mla': nrm(ks[18], (L, MLA_WIDTH, D_MODEL), MLA_WIDTH),
        'w_o': nrm(ks[19], (L, D_MODEL, D_MODEL), D_MODEL),
        'g_post_mix': gain(ks[20], D_MODEL),
        'g_pre_ffn': gain(ks[21], D_MODEL),
        'w_up': nrm(ks[22], (L, D_MODEL, 2 * D_FF), D_MODEL),
        'w_conv': nrm(ks[23], (L, CONV_WIDTH, 2 * D_FF), CONV_WIDTH),
        'b_conv': small(ks[24], (L, 2 * D_FF), 0.02),
        'w_down': nrm(ks[25], (L, D_FF, D_MODEL), D_FF),
        'g_post_ffn': gain(ks[26], D_MODEL),
        'w_ple': nrm(ks[27], (L, PLE_DIM, D_MODEL), PLE_DIM),
        'w_ple_gate': nrm(ks[28], (L, D_MODEL, D_MODEL), D_MODEL),
        'g_ple': gain(ks[29], D_MODEL),
    }


def reference(x, p, positions, g_pre_mix, w_in, nsa_pos_k, nsa_pos_v, nsa_ck_w1, nsa_ck_b1, nsa_ck_w2, nsa_cv_w1, nsa_cv_b1, nsa_cv_w2, mla_g_q, mla_w_uq, mla_g_kv, mla_w_ukv, w_br_nsa, w_br_mla, w_o, g_post_mix, g_pre_ffn, w_up, w_conv, b_conv, w_down, g_post_ffn, w_ple, w_ple_gate, g_ple):
    B, T, D = x.shape
    G, R, dh = NSA_KV_GROUPS, NSA_REP, NSA_HEAD_DIM
    offsets = [int(v) for v in np.cumsum(np.array(IN_SIZES))[:-1]]
    inv_freq = ROPE_THETA ** (-jnp.arange(0, QK_ROPE, 2, dtype=jnp.float32) / QK_ROPE)
    ang = positions.astype(jnp.float32)[..., None] * inv_freq
    cos = jnp.cos(ang).astype(x.dtype)
    sin = jnp.sin(ang).astype(x.dtype)
    for i in range(DEPTH):
        h = _rmsnorm(x, g_pre_mix[i])
        proj = h @ w_in[i]
        q_n, kv_n, gate_n, c_q, c_kv, k_r, merge = jnp.split(proj, offsets, axis=-1)
        q_n = q_n.reshape(B, T, G, R, dh)
        kv_n = kv_n.reshape(B, T, 6, G, dh)
        o_nsa = _nsa_mixer(q_n, kv_n[:, :, 0], kv_n[:, :, 1], kv_n[:, :, 2], kv_n[:, :, 3], kv_n[:, :, 4], kv_n[:, :, 5], gate_n, nsa_pos_k[i], nsa_pos_v[i], nsa_ck_w1[i], nsa_ck_b1[i], nsa_ck_w2[i], nsa_cv_w1[i], nsa_cv_b1[i], nsa_cv_w2[i])
        o_mla = _mla_mixer(c_q, c_kv, k_r, cos, sin, mla_g_q[i], mla_w_uq[i], mla_g_kv[i], mla_w_ukv[i])
        gm = jax.nn.sigmoid(merge.astype(jnp.float32)).astype(x.dtype).reshape(B, T, 2, D)
        merged = gm[:, :, 0] * (o_nsa @ w_br_nsa[i]) + gm[:, :, 1] * (o_mla @ w_br_mla[i])
        x = x + _rmsnorm(merged @ w_o[i], g_post_mix[i])
        h = _rmsnorm(x, g_pre_ffn[i])
        u = _causal_dwconv(h @ w_up[i], w_conv[i], b_conv[i])
        a, v = u[..., :D_FF], u[..., D_FF:]
        x = x + _rmsnorm((jax.nn.gelu(a) * v) @ w_down[i], g_post_ffn[i])
        e = p[i] @ w_ple[i]
        gate = jax.nn.sigmoid((x @ w_ple_gate[i]).astype(jnp.float32)).astype(x.dtype)
        x = x + _rmsnorm(e * gate, g_ple[i])
    return x
```

```python
import os
import numpy as np
from contextlib import ExitStack
import concourse.bass as bass
import concourse.mybir as mybir
from concourse.bass_utils import run_bass_kernel_spmd

ACT = mybir.ActivationFunctionType
ALU = mybir.AluOpType
AX = mybir.AxisListType
F32 = mybir.dt.float32
BF16 = mybir.dt.bfloat16
I32 = mybir.dt.int32

NEG = -30000.0
BIG = 1.0e9
EPS = 1e-6
T = 2048
D = 1024
NCORES = 8
DFF = 2816
SC_N = 0.125
SC_M = 96.0 ** -0.5
TWO_PI = 6.283185307179586
PI = 3.141592653589793


_UNIQ = [0]


def _sbt(nc, name, shape, dt):
    _UNIQ[0] += 1
    return nc.sbuf_tensor(f"{name}_u{_UNIQ[0]}", shape, dt)


class Buf:
    __slots__ = ("w", "r", "x")

    def __init__(self, x=False):
        self.w = None
        self.r = {}
        self.x = x


class Sched:
    NDS = 24

    def __init__(self, nc, stack):
        self.nc = nc
        self.engs = {"pe": nc.tensor, "act": nc.scalar, "dve": nc.vector, "pool": nc.gpsimd, "sp": nc.sync}
        self.sem = {k: stack.enter_context(nc.semaphore(k + "_s")) for k in self.engs}
        self.cnt = {k: 0 for k in self.engs}
        self.seen = {k: {} for k in self.engs}
        self.dsem = [stack.enter_context(nc.semaphore(f"dq{i}")) for i in range(self.NDS)]
        self.dcnt = [0] * self.NDS
        self.dnext = 0

    def _semof(self, k):
        return self.dsem[k] if isinstance(k, int) else self.sem[k]

    def _waits(self, eng, reads, writes):
        deps = {}
        for b in reads:
            if b.w is not None and deps.get(b.w[0], 0) < b.w[1]:
                deps[b.w[0]] = b.w[1]
            if b.x:
                for k, v in b.r.items():
                    if k != eng and deps.get(k, 0) < v:
                        deps[k] = v
        for b in writes:
            if b.w is not None and deps.get(b.w[0], 0) < b.w[1]:
                deps[b.w[0]] = b.w[1]
            for k, v in b.r.items():
                if deps.get(k, 0) < v:
                    deps[k] = v
        e = self.engs[eng]
        for k, v in deps.items():
            if k == eng and eng in ("pe", "sp"):
                continue
            if self.seen[eng].get(k, 0) >= v:
                continue
            e.wait_ge(self._semof(k), v)
            self.seen[eng][k] = v

    def _mark(self, key, val, reads, writes):
        for b in writes:
            b.w = (key, val)
            b.r = {}
        for b in reads:
            if b not in writes:
                b.r[key] = val

    def op(self, eng, fn, reads=(), writes=()):
        self._waits(eng, reads, writes)
        ins = fn(self.engs[eng])
        self.cnt[eng] += 1
        ins.then_inc(self.sem[eng], 1)
        self._mark(eng, self.cnt[eng], reads, writes)

    def dma(self, out, in_, reads=(), writes=(), eng="sp", **kw):
        i = self.dnext
        self.dnext = (self.dnext + 1) % self.NDS
        e = self.engs[eng]
        if self.dcnt[i] > 0 and self.seen[eng].get(i, 0) < self.dcnt[i]:
            e.wait_ge(self.dsem[i], self.dcnt[i])
            self.seen[eng][i] = self.dcnt[i]
        self._waits(eng, reads, writes)
        ins = e.dma_start(out=out, in_=in_, **kw)
        self.dcnt[i] += 16
        ins.then_inc(self.dsem[i], 16)
        self._mark(i, self.dcnt[i], reads, writes)

    def barrier(self):
        for en, e in self.engs.items():
            for k in self.engs:
                if k != en and self.cnt[k] > self.seen[en].get(k, 0):
                    e.wait_ge(self.sem[k], self.cnt[k])
                    self.seen[en][k] = self.cnt[k]
            for i in range(self.NDS):
                if self.dcnt[i] > self.seen[en].get(i, 0):
                    e.wait_ge(self.dsem[i], self.dcnt[i])
                    self.seen[en][i] = self.dcnt[i]


def make_consts():
    c = {}
    c["ident"] = np.eye(128, dtype=np.float32)
    ds = np.arange(128)[:, None]
    dt = np.arange(128)[None, :]
    negc = np.where(ds <= dt, 0.0, NEG).astype(np.float32)
    negw = np.where(ds > dt, 0.0, NEG).astype(np.float32)
    c["negc"] = np.tile(negc, (1, 4))
    c["negw"] = np.tile(negw, (1, 4))
    dt5 = np.arange(512)[None, :]
    c["negm"] = np.stack([np.where(ds + 128 * m <= dt5, 0.0, NEG) for m in range(4)], 0).astype(np.float32)
    s = np.arange(T)
    E = (s[None, :] // 64 == np.arange(32)[:, None]).astype(np.float32)
    kal = np.stack([np.ones(T), np.ones(T), s // 128, s % 128], 0).astype(np.float32)
    c["kaug_sel"] = np.concatenate([E, kal], 0)
    c["kaug_win"] = np.concatenate([np.zeros_like(E), kal], 0)
    pj = 16 * np.arange(128) + 31
    kc = np.stack([np.ones(128), np.ones(128), pj // 128, pj % 128], 0).astype(np.float32)
    kc[:, 127] = 0.0
    c["kaug_cmp"] = np.concatenate([np.zeros((32, 128), np.float32), kc], 0)
    slopes = np.array([2.0 ** (-(h + 1)) for h in range(8)], dtype=np.float64)
    qa = np.zeros((2, 36, 16, 4, 128), np.float32)
    for g in range(2):
        for r in range(4):
            sl = slopes[4 * g + r]
            for i in range(16):
                qa[g, 32, i, r, :] = -sl * 128.0 * i
                qa[g, 33, i, r, :] = -sl * np.arange(128)
                qa[g, 34, i, r, :] = sl * 128.0
                qa[g, 35, i, r, :] = sl
    c["qaug"] = qa.reshape(2, 36, 16 * 4 * 128)
    selg = np.zeros((24, 24, 64), np.float32)
    for r in range(24):
        selg[r, r, :] = 1.0
    c["selg"] = selg.reshape(24, 24 * 64)
    cc = np.arange(128)[None, :] - 120
    dtt = np.arange(128)[:, None]
    c["cmptbl"] = np.where(16 * cc + 31 <= dtt, 0.0, NEG).astype(np.float32)
    v12 = np.zeros((128, 2), np.float32)
    v12[:, 0] = np.where(np.arange(128) < 64, -BIG, BIG)
    v12[:, 1] = np.where(np.arange(128) < 64, BIG, 0.0)
    c["v12"] = v12
    invf = (np.float32(10000.0) ** (-np.arange(0, 32, 2, dtype=np.float32) / np.float32(32))).astype(np.float32)
    c["invf"] = np.concatenate([invf, invf])[:, None].astype(np.float32)
    return c


WSHAPES = {
    "g_pre_mix": [1, D], "g_post_mix": [1, D], "g_pre_ffn": [1, D], "g_post_ffn": [1, D], "g_ple": [1, D],
    "w_in": [128, 8, 3768], "posT_k": [64, 32], "posT_v": [64, 32],
    "ck_w1": [64, 32, 64], "cv_w1": [64, 32, 64], "ck_b1": [64, 1], "cv_b1": [64, 1],
    "ck_w2": [64, 64], "cv_w2": [64, 64],
    "g_q": [128, 2], "w_uq": [128, 2, 768], "g_kv": [128, 1], "w_ukv": [128, 1024],
    "w_br_nsa": [128, 4, D], "w_br_mla": [128, 4, D], "w_o": [128, 8, D],
    "w_up": [128, 8, 2 * DFF], "w_conv": [128, 3, 44], "b_conv": [128, 44], "w_down": [128, 22, D],
    "w_ple": [128, 2, D], "w_ple_gate": [128, 8, D],
}


def host_weights(inp):
    w = {}
    f = lambda a: np.ascontiguousarray(a, dtype=np.float32)
    for k in ("g_pre_mix", "g_post_mix", "g_pre_ffn", "g_post_ffn", "g_ple"):
        w[k] = f(inp[k][0][None, :])
    kcp = lambda a, p=128: f(a.reshape(a.shape[0] // p, p, a.shape[1]).transpose(1, 0, 2))
    w["w_in"] = kcp(inp["w_in"][0])
    w["posT_k"] = f(inp["nsa_pos_k"][0].T)
    w["posT_v"] = f(inp["nsa_pos_v"][0].T)
    w["ck_w1"] = f(inp["nsa_ck_w1"][0].reshape(32, 64, 64).transpose(1, 0, 2))
    w["cv_w1"] = f(inp["nsa_cv_w1"][0].reshape(32, 64, 64).transpose(1, 0, 2))
    w["ck_b1"] = f(inp["nsa_ck_b1"][0][:, None])
    w["cv_b1"] = f(inp["nsa_cv_b1"][0][:, None])
    w["ck_w2"] = f(inp["nsa_ck_w2"][0])
    w["cv_w2"] = f(inp["nsa_cv_w2"][0])
    w["g_q"] = f(inp["mla_g_q"][0].reshape(2, 128).T)
    w["w_uq"] = kcp(inp["mla_w_uq"][0])
    w["g_kv"] = f(inp["mla_g_kv"][0][:, None])
    w["w_ukv"] = f(inp["mla_w_ukv"][0])
    w["w_br_nsa"] = kcp(inp["w_br_nsa"][0])
    w["w_br_mla"] = kcp(inp["w_br_mla"][0])
    w["w_o"] = kcp(inp["w_o"][0])
    w["w_up"] = kcp(inp["w_up"][0])
    w["w_conv"] = f(inp["w_conv"][0].reshape(3, 44, 128).transpose(2, 0, 1))
    w["b_conv"] = f(inp["b_conv"][0].reshape(44, 128).T)
    w["w_down"] = kcp(inp["w_down"][0])
    w["w_ple"] = kcp(inp["w_ple"][0])
    w["w_ple_gate"] = kcp(inp["w_ple_gate"][0])
    return w


def build(stages=("mix", "ffn", "ple"), dbg=False, nseq=2):
    nc = bass.Bass("TRN2", target_bir_lowering=False)
    consts = make_consts()
    A = {}
    A["x"] = nc.dram_tensor("x", [2 * T, D], F32, kind="ExternalInput").ap()
    A["p"] = nc.dram_tensor("p", [2 * T, 256], F32, kind="ExternalInput").ap()
    A["pos"] = nc.dram_tensor("pos", [2, T], I32, kind="ExternalInput").ap()
    for k, shp in WSHAPES.items():
        A[k] = nc.dram_tensor(k, shp, F32, kind="ExternalInput").ap()
    for k, v in consts.items():
        A[k] = nc.dram_tensor("c_" + k, list(v.shape), F32, kind="ExternalInput").ap()
    A["out"] = nc.dram_tensor("out", [2 * T, D], F32, kind="ExternalOutput").ap()
    if "mix" in stages:
        A["x1s"] = nc.dram_tensor("x1s", [2 * T, D], F32, kind="ExternalOutput" if dbg else "Internal").ap()
    else:
        A["x1s"] = nc.dram_tensor("x1s", [2 * T, D], F32, kind="ExternalInput").ap()
    if dbg:
        A["d_onsa"] = nc.dram_tensor("d_onsa", [128, 4, T], F32, kind="ExternalOutput").ap()
        A["d_omla"] = nc.dram_tensor("d_omla", [128, 4, T], F32, kind="ExternalOutput").ap()
    x1b = [Buf() for _ in range(32)]

    with ExitStack() as st:
        S = Sched(nc, st)
        sbg = lambda name, shape, dt: st.enter_context(_sbt(nc, name, shape, dt))
        PS = [st.enter_context(nc.psum_tensor(f"ps{i}", [128, 512], F32)) for i in range(7)]
        PSb = [Buf(True) for _ in range(7)]
        PB = st.enter_context(nc.psum_tensor("pb", [128, 1024], BF16))
        PBb = Buf(True)
        ident = sbg("ident", [128, 128], BF16)
        bid = Buf()
        S.dma(ident[:], A["ident"], writes=[bid], eng="pool")

        if "mix" in stages:
            for sq in range(nseq):
                stage_mix(nc, S, A, sq, PS, PSb, PB, PBb, ident, bid, x1b, dbg and sq == 0)
                S.barrier()
        if "ffn" in stages:
            stage_ffn(nc, S, A, PS, PSb, PB, PBb, ident, bid, x1b, nseq)
            S.barrier()
        if "ple" in stages:
            stage_ple(nc, S, A, PS, PSb, PB, PBb, ident, bid, x1b, nseq)
        S.barrier()
        print('COUNTS', S.cnt, max(S.dcnt))
    return nc


def rstd_from_ss(S, ss_ap, rstd_ap, n, rb, wb):
    S.op("act", lambda e: e.activation(out=rstd_ap, in_=ss_ap, func=ACT.Sqrt, scale=1.0 / n, bias=EPS), reads=rb, writes=wb)
    S.op("dve", lambda e: e.reciprocal(out=rstd_ap, in_=rstd_ap), reads=wb, writes=wb)


def stage_ffn(nc, S, A, PS, PSb, PB, PBb, ident, bid, x1b, nseq):
    with ExitStack() as s2:
        sb = lambda name, shape, dt: s2.enter_context(_sbt(nc, name, shape, dt))
        wup = sb("wup", [128, 8, 2 * DFF], BF16)
        bwup = [Buf() for _ in range(11)]
        wdn = sb("wdn", [128, 22, D], BF16)
        bwdn = [Buf() for _ in range(2)]
        wc = sb("wc", [128, 3, 44], F32)
        bc = sb("bc", [128, 44], F32)
        bwc = Buf()
        gpre = sb("gpre", [128, D], F32)
        gpost = sb("gpost", [128, D], F32)
        bg = Buf()
        S.dma(wc[:], A["w_conv"], writes=[bwc])
        S.dma(bc[:], A["b_conv"], writes=[bwc])
        S.dma(gpre[:], A["g_pre_ffn"][0:1, :].to_broadcast([128, D]), writes=[bg])
        S.dma(gpost[:], A["g_post_ffn"][0:1, :].to_broadcast([128, D]), writes=[bg])
        order = []
        for q in range(6):
            order.append(q)
            if q + 6 < 11:
                pass
        for q in range(11):
            S.dma(wup[:, :, q * 512:(q + 1) * 512], A["w_up"][:, :, q * 512:(q + 1) * 512], writes=[bwup[q]], eng="pool")
        for q in range(2):
            S.dma(wdn[:, q * 11:(q + 1) * 11, :], A["w_down"][:, q * 11:(q + 1) * 11, :], writes=[bwdn[q]], eng="pool")
        x1t = [sb(f"x1t{i}", [128, D], F32) for i in range(4)]
        bx1t = [Buf() for _ in range(4)]
        hn = [sb(f"hn{i}", [128, D], BF16) for i in range(2)]
        bhn = [Buf() for _ in range(2)]
        h2T = [sb(f"h2T{i}", [128, 8, 258], BF16) for i in range(2)]
        bh2T = [Buf() for _ in range(2)]
        junk = sb("junk", [128, D], BF16)
        bjunk = Buf()
        ssq = sb("ssq", [128, 8], F32)
        bss = [Buf() for _ in range(4)]
        cA = [sb(f"cA{i}", [128, 256], F32) for i in range(2)]
        cV = [sb(f"cV{i}", [128, 256], F32) for i in range(2)]
        gA = [sb(f"gA{i}", [128, 256], F32) for i in range(2)]
        bcA = [Buf() for _ in range(2)]
        bcV = [Buf() for _ in range(2)]
        bgA = [Buf() for _ in range(2)]
        actT = sb("actT", [128, 22, 256], BF16)
        bact = [Buf() for _ in range(22)]
        tmp = [sb(f"tmp{i}", [128, D], F32) for i in range(2)]
        btmp = [Buf() for _ in range(2)]
        ss2 = sb("ss2", [128, 8], F32)
        bss2 = [Buf() for _ in range(2)]
        uprot = [4, 5, 6]
        upi = 0
        nblk = 8 * nseq
        for b in range(nblk):
            r0 = 256 * b
            hT = h2T[b % 2]
            bh = bh2T[b % 2]
            for tt in range(2):
                xi = (2 * b + tt) % 4
                gi = 2 * b + tt
                S.dma(x1t[xi][:], A["x1s"][r0 + 128 * tt: r0 + 128 * (tt + 1), :], reads=[x1b[gi]], writes=[bx1t[xi]])
                S.op("act", lambda e: e.activation(out=junk[:], in_=x1t[xi][:], func=ACT.Square, accum_out=ssq[:, xi:xi + 1]), reads=[bx1t[xi]], writes=[bjunk, bss[xi]])
                rstd_from_ss(S, ssq[:, xi:xi + 1], ssq[:, 4 + xi:5 + xi], D, [bss[xi]], [bss[xi]])
                S.op("dve", lambda e: e.scalar_tensor_tensor(out=hn[tt][:], in0=x1t[xi][:], scalar=ssq[:, 4 + xi:5 + xi], in1=gpre[:], op0=ALU.mult, op1=ALU.mult), reads=[bx1t[xi], bss[xi], bg], writes=[bhn[tt]])
                for kc in range(8):
                    S.op("pe", lambda e: e.transpose(out=PB[:, kc * 128:(kc + 1) * 128], in_=hn[tt][:, kc * 128:(kc + 1) * 128], identity=ident[:]), reads=[bhn[tt], bid], writes=[PBb])
                S.op("act", lambda e: e.copy(out=hT[:, :, 2 + 128 * tt: 2 + 128 * (tt + 1)], in_=PB[:, :].rearrange("p (k t) -> p k t", k=8)), reads=[PBb], writes=[bh])
            if b % 8 == 0:
                S.op("pool", lambda e: e.memset(hT[:, :, 0:2], 0.0), writes=[bh])
            else:
                S.op("pool", lambda e: e.tensor_copy(out=hT[:, :, 0:2], in_=h2T[(b - 1) % 2][:, :, 256:258]), reads=[bh2T[(b - 1) % 2]], writes=[bh])
            for j in range(22):
                ia = uprot[upi % 3]
                upi += 1
                iv = uprot[upi % 3]
                upi += 1
                ca, cv, ga = cA[j % 2], cV[j % 2], gA[j % 2]
                bca, bcv, bga = bcA[j % 2], bcV[j % 2], bgA[j % 2]
                for (ib, col0) in ((ia, j * 128), (iv, DFF + j * 128)):
                    for kc in range(8):
                        S.op("pe", lambda e: e.matmul(PS[ib][:, 0:258], lhsT=wup[:, kc, col0:col0 + 128], rhs=hT[:, kc, :], start=(kc == 0), stop=(kc == 7)),
                             reads=[bwup[col0 // 512], bh], writes=[PSb[ib]])
                for (ib, cc, bcc, jj) in ((ia, ca, bca, j), (iv, cv, bcv, 22 + j)):
                    S.op("act", lambda e: e.activation(out=cc[:], in_=PS[ib][:, 2:258], func=ACT.Identity, scale=wc[:, 2, jj:jj + 1], bias=bc[:, jj:jj + 1]), reads=[PSb[ib], bwc], writes=[bcc])
                    S.op("dve", lambda e: e.scalar_tensor_tensor(out=cc[:], in0=PS[ib][:, 1:257], scalar=wc[:, 1, jj:jj + 1], in1=cc[:], op0=ALU.mult, op1=ALU.add), reads=[PSb[ib], bwc], writes=[bcc])
                    S.op("dve", lambda e: e.scalar_tensor_tensor(out=cc[:], in0=PS[ib][:, 0:256], scalar=wc[:, 0, jj:jj + 1], in1=cc[:], op0=ALU.mult, op1=ALU.add), reads=[PSb[ib], bwc], writes=[bcc])
                S.op("act", lambda e: e.activation(out=ga[:], in_=ca[:], func=ACT.Gelu_apprx_tanh), reads=[bca], writes=[bga])
                S.op("pool", lambda e: e.tensor_tensor(out=actT[:, j, :], in0=ga[:], in1=cv[:], op=ALU.mult), reads=[bga, bcv], writes=[bact[j]])
            for tt in range(2):
                xi = (2 * b + tt) % 4
                gi = 2 * b + tt
                for j in range(22):
                    for hf in range(2):
                        S.op("pe", lambda e: e.matmul(PS[2 * tt + hf][:, :], lhsT=actT[:, j, 128 * tt:128 * (tt + 1)], rhs=wdn[:, j, hf * 512:(hf + 1) * 512], start=(j == 0), stop=(j == 21)),
                             reads=[bact[j], bwdn[j // 11]], writes=[PSb[2 * tt + hf]])
                for hf in range(2):
                    S.op("act", lambda e: e.activation(out=junk[:, 0:512], in_=PS[2 * tt + hf][:, :], func=ACT.Square, accum_out=ss2[:, 4 * tt + hf:4 * tt + hf + 1]), reads=[PSb[2 * tt + hf]], writes=[bjunk, bss2[tt]])
                S.op("dve", lambda e: e.tensor_tensor(out=ss2[:, 4 * tt + 2:4 * tt + 3], in0=ss2[:, 4 * tt:4 * tt + 1], in1=ss2[:, 4 * tt + 1:4 * tt + 2], op=ALU.add), reads=[bss2[tt]], writes=[bss2[tt]])
                rstd_from_ss(S, ss2[:, 4 * tt + 2:4 * tt + 3], ss2[:, 4 * tt + 3:4 * tt + 4], D, [bss2[tt]], [bss2[tt]])
                for hf in range(2):
                    S.op("dve", lambda e: e.scalar_tensor_tensor(out=tmp[tt][:, hf * 512:(hf + 1) * 512], in0=PS[2 * tt + hf][:, :], scalar=ss2[:, 4 * tt + 3:4 * tt + 4], in1=gpost[:, hf * 512:(hf + 1) * 512], op0=ALU.mult, op1=ALU.mult),
                         reads=[PSb[2 * tt + hf], bss2[tt], bg], writes=[btmp[tt]])
                S.op("pool", lambda e: e.tensor_tensor(out=tmp[tt][:], in0=tmp[tt][:], in1=x1t[xi][:], op=ALU.add), reads=[bx1t[xi]], writes=[btmp[tt]])
                S.dma(A["x1s"][r0 + 128 * tt: r0 + 128 * (tt + 1), :], tmp[tt][:], reads=[btmp[tt]], writes=[x1b[gi]])


def stage_ple(nc, S, A, PS, PSb, PB, PBb, ident, bid, x1b, nseq):
    with ExitStack() as s2:
        sb = lambda name, shape, dt: s2.enter_context(_sbt(nc, name, shape, dt))
        wpg = sb("wpg", [128, 8, D], BF16)
        wpl = sb("wpl", [128, 2, D], BF16)
        bw = Buf()
        gple = sb("gple", [128, D], F32)
        S.dma(wpg[:], A["w_ple_gate"], writes=[bw], eng="pool")
        S.dma(wpl[:], A["w_ple"], writes=[bw], eng="pool")
        S.dma(gple[:], A["g_ple"][0:1, :].to_broadcast([128, D]), writes=[bw])
        R = 2
        x2t = [sb(f"x2t{i}", [128, D], F32) for i in range(R)]
        pt = [sb(f"pt{i}", [128, 256], F32) for i in range(R)]
        x2b = [sb(f"x2b{i}", [128, D], BF16) for i in range(R)]
        pbf = [sb(f"pbf{i}", [128, 256], BF16) for i in range(R)]
        x2T = [sb(f"x2T{i}", [128, 8, 128], BF16) for i in range(R)]
        pT = [sb(f"pT{i}", [128, 2, 128], BF16) for i in range(R)]
        sg = [sb(f"sg{i}", [128, D], F32) for i in range(R)]
        eg = [sb(f"eg{i}", [128, D], F32) for i in range(R)]
        junk = sb("junkp", [128, D], BF16)
        ssp = sb("ssp", [128, 2 * R], F32)
        bx2t, bpt, bx2b, bpbf, bx2T, bpT, bsg, beg, bssp = [[Buf() for _ in range(R)] for _ in range(9)]
        bjunk = Buf()
        for i in range(16 * nseq):
            k = i % R
            rows = slice(128 * i, 128 * (i + 1))
            S.dma(x2t[k][:], A["x1s"][rows, :], reads=[x1b[i]], writes=[bx2t[k]])
            S.dma(pt[k][:], A["p"][rows, :], writes=[bpt[k]])
            S.op("dve", lambda e: e.tensor_copy(out=x2b[k][:], in_=x2t[k][:]), reads=[bx2t[k]], writes=[bx2b[k]])
            S.op("pool", lambda e: e.tensor_copy(out=pbf[k][:], in_=pt[k][:]), reads=[bpt[k]], writes=[bpbf[k]])
            for kc in range(8):
                S.op("pe", lambda e: e.transpose(out=PB[:, kc * 128:(kc + 1) * 128], in_=x2b[k][:, kc * 128:(kc + 1) * 128], identity=ident[:]), reads=[bx2b[k], bid], writes=[PBb])
            S.op("act", lambda e: e.copy(out=x2T[k][:], in_=PB[:, :].rearrange("p (k t) -> p k t", k=8)), reads=[PBb], writes=[bx2T[k]])
            for kc in range(2):
                S.op("pe", lambda e: e.transpose(out=PB[:, kc * 128:(kc + 1) * 128], in_=pbf[k][:, kc * 128:(kc + 1) * 128], identity=ident[:]), reads=[bpbf[k], bid], writes=[PBb])
            S.op("act", lambda e: e.copy(out=pT[k][:], in_=PB[:, 0:256].rearrange("p (k t) -> p k t", k=2)), reads=[PBb], writes=[bpT[k]])
            base = 4 * (i % 1)
            for hf in range(2):
                for kc in range(8):
                    S.op("pe", lambda e: e.matmul(PS[hf][:, :], lhsT=x2T[k][:, kc, :], rhs=wpg[:, kc, hf * 512:(hf + 1) * 512], start=(kc == 0), stop=(kc == 7)), reads=[bx2T[k], bw], writes=[PSb[hf]])
                for kc in range(2):
                    S.op("pe", lambda e: e.matmul(PS[2 + hf][:, :], lhsT=pT[k][:, kc, :], rhs=wpl[:, kc, hf * 512:(hf + 1) * 512], start=(kc == 0), stop=(kc == 1)), reads=[bpT[k], bw], writes=[PSb[2 + hf]])
            for hf in range(2):
                cs = slice(hf * 512, (hf + 1) * 512)
                S.op("act", lambda e: e.activation(out=sg[k][:, cs], in_=PS[hf][:, :], func=ACT.Sigmoid), reads=[PSb[hf]], writes=[bsg[k]])
                S.op("dve", lambda e: e.tensor_tensor(out=eg[k][:, cs], in0=PS[2 + hf][:, :], in1=sg[k][:, cs], op=ALU.mult), reads=[PSb[2 + hf], bsg[k]], writes=[beg[k]])
            S.op("act", lambda e: e.activation(out=junk[:], in_=eg[k][:], func=ACT.Square, accum_out=ssp[:, 2 * k:2 * k + 1]), reads=[beg[k]], writes=[bjunk, bssp[k]])
            rstd_from_ss(S, ssp[:, 2 * k:2 * k + 1], ssp[:, 2 * k + 1:2 * k + 2], D, [bssp[k]], [bssp[k]])
            S.op("dve", lambda e: e.scalar_tensor_tensor(out=eg[k][:], in0=eg[k][:], scalar=ssp[:, 2 * k + 1:2 * k + 2], in1=gple[:], op0=ALU.mult, op1=ALU.mult), reads=[bssp[k], bw], writes=[beg[k]])
            S.op("pool", lambda e: e.tensor_tensor(out=eg[k][:], in0=eg[k][:], in1=x2t[k][:], op=ALU.add), reads=[bx2t[k]], writes=[beg[k]])
            S.dma(A["out"][rows, :], eg[k][:], reads=[beg[k]])


def attn_stream(S, pairs, PS, PSb, PT, bPT, srot=(0, 1, 2), skew=2):
    n = len(pairs)
    NP = len(PT)

    def emitS(k):
        pr = pairs[k]
        ib = srot[k % len(srot)]
        m = len(pr["smm"])
        for q, (l, r, rd) in enumerate(pr["smm"]):
            S.op("pe", lambda e: e.matmul(PS[ib][:, :], lhsT=l, rhs=r, start=(q == 0), stop=(q == m - 1)), reads=rd, writes=[PSb[ib]])

    for k in range(min(skew, n)):
        emitS(k)
    for k in range(n):
        if k + skew < n:
            emitS(k + skew)
        pr = pairs[k]
        ib = srot[k % len(srot)]
        S.op("act", lambda e: e.activation(out=PT[k % NP][:], in_=PS[ib][:, :], func=ACT.Exp), reads=[PSb[ib]], writes=[bPT[k % NP]])
        ob, first, last = pr["O"]
        vl, vrd = pr["v"]
        S.op("pe", lambda e: e.matmul(PS[ob][:, :], lhsT=vl, rhs=PT[k % NP][:], start=first, stop=last), reads=[bPT[k % NP]] + vrd, writes=[PSb[ob]])
        if pr.get("fin") is not None:
            pr["fin"]()


def stage_mix(nc, S, A, sq, PS, PSb, PB, PBb, ident, bid, x1b, dbg):
    r0 = sq * T
    with ExitStack() as s1:
        sb = lambda name, shape, dt: s1.enter_context(_sbt(nc, name, shape, dt))
        nbc = [0]

        def nb():
            nbc[0] = (nbc[0] + 1) % 7
            return nbc[0]

        cst = Buf()
        negc = sb("negc", [128, 512], BF16)
        negw = sb("negw", [128, 512], BF16)
        negm = sb("negm", [128, 4, 512], BF16)
        selg = sb("selg", [24, 24 * 64], BF16)
        cmptbl = sb("cmptbl", [128, 128], F32)
        v12 = sb("v12", [128, 2], F32)
        invf = sb("invf", [32, 1], F32)
        ones = sb("ones", [128, 128], BF16)
        S.dma(negc[:], A["negc"], writes=[cst], eng="pool")
        S.dma(negw[:], A["negw"], writes=[cst], eng="pool")
        S.dma(negm[:], A["negm"].rearrange("m p t -> p m t"), writes=[cst], eng="pool")
        S.dma(selg[:], A["selg"], writes=[cst], eng="pool")
        S.dma(cmptbl[:], A["cmptbl"], writes=[cst])
        S.dma(v12[:], A["v12"], writes=[cst])
        S.dma(invf[:], A["invf"], writes=[cst])
        S.op("pool", lambda e: e.memset(ones[:], 1.0), writes=[cst])
        gpm = sb("gpm", [128, D], F32)
        S.dma(gpm[:], A["g_pre_mix"][0:1, :].to_broadcast([128, D]), writes=[cst])

        hT = sb("hT", [128, 8, T], BF16)
        bhT = [Buf() for _ in range(4)]
        onsa = sb("onsa", [128, 4, T], BF16)
        omla = sb("omla", [128, 4, T], BF16)
        bonsa = Buf()
        bomla = Buf()
        junk = sb("junkm", [128, D], BF16)
        bjunk = Buf()
        sst = sb("sst", [128, 8], F32)

        with ExitStack() as s2:
            sb2 = lambda name, shape, dt: s2.enter_context(_sbt(nc, name, shape, dt))
            xt = [sb2(f"xt{i}", [128, D], F32) for i in range(2)]
            hn = [sb2(f"hnm{i}", [128, D], BF16) for i in range(2)]
            bxt = [Buf() for _ in range(2)]
            bhn = [Buf() for _ in range(2)]
            bs = [Buf() for _ in range(2)]
            for i in range(16):
                k = i % 2
                S.dma(xt[k][:], A["x"][r0 + 128 * i: r0 + 128 * (i + 1), :], writes=[bxt[k]])
                S.op("act", lambda e: e.activation(out=junk[:], in_=xt[k][:], func=ACT.Square, accum_out=sst[:, k:k + 1]), reads=[bxt[k]], writes=[bjunk, bs[k]])
                rstd_from_ss(S, sst[:, k:k + 1], sst[:, 2 + k:3 + k], D, [bs[k]], [bs[k]])
                S.op("dve", lambda e: e.scalar_tensor_tensor(out=hn[k][:], in0=xt[k][:], scalar=sst[:, 2 + k:3 + k], in1=gpm[:], op0=ALU.mult, op1=ALU.mult), reads=[bxt[k], bs[k], cst], writes=[bhn[k]])
                for kc in range(8):
                    S.op("pe", lambda e: e.transpose(out=PB[:, kc * 128:(kc + 1) * 128], in_=hn[k][:, kc * 128:(kc + 1) * 128], identity=ident[:]), reads=[bhn[k], bid], writes=[PBb])
                S.op("act", lambda e: e.copy(out=hT[:, :, i * 128:(i + 1) * 128], in_=PB[:, :].rearrange("p (k t) -> p k t", k=8)), reads=[PBb], writes=[bhT[i // 4]])

        with ExitStack() as s2:
            sb2 = lambda name, shape, dt: s2.enter_context(_sbt(nc, name, shape, dt))
            s3 = ExitStack()
            sb3 = lambda name, shape, dt: s3.enter_context(_sbt(nc, name, shape, dt))
            cos2 = sb2("cos2", [32, T], F32)
            sin2 = sb2("sin2", [32, T], F32)
            brope = Buf()
            krope = sb2("krope", [32, T], BF16)
            cqn = sb2("cqn", [128, 2, T], BF16)
            ckvn = sb2("ckvn", [128, T], BF16)
            t1 = sb2("t1", [32, 512], F32)
            t2 = sb2("t2", [32, 512], F32)
            posi = sb3("posi", [32, T], I32)
            ang = sb3("ang", [32, T], F32)
            tmpa = sb3("tmpa", [32, T], F32)
            S.dma(posi[:], A["pos"][sq:sq + 1, :].to_broadcast([32, T]), writes=[brope])
            S.op("dve", lambda e: e.tensor_copy(out=ang[:], in_=posi[:]), reads=[brope], writes=[brope])
            S.op("dve", lambda e: e.tensor_scalar(out=ang[:], in0=ang[:], scalar1=invf[:, 0:1], scalar2=None, op0=ALU.mult), reads=[cst], writes=[brope])
            qi = sb3("qi", [32, T], I32)
            for (addc, dst) in ((0.0, sin2), (PI / 2, cos2)):
                S.op("dve", lambda e: e.tensor_scalar(out=tmpa[:], in0=ang[:], scalar1=addc, scalar2=1.0 / TWO_PI, op0=ALU.add, op1=ALU.mult), reads=[brope], writes=[brope])
                S.op("dve", lambda e: e.tensor_copy(out=qi[:], in_=tmpa[:]), reads=[brope], writes=[brope])
                S.op("dve", lambda e: e.tensor_copy(out=tmpa[:], in_=qi[:]), reads=[brope], writes=[brope])
                S.op("dve", lambda e: e.scalar_tensor_tensor(out=tmpa[:], in0=tmpa[:], scalar=-TWO_PI, in1=ang[:], op0=ALU.mult, op1=ALU.add), reads=[brope], writes=[brope])
                if addc != 0.0:
                    S.op("dve", lambda e: e.tensor_scalar(out=tmpa[:], in0=tmpa[:], scalar1=addc, scalar2=None, op0=ALU.add), reads=[brope], writes=[brope])
                S.op("dve", lambda e: e.tensor_scalar(out=dst[:], in0=tmpa[:], scalar1=PI, scalar2=-TWO_PI, op0=ALU.is_gt, op1=ALU.mult), reads=[brope], writes=[brope])
                S.op("dve", lambda e: e.tensor_tensor(out=tmpa[:], in0=tmpa[:], in1=dst[:], op=ALU.add), reads=[brope], writes=[brope])
                S.op("dve", lambda e: e.tensor_scalar(out=tmpa[:], in0=tmpa[:], scalar1=-PI, scalar2=PI, op0=ALU.max, op1=ALU.min), reads=[brope], writes=[brope])
                S.op("act", lambda e: e.activation(out=dst[:], in_=tmpa[:], func=ACT.Sin), reads=[brope], writes=[brope])
            wA = sb3("wA", [128, 8, 416], BF16)
            wkrot = sb3("wkrot", [128, 8, 32], BF16)
            bwA = Buf()
            S.dma(wA[:], A["w_in"][:, :, 1304:1720], writes=[bwA], eng="pool")
            S.op("pool", lambda e: e.tensor_scalar(out=wkrot[:, :, 0:16], in0=wA[:, :, 400:416], scalar1=-1.0, scalar2=None, op0=ALU.mult), reads=[bwA], writes=[bwA])
            S.op("pool", lambda e: e.tensor_copy(out=wkrot[:, :, 16:32], in_=wA[:, :, 384:400]), reads=[bwA], writes=[bwA])
            gq = sb3("gq", [128, 2], F32)
            gkv = sb3("gkv", [128, 1], F32)
            S.dma(gq[:], A["g_q"], writes=[bwA])
            S.dma(gkv[:], A["g_kv"], writes=[bwA])
            bcq = Buf()
            cf = [sb3(f"cf{m}", [128, 512], F32) for m in range(3)]
            sqb = [sb3(f"sqb{m}", [128, 512], BF16) for m in range(3)]
            rq = sb3("rq", [128, 512], F32)
            rk = sb3("rk", [128, 512], F32)
            bcf = [Buf() for _ in range(3)]
            bsqb = [Buf() for _ in range(3)]
            brq, brk, bt1, bt2 = Buf(), Buf(), Buf(), Buf()
            for c in range(4):
                cs = slice(c * 512, (c + 1) * 512)
                for m, col0 in enumerate((0, 128, 256)):
                    ib = nb()
                    for kc in range(8):
                        S.op("pe", lambda e: e.matmul(PS[ib][:, :], lhsT=wA[:, kc, col0:col0 + 128], rhs=hT[:, kc, cs], start=(kc == 0), stop=(kc == 7)), reads=[bwA, bhT[c]], writes=[PSb[ib]])
                    S.op("act", lambda e: e.copy(out=cf[m][:], in_=PS[ib][:, :]), reads=[PSb[ib]], writes=[bcf[m]])
                    S.op("act", lambda e: e.activation(out=sqb[m][:], in_=PS[ib][:, :], func=ACT.Square), reads=[PSb[ib]], writes=[bsqb[m]])
                iq = nb()
                S.op("pe", lambda e: e.matmul(PS[iq][:, :], lhsT=ones[:], rhs=sqb[0][:], start=True, stop=False), reads=[cst, bsqb[0]], writes=[PSb[iq]])
                S.op("pe", lambda e: e.matmul(PS[iq][:, :], lhsT=ones[:], rhs=sqb[1][:], start=False, stop=True), reads=[cst, bsqb[1]], writes=[PSb[iq]])
                ik = nb()
                S.op("pe", lambda e: e.matmul(PS[ik][:, :], lhsT=ones[:], rhs=sqb[2][:], start=True, stop=True), reads=[cst, bsqb[2]], writes=[PSb[ik]])
                S.op("act", lambda e: e.activation(out=rq[:], in_=PS[iq][:, :], func=ACT.Sqrt, scale=1.0 / 256, bias=EPS), reads=[PSb[iq]], writes=[brq])
                S.op("dve", lambda e: e.reciprocal(out=rq[:], in_=rq[:]), reads=[brq], writes=[brq])
                S.op("act", lambda e: e.activation(out=rk[:], in_=PS[ik][:, :], func=ACT.Sqrt, scale=1.0 / 128, bias=EPS), reads=[PSb[ik]], writes=[brk])
                S.op("dve", lambda e: e.reciprocal(out=rk[:], in_=rk[:]), reads=[brk], writes=[brk])
                for m in range(2):
                    S.op("dve", lambda e: e.scalar_tensor_tensor(out=cqn[:, m, cs], in0=cf[m][:], scalar=gq[:, m:m + 1], in1=rq[:], op0=ALU.mult, op1=ALU.mult), reads=[bcf[m], brq, bwA], writes=[bcq])
                S.op("dve", lambda e: e.scalar_tensor_tensor(out=ckvn[:, cs], in0=cf[2][:], scalar=gkv[:, 0:1], in1=rk[:], op0=ALU.mult, op1=ALU.mult), reads=[bcf[2], brk, bwA], writes=[bcq])
                i1 = nb()
                i2 = nb()
                for kc in range(8):
                    S.op("pe", lambda e: e.matmul(PS[i1][0:32, :], lhsT=wA[:, kc, 384:416], rhs=hT[:, kc, cs], start=(kc == 0), stop=(kc == 7)), reads=[bwA, bhT[c]], writes=[PSb[i1]])
                for kc in range(8):
                    S.op("pe", lambda e: e.matmul(PS[i2][0:32, :], lhsT=wkrot[:, kc, :], rhs=hT[:, kc, cs], start=(kc == 0), stop=(kc == 7)), reads=[bwA, bhT[c]], writes=[PSb[i2]])
                S.op("dve", lambda e: e.tensor_tensor(out=t1[:], in0=PS[i1][0:32, :], in1=cos2[:, cs], op=ALU.mult), reads=[PSb[i1], brope], writes=[bt1])
                S.op("dve", lambda e: e.tensor_tensor(out=t2[:], in0=PS[i2][0:32, :], in1=sin2[:, cs], op=ALU.mult), reads=[PSb[i2], brope], writes=[bt2])
                S.op("pool", lambda e: e.tensor_tensor(out=krope[:, cs], in0=t1[:], in1=t2[:], op=ALU.add), reads=[bt1, bt2], writes=[bcq])

            S.barrier()
            s3.close()
            wuq = sb2("wuq", [128, 2, 768], BF16)
            wuqr = sb2("wuqr", [128, 2, 8, 32], BF16)
            wukv = sb2("wukv", [128, 1024], BF16)
            bwu = Buf()
            S.dma(wuq[:], A["w_uq"], writes=[bwu], eng="pool")
            S.dma(wukv[:], A["w_ukv"], writes=[bwu], eng="pool")
            wuq4 = wuq[:, :, :].rearrange("p m (h c) -> p m h c", h=8)
            for m in range(2):
                S.op("pool", lambda e: e.tensor_scalar(out=wuqr[:, m, :, 0:16], in0=wuq4[:, m, :, 80:96], scalar1=-1.0, scalar2=None, op0=ALU.mult), reads=[bwu], writes=[bwu])
                S.op("pool", lambda e: e.tensor_copy(out=wuqr[:, m, :, 16:32], in_=wuq4[:, m, :, 64:80]), reads=[bwu], writes=[bwu])
            VAm = sb2("VAm", [128, 16, 8, 128], BF16)
            bVA = Buf()
            S.op("pool", lambda e: e.memset(VAm[:], 1.0), writes=[bVA])
            wukv3 = wukv[:, :].rearrange("p (h c) -> p h c", h=8)
            for kt in range(16):
                ib = nb()
                S.op("pe", lambda e: e.matmul(PS[ib][:, :], lhsT=ckvn[:, kt * 128:(kt + 1) * 128], rhs=wukv3[:, :, 64:128], start=True, stop=True), reads=[bcq, bwu], writes=[PSb[ib]])
                S.op("act", lambda e: e.copy(out=VAm[:, kt, :, 0:64], in_=PS[ib][:, :].rearrange("p (h c) -> p h c", h=8)), reads=[PSb[ib]], writes=[bVA])
            qnt = [sb2(f"qnt{i}", [64, T], BF16) for i in range(2)]
            qrt = [sb2(f"qrt{i}", [32, T], BF16) for i in range(2)]
            knt = [sb2(f"knt{i}", [64, T], BF16) for i in range(2)]
            bqk = [Buf() for _ in range(2)]
            PT = [sb2(f"PT{i}", [128, 512], BF16) for i in range(4)]
            bPT = [Buf() for _ in range(4)]
            rcm = sb2("rcm", [64, 512], F32)
            brcm = Buf()
            for h in range(8):
                hk = h % 2
                qn, qr, kn = qnt[hk], qrt[hk], knt[hk]
                for c in range(4):
                    cs = slice(c * 512, (c + 1) * 512)
                    ib = 3 + (c % 4)
                    for m in range(2):
                        S.op("pe", lambda e: e.matmul(PS[ib][0:64, :], lhsT=wuq[:, m, h * 96:h * 96 + 64], rhs=cqn[:, m, cs], start=(m == 0), stop=(m == 1)), reads=[bwu, bcq], writes=[PSb[ib]])
                    S.op("act", lambda e: e.mul(out=qn[:, cs], in_=PS[ib][0:64, :], mul=SC_M), reads=[PSb[ib]], writes=[bqk[hk]])
                    S.op("pe", lambda e: e.matmul(PS[ib][0:64, :], lhsT=wukv[:, h * 128:h * 128 + 64], rhs=ckvn[:, cs], start=True, stop=True), reads=[bwu, bcq], writes=[PSb[ib]])
                    S.op("act", lambda e: e.copy(out=kn[:, cs], in_=PS[ib][0:64, :]), reads=[PSb[ib]], writes=[bqk[hk]])
                    i1 = 3 + ((c + 1) % 4)
                    i2 = 3 + ((c + 2) % 4)
                    for m in range(2):
                        S.op("pe", lambda e: e.matmul(PS[i1][0:32, :], lhsT=wuq[:, m, h * 96 + 64:h * 96 + 96], rhs=cqn[:, m, cs], start=(m == 0), stop=(m == 1)), reads=[bwu, bcq], writes=[PSb[i1]])
                    for m in range(2):
                        S.op("pe", lambda e: e.matmul(PS[i2][0:32, :], lhsT=wuqr[:, m, h, :], rhs=cqn[:, m, cs], start=(m == 0), stop=(m == 1)), reads=[bwu, bcq], writes=[PSb[i2]])
                    S.op("dve", lambda e: e.tensor_tensor(out=t1[:], in0=PS[i1][0:32, :], in1=cos2[:, cs], op=ALU.mult), reads=[PSb[i1], brope], writes=[bt1])
                    S.op("dve", lambda e: e.tensor_tensor(out=t2[:], in0=PS[i2][0:32, :], in1=sin2[:, cs], op=ALU.mult), reads=[PSb[i2], brope], writes=[bt2])
                    S.op("pool", lambda e: e.tensor_tensor(out=t1[:], in0=t1[:], in1=t2[:], op=ALU.add), reads=[bt2], writes=[bt1])
                    S.op("act", lambda e: e.mul(out=qr[:, cs], in_=t1[:], mul=SC_M), reads=[bt1], writes=[bqk[hk]])
                pairs = []
                for c in range(4):
                    cs = slice(c * 512, (c + 1) * 512)
                    ob = 3 + (c % 3)

                    def fin(c=c, cs=cs, ob=ob):
                        S.op("dve", lambda e: e.reciprocal(out=rcm[:], in_=PS[ob][64:128, :]), reads=[PSb[ob]], writes=[brcm])
                        S.op("dve", lambda e: e.tensor_tensor(out=omla[hk * 64:hk * 64 + 64, h // 2, cs], in0=PS[ob][0:64, :], in1=rcm[:], op=ALU.mult), reads=[PSb[ob], brcm], writes=[bomla])

                    nk = 4 * c + 4
                    for kt in range(nk):
                        ks = slice(kt * 128, (kt + 1) * 128)
                        smm = [(kn[:, ks], qn[:, cs], [bqk[hk]]), (krope[:, ks], qr[:, cs], [bqk[hk], bcq])]
                        if kt >= 4 * c:
                            smm.append((ident[:], negm[:, kt - 4 * c, :], [bid, cst]))
                        pairs.append(dict(smm=smm, v=(VAm[:, kt, h, :], [bVA]), O=(ob, kt == 0, kt == nk - 1), fin=fin if kt == nk - 1 else None))
                attn_stream(S, pairs, PS, PSb, PT, bPT)
            if dbg:
                S.dma(A["d_omla"], omla[:], reads=[bomla], eng="pool")
        S.barrier()
        if os.environ.get('SKIP_NSA') is None:
          stage_nsa(nc, S, A, sq, PS, PSb, PB, PBb, ident, bid, s1, hT, bhT, onsa, bonsa, cst, negc, negw, selg, cmptbl, v12, dbg, nb)
        S.barrier()
        if os.environ.get('SKIP_MERGE') is None:
          stage_merge(nc, S, A, sq, PS, PSb, PB, PBb, ident, bid, hT, bhT, onsa, bonsa, omla, bomla, x1b, nb, junk, bjunk)


def stage_nsa(nc, S, A, sq, PS, PSb, PB, PBb, ident, bid, s1, hT, bhT, onsa, bonsa, cst, negc, negw, selg, cmptbl, v12, dbg, nb):
    with ExitStack() as s2:
        sb = lambda name, shape, dt: s2.enter_context(_sbt(nc, name, shape, dt))
        QA = [sb(f"QA{g}", [128, 16 * 512], BF16) for g in range(2)]
        bQA = [[Buf() for _ in range(16)] for g in range(2)]
        BR = ("sel", "win")
        KA = {(b, g): sb(f"KA{b}{g}", [128, T], BF16) for b in BR for g in range(2)}
        bKA = {k: Buf() for k in KA}
        VA = {(b, g): sb(f"VA{b}{g}", [128, 16, 128], BF16) for b in BR for g in range(2)}
        bVA = {k: Buf() for k in VA}
        gsig = sb("gsig", [24, T], BF16)
        bgs = Buf()
        KC = [sb(f"KC{g}", [128, 128], BF16) for g in range(2)]
        VC = [sb(f"VC{g}", [128, 64], BF16) for g in range(2)]
        bKC = [Buf() for _ in range(2)]
        bVC = [Buf() for _ in range(2)]
        for g in range(2):
            S.op("pool", lambda e: e.memset(QA[g][64:96, :], 0.0), writes=bQA[g])
            S.dma(QA[g][96:100, :], A["qaug"][g, 32:36, :], writes=bQA[g], eng="pool")
            S.op("pool", lambda e: e.memset(KC[g][:], 0.0), writes=[bKC[g]])
            S.dma(KC[g][64:100, :], A["kaug_cmp"], writes=[bKC[g]], eng="pool")
            S.op("pool", lambda e: e.memset(VC[g][:], 0.0), writes=[bVC[g]])
            for b in BR:
                S.dma(KA[(b, g)][64:100, :], A["kaug_" + b], writes=[bKA[(b, g)]], eng="pool")
                S.op("pool", lambda e: e.memset(VA[(b, g)][:], 1.0), writes=[bVA[(b, g)]])
        NSTOP = int(os.environ.get('NSA_STOP', '99'))
        if NSTOP <= 1:
            S.barrier()
            return
        with ExitStack() as s3:
            sb3 = lambda name, shape, dt: s3.enter_context(_sbt(nc, name, shape, dt))
            wN = sb3("wN", [128, 8, 1304], BF16)
            bwN = Buf()
            S.dma(wN[:], A["w_in"][:, :, 0:1304], writes=[bwN], eng="pool")
            kcT = {(kd, g): sb3(f"kcT{kd}{g}", [64, T], BF16) for kd in range(2) for g in range(2)}
            bkc = {k: Buf() for k in kcT}
            for c in range(4):
                cs = slice(c * 512, (c + 1) * 512)
                for n in range(4):
                    ib = nb()
                    for kc in range(8):
                        S.op("pe", lambda e: e.matmul(PS[ib][:, :], lhsT=wN[:, kc, n * 128:(n + 1) * 128], rhs=hT[:, kc, cs], start=(kc == 0), stop=(kc == 7)), reads=[bwN, bhT[c]], writes=[PSb[ib]])
                    g = n // 2
                    rr = (2 * n) % 4
                    QAv = QA[g][:, :].rearrange("p (i r t) -> p i r t", i=16, r=4)
                    S.op("act", lambda e: e.mul(out=QAv[0:64, 4 * c:4 * c + 4, rr, :], in_=PS[ib][0:64, :].rearrange("p (i t) -> p i t", i=4), mul=SC_N), reads=[PSb[ib]], writes=bQA[g][4 * c:4 * c + 4])
                    S.op("dve", lambda e: e.tensor_scalar(out=QAv[0:64, 4 * c:4 * c + 4, rr + 1, :], in0=PS[ib][64:128, :].rearrange("p (i t) -> p i t", i=4), scalar1=SC_N, scalar2=None, op0=ALU.mult), reads=[PSb[ib]], writes=bQA[g][4 * c:4 * c + 4])
                for kind in (0, 1, 2, 4):
                    ib = nb()
                    col0 = 512 + kind * 128
                    for kc in range(8):
                        S.op("pe", lambda e: e.matmul(PS[ib][:, :], lhsT=wN[:, kc, col0:col0 + 128], rhs=hT[:, kc, cs], start=(kc == 0), stop=(kc == 7)), reads=[bwN, bhT[c]], writes=[PSb[ib]])
                    for g in range(2):
                        if kind < 2:
                            dst, bd = kcT[(kind, g)][0:64, cs], bkc[(kind, g)]
                        else:
                            key = ("sel" if kind == 2 else "win", g)
                            dst, bd = KA[key][0:64, cs], bKA[key]
                        if g == 0:
                            S.op("act", lambda e: e.copy(out=dst, in_=PS[ib][0:64, :]), reads=[PSb[ib]], writes=[bd])
                        else:
                            S.op("dve", lambda e: e.tensor_copy(out=dst, in_=PS[ib][64:128, :]), reads=[PSb[ib]], writes=[bd])
                ib = nb()
                for kc in range(8):
                    S.op("pe", lambda e: e.matmul(PS[ib][0:24, :], lhsT=wN[:, kc, 1280:1304], rhs=hT[:, kc, cs], start=(kc == 0), stop=(kc == 7)), reads=[bwN, bhT[c]], writes=[PSb[ib]])
                S.op("act", lambda e: e.activation(out=gsig[:, cs], in_=PS[ib][0:24, :], func=ACT.Sigmoid), reads=[PSb[ib]], writes=[bgs])
            for kt in range(int(os.environ.get('NKT', '16')) if NSTOP > 2 else 0):
                ib = nb()
                ts = slice(kt * 128, (kt + 1) * 128)
                for q, col0 in enumerate((896, 1152)):
                    for kc in range(8):
                        S.op("pe", lambda e: e.matmul(PS[ib][:, q * 128:(q + 1) * 128], lhsT=hT[:, kc, ts], rhs=wN[:, kc, col0:col0 + 128], start=(kc == 0), stop=(kc == 7)), reads=[bwN, bhT[kt // 4]], writes=[PSb[ib]])
                for q, b in enumerate(BR):
                    for g in range(int(os.environ.get('VCOPY', '2'))):
                        if g == 0:
                            S.op("act", lambda e: e.copy(out=VA[(b, g)][:, kt, 0:64], in_=PS[ib][:, q * 128 + g * 64: q * 128 + g * 64 + 64]), reads=[PSb[ib]], writes=[bVA[(b, g)]])
                        else:
                            S.op("dve", lambda e: e.tensor_copy(out=VA[(b, g)][:, kt, 0:64], in_=PS[ib][:, q * 128 + g * 64: q * 128 + g * 64 + 64]), reads=[PSb[ib]], writes=[bVA[(b, g)]])
            for kd, (w1n, b1n, w2n, posn) in enumerate((("ck_w1", "ck_b1", "ck_w2", "posT_k"), ("cv_w1", "cv_b1", "cv_w2", "posT_v"))[:int(os.environ.get('NCMP', '2'))]):
                W1 = sb3(f"W1{kd}", [64, 32, 64], BF16)
                posT = sb3(f"posT{kd}", [64, 32], BF16)
                b1 = sb3(f"b1{kd}", [64, 1], F32)
                W2 = sb3(f"W2{kd}", [64, 64], BF16)
                bias = sb3(f"bias{kd}", [64, 1], F32)
                bcw = Buf()
                bbias = Buf()
                S.dma(W1[:], A[w1n], writes=[bcw], eng="pool")
                S.dma(posT[:], A[posn], writes=[bcw], eng="pool")
                S.dma(W2[:], A[w2n], writes=[bcw], eng="pool")
                S.dma(b1[:], A[b1n], writes=[bcw])
                ip = nb()
                for l in range(32):
                    S.op("pe", lambda e: e.matmul(PS[ip][0:64, 0:1], lhsT=W1[:, l, :], rhs=posT[:, l:l + 1], start=(l == 0), stop=(l == 31)), reads=[bcw], writes=[PSb[ip]])
                S.op("dve", lambda e: e.tensor_tensor(out=bias[:], in0=PS[ip][0:64, 0:1], in1=b1[:], op=ALU.add), reads=[PSb[ip], bcw], writes=[bbias])
                for g in range(2):
                    G = sb3(f"G{kd}{g}", [64, 128], BF16)
                    bG = Buf()
                    S.op("pool", lambda e: e.memset(G[:], 0.0), writes=[bG])
                    ia = nb()
                    for l in range(32):
                        S.op("pe", lambda e: e.matmul(PS[ia][0:64, 0:127], lhsT=W1[:, l, :], rhs=kcT[(kd, g)][0:64, l:l + 2017:16], start=(l == 0), stop=(l == 31)), reads=[bcw, bkc[(kd, g)]], writes=[PSb[ia]])
                    S.op("act", lambda e: e.activation(out=G[:, 0:127], in_=PS[ia][0:64, 0:127], func=ACT.Gelu_apprx_tanh, bias=bias[:, 0:1]), reads=[PSb[ia], bbias], writes=[bG])
                    io = nb()
                    if kd == 0:
                        S.op("pe", lambda e: e.matmul(PS[io][0:64, 0:127], lhsT=W2[:, :], rhs=G[:, 0:127], start=True, stop=True), reads=[bcw, bG], writes=[PSb[io]])
                        S.op("act", lambda e: e.copy(out=KC[g][0:64, 0:127], in_=PS[io][0:64, 0:127]), reads=[PSb[io]], writes=[bKC[g]])
                    else:
                        S.op("pe", lambda e: e.matmul(PS[io][0:127, 0:64], lhsT=G[:, 0:127], rhs=W2[:, :], start=True, stop=True), reads=[bcw, bG], writes=[PSb[io]])
                        S.op("act", lambda e: e.copy(out=VC[g][0:127, :], in_=PS[io][0:127, 0:64]), reads=[PSb[io]], writes=[bVC[g]])
            S.barrier()
        f32t = lambda name, shape: sb(name, shape, F32)
        sc = f32t("sc", [128, 512])
        ex = f32t("ex", [128, 512])
        pp = f32t("pp", [128, 512])
        pbf = sb("pbf", [128, 512], BF16)
        PTc = sb("PTc", [128, 512], BF16)
        sm = f32t("sm", [128, 8])
        t1i = f32t("t1i", [128, 4, 32])
        imp = f32t("imp", [128, 32])
        score = f32t("score", [128, 32])
        sc2 = f32t("sc2", [128, 32])
        m8 = f32t("m8", [128, 16])
        Z = sb("Z", [128, 128], BF16)
        rcs = f32t("rcs", [64, 512])
        gc = f32t("gc", [64, 512])
        cfm = f32t("cfm", [64, 512])
        tb = f32t("tb", [64, 512])
        acc = f32t("acc", [64, 512])
        bsc, bex, bpp, bpbf, bPTc, bsm, bsel, bZ, brcs, bgc, bcfm, btb, bacc = [Buf() for _ in range(13)]
        PT = [sb(f"PTn{i}", [128, 512], BF16) for i in range(4)]
        bPT = [Buf() for _ in range(4)]
        S.op("pool", lambda e: e.memset(Z[:], 0.0), writes=[bZ])
        v3 = lambda t: t[:, :].rearrange("p (r j) -> p r j", r=4)
        for g in range(2):
            QAv = QA[g][:, :].rearrange("p (i r t) -> p i r t", i=16, r=4)
            for i in range(int(os.environ.get('NSA_NI', '16'))):
                nj = 8 * i + 8
                ts = slice(i * 128, (i + 1) * 128)
                rhsQ = QA[g][0:100, i * 512:(i + 1) * 512]
                for r in range(4):
                    S.op("pe", lambda e: e.matmul(PS[6][:, r * 128:r * 128 + nj], lhsT=QA[g][0:100, (4 * i + r) * 128:(4 * i + r + 1) * 128], rhs=KC[g][0:100, 0:nj], start=True, stop=True), reads=[bQA[g][i], bKC[g]], writes=[PSb[6]])
                S.op("dve", lambda e: e.tensor_tensor(out=v3(sc)[:, :, 0:nj], in0=v3(PS[6])[:, :, 0:nj], in1=cmptbl[:, 120 - 8 * i:120 - 8 * i + nj].unsqueeze(1).to_broadcast([128, 4, nj]), op=ALU.add), reads=[PSb[6], cst], writes=[bsc])
                S.op("act", lambda e: e.activation(out=v3(ex)[:, :, 0:nj], in_=v3(sc)[:, :, 0:nj], func=ACT.Exp), reads=[bsc], writes=[bex])
                S.op("dve", lambda e: e.tensor_reduce(out=sm[:, 0:4], in_=v3(ex)[:, :, 0:nj], axis=AX.X, op=ALU.add), reads=[bex], writes=[bsm])
                S.op("dve", lambda e: e.tensor_scalar(out=sm[:, 0:4], in0=sm[:, 0:4], scalar1=1e-30, scalar2=None, op0=ALU.max), reads=[bsm], writes=[bsm])
                S.op("dve", lambda e: e.reciprocal(out=sm[:, 4:8], in_=sm[:, 0:4]), reads=[bsm], writes=[bsm])
                S.op("dve", lambda e: e.tensor_tensor(out=v3(pp)[:, :, 0:nj], in0=v3(ex)[:, :, 0:nj], in1=sm[:, 4:8].unsqueeze(2).to_broadcast([128, 4, nj]), op=ALU.mult), reads=[bex, bsm], writes=[bpp])
                S.op("pool", lambda e: e.tensor_copy(out=v3(pbf)[:, :, 0:nj], in_=v3(pp)[:, :, 0:nj]), reads=[bpp], writes=[bpbf])
                for r in range(4):
                    S.op("pe", lambda e: e.transpose(out=PB[0:nj, r * 128:(r + 1) * 128], in_=pbf[:, r * 128:r * 128 + nj], identity=ident[:]), reads=[bpbf, bid], writes=[PBb])
                S.op("act", lambda e: e.copy(out=PTc[0:nj, :], in_=PB[0:nj, 0:512]), reads=[PBb], writes=[bPTc])
                S.op("pe", lambda e: e.matmul(PS[5][0:64, :], lhsT=VC[g][0:nj, 0:64], rhs=PTc[0:nj, :], start=True, stop=True), reads=[bVC[g], bPTc], writes=[PSb[5]])
                pairs = []
                k0 = max(0, i - 4)
                for kt in range(k0, i + 1):
                    ks = slice(kt * 128, (kt + 1) * 128)
                    smm = [(KA[("win", g)][0:100, ks], rhsQ, [bKA[("win", g)], bQA[g][i]])]
                    if kt == i:
                        smm.append((ident[:], negc[:], [bid, cst]))
                    elif kt == i - 4:
                        smm.append((ident[:], negw[:], [bid, cst]))
                    pairs.append(dict(smm=smm, v=(VA[("win", g)][:, kt, :], [bVA[("win", g)]]), O=(4, kt == k0, kt == i), fin=None))
                attn_stream(S, pairs, PS, PSb, PT, bPT)
                if i >= 8:
                    nsb = nj // 4
                    S.op("dve", lambda e: e.tensor_reduce(out=t1i[:, :, 0:nsb], in_=v3(pp)[:, :, 0:nj].rearrange("p r (s q) -> p r s q", q=4), axis=AX.X, op=ALU.add), reads=[bpp], writes=[bsel])
                    S.op("dve", lambda e: e.tensor_reduce(out=imp[:, 0:nsb], in_=t1i[:, :, 0:nsb].rearrange("p r s -> p s r"), axis=AX.X, op=ALU.add), reads=[bsel], writes=[bsel])
                    S.op("pool", lambda e: e.memset(score[:], -BIG), reads=[bsel], writes=[bsel])
                    S.op("dve", lambda e: e.tensor_copy(out=score[:, 0:2 * i], in_=imp[:, 0:2 * i]), reads=[bsel], writes=[bsel])
                    S.op("dve", lambda e: e.tensor_scalar(out=score[:, 2 * i - 1:2 * i], in0=imp[:, 2 * i - 1:2 * i], scalar1=v12[:, 1:2], scalar2=None, op0=ALU.add), reads=[bsel, cst], writes=[bsel])
                    S.op("pool", lambda e: e.memset(score[:, 0:1], BIG), reads=[bsel], writes=[bsel])
                    S.op("pool", lambda e: e.memset(score[:, 2 * i:2 * i + 1], BIG), reads=[bsel], writes=[bsel])
                    S.op("dve", lambda e: e.tensor_copy(out=score[:, 2 * i + 1:2 * i + 2], in_=v12[:, 0:1]), reads=[bsel, cst], writes=[bsel])
                    S.op("dve", lambda e: e.max(out=m8[:, 0:8], in_=score[:]), reads=[bsel], writes=[bsel])
                    S.op("dve", lambda e: e.match_replace(out=sc2[:], in_to_replace=m8[:, 0:8], in_values=score[:], imm_value=-3.0e9), reads=[bsel], writes=[bsel])
                    S.op("dve", lambda e: e.max(out=m8[:, 8:16], in_=sc2[:]), reads=[bsel], writes=[bsel])
                    S.op("dve", lambda e: e.tensor_scalar(out=Z[:, 64:96], in0=score[:], scalar1=m8[:, 15:16], scalar2=NEG, op0=ALU.is_lt, op1=ALU.mult), reads=[bsel], writes=[bZ])
                    S.op("pe", lambda e: e.transpose(out=PB[:, 512:640], in_=Z[:, :], identity=ident[:]), reads=[bZ, bid], writes=[PBb])
                    S.op("act", lambda e: e.copy(out=QAv[64:96, i, :, :], in_=PB[64:96, 512:640].unsqueeze(1).to_broadcast([32, 4, 128])), reads=[PBb], writes=[bQA[g][i]])
                pairs = []
                for kt in range(0, i + 1):
                    ks = slice(kt * 128, (kt + 1) * 128)
                    smm = [(KA[("sel", g)][0:100, ks], rhsQ, [bKA[("sel", g)], bQA[g][i]])]
                    if kt == i:
                        smm.append((ident[:], negc[:], [bid, cst]))
                    pairs.append(dict(smm=smm, v=(VA[("sel", g)][:, kt, :], [bVA[("sel", g)]]), O=(3, kt == 0, kt == i), fin=None))
                attn_stream(S, pairs, PS, PSb, PT, bPT)
                for b in range(3):
                    for r in range(4):
                        col = (b * 8 + 4 * g + r) * 64
                        S.op("pe", lambda e: e.matmul(PS[6][0:64, r * 128:(r + 1) * 128], lhsT=selg[0:24, col:col + 64], rhs=gsig[0:24, ts], start=True, stop=True), reads=[cst, bgs], writes=[PSb[6]])
                    if b == 0:
                        S.op("act", lambda e: e.copy(out=gc[:], in_=PS[6][0:64, :]), reads=[PSb[6]], writes=[bgc])
                        S.op("dve", lambda e: e.tensor_tensor(out=acc[:], in0=PS[5][0:64, :], in1=gc[:], op=ALU.mult), reads=[PSb[5], bgc], writes=[bacc])
                    else:
                        ob = 3 if b == 1 else 4
                        S.op("dve", lambda e: e.reciprocal(out=rcs[:], in_=PS[ob][64:128, :]), reads=[PSb[ob]], writes=[brcs])
                        S.op("dve", lambda e: e.tensor_tensor(out=cfm[:], in0=PS[6][0:64, :], in1=rcs[:], op=ALU.mult), reads=[PSb[6], brcs], writes=[bcfm])
                        S.op("dve", lambda e: e.tensor_tensor(out=tb[:], in0=PS[ob][0:64, :], in1=cfm[:], op=ALU.mult), reads=[PSb[ob], bcfm], writes=[btb])
                        S.op("pool", lambda e: e.tensor_tensor(out=acc[:], in0=acc[:], in1=tb[:], op=ALU.add), reads=[btb], writes=[bacc])
                a3 = acc[:, :].rearrange("p (r t) -> p r t", r=4)
                S.op("pool", lambda e: e.tensor_copy(out=onsa[0:64, 2 * g:2 * g + 2, ts], in_=a3[:, 0::2, :]), reads=[bacc], writes=[bonsa])
                S.op("act", lambda e: e.copy(out=onsa[64:128, 2 * g:2 * g + 2, ts], in_=a3[:, 1::2, :]), reads=[bacc], writes=[bonsa])
        if dbg:
            S.dma(A["d_onsa"], onsa[:], reads=[bonsa], eng="pool")
        S.barrier()


def stage_merge(nc, S, A, sq, PS, PSb, PB, PBb, ident, bid, hT, bhT, onsa, bonsa, omla, bomla, x1b, nb, junk, bjunk):
    r0 = sq * T
    with ExitStack() as s2:
        sb = lambda name, shape, dt: s2.enter_context(_sbt(nc, name, shape, dt))
        wbn = sb("wbn", [128, 4, D], BF16)
        wbm = sb("wbm", [128, 4, D], BF16)
        wo = sb("wo", [128, 8, D], BF16)
        wmg = sb("wmg", [128, 8, 2048], BF16)
        gpo = sb("gpo", [128, D], F32)
        bw = Buf()
        S.dma(wmg[:], A["w_in"][:, :, 1720:3768], writes=[bw], eng="pool")
        S.dma(wbn[:], A["w_br_nsa"], writes=[bw], eng="pool")
        S.dma(wbm[:], A["w_br_mla"], writes=[bw], eng="pool")
        S.dma(wo[:], A["w_o"], writes=[bw], eng="pool")
        S.dma(gpo[:], A["g_post_mix"][0:1, :].to_broadcast([128, D]), writes=[bw])
        R = 2
        xt = [sb(f"xtm{i}", [128, D], F32) for i in range(R)]
        s0 = [sb(f"s0{i}", [128, 512], F32) for i in range(2)]
        m0 = [sb(f"m0{i}", [128, 512], F32) for i in range(2)]
        mgb = [sb(f"mgb{i}", [128, D], BF16) for i in range(R)]
        mgT = [sb(f"mgT{i}", [128, 8, 128], BF16) for i in range(R)]
        tmp = [sb(f"tmpm{i}", [128, D], F32) for i in range(R)]
        ssm = sb("ssm", [128, 8], F32)
        bxt, bmgb, bmgT, btmp, bssm = [[Buf() for _ in range(R)] for _ in range(5)]
        bs0 = [Buf() for _ in range(2)]
        bm0 = [Buf() for _ in range(2)]
        for i in range(16):
            k = i % R
            ts = slice(i * 128, (i + 1) * 128)
            S.dma(xt[k][:], A["x"][r0 + 128 * i: r0 + 128 * (i + 1), :], writes=[bxt[k]])
            for hf in range(2):
                cs = slice(hf * 512, (hf + 1) * 512)
                for br, (wb_, osrc, bo) in enumerate(((wbn, onsa, bonsa), (wbm, omla, bomla))):
                    ig = nb()
                    for kc in range(8):
                        S.op("pe", lambda e: e.matmul(PS[ig][:, :], lhsT=hT[:, kc, ts], rhs=wmg[:, kc, br * 1024 + hf * 512: br * 1024 + (hf + 1) * 512], start=(kc == 0), stop=(kc == 7)), reads=[bhT[i // 4], bw], writes=[PSb[ig]])
                    ibr = nb()
                    for c in range(4):
                        S.op("pe", lambda e: e.matmul(PS[ibr][:, :], lhsT=osrc[:, c, ts], rhs=wb_[:, c, cs], start=(c == 0), stop=(c == 3)), reads=[bo, bw], writes=[PSb[ibr]])
                    S.op("act", lambda e: e.activation(out=s0[br][:], in_=PS[ig][:, :], func=ACT.Sigmoid), reads=[PSb[ig]], writes=[bs0[br]])
                    S.op("dve", lambda e: e.tensor_tensor(out=m0[br][:], in0=PS[ibr][:, :], in1=s0[br][:], op=ALU.mult), reads=[PSb[ibr], bs0[br]], writes=[bm0[br]])
                S.op("pool", lambda e: e.tensor_tensor(out=mgb[k][:, cs], in0=m0[0][:], in1=m0[1][:], op=ALU.add), reads=[bm0[0], bm0[1]], writes=[bmgb[k]])
            for kc in range(8):
                S.op("pe", lambda e: e.transpose(out=PB[:, kc * 128:(kc + 1) * 128], in_=mgb[k][:, kc * 128:(kc + 1) * 128], identity=ident[:]), reads=[bmgb[k], bid], writes=[PBb])
            S.op("act", lambda e: e.copy(out=mgT[k][:], in_=PB[:, :].rearrange("p (k t) -> p k t", k=8)), reads=[PBb], writes=[bmgT[k]])
            iy = [nb(), nb()]
            for hf in range(2):
                for kc in range(8):
                    S.op("pe", lambda e: e.matmul(PS[iy[hf]][:, :], lhsT=mgT[k][:, kc, :], rhs=wo[:, kc, hf * 512:(hf + 1) * 512], start=(kc == 0), stop=(kc == 7)), reads=[bmgT[k], bw], writes=[PSb[iy[hf]]])
                S.op("act", lambda e: e.activation(out=junk[:, 0:512], in_=PS[iy[hf]][:, :], func=ACT.Square, accum_out=ssm[:, 4 * k + hf:4 * k + hf + 1]), reads=[PSb[iy[hf]]], writes=[bjunk, bssm[k]])
            S.op("dve", lambda e: e.tensor_tensor(out=ssm[:, 4 * k + 2:4 * k + 3], in0=ssm[:, 4 * k:4 * k + 1], in1=ssm[:, 4 * k + 1:4 * k + 2], op=ALU.add), reads=[bssm[k]], writes=[bssm[k]])
            rstd_from_ss(S, ssm[:, 4 * k + 2:4 * k + 3], ssm[:, 4 * k + 3:4 * k + 4], D, [bssm[k]], [bssm[k]])
            for hf in range(2):
                S.op("dve", lambda e: e.scalar_tensor_tensor(out=tmp[k][:, hf * 512:(hf + 1) * 512], in0=PS[iy[hf]][:, :], scalar=ssm[:, 4 * k + 3:4 * k + 4], in1=gpo[:, hf * 512:(hf + 1) * 512], op0=ALU.mult, op1=ALU.mult), reads=[PSb[iy[hf]], bssm[k], bw], writes=[btmp[k]])
            S.op("pool", lambda e: e.tensor_tensor(out=tmp[k][:], in0=tmp[k][:], in1=xt[k][:], op=ALU.add), reads=[bxt[k]], writes=[btmp[k]])
            S.dma(A["x1s"][r0 + 128 * i: r0 + 128 * (i + 1), :], tmp[k][:], reads=[btmp[k]], writes=[x1b[16 * sq + i]])


def kernel(**inp):
    inp = {k: np.asarray(v) for k, v in inp.items()}
    nc = build()
    consts = make_consts()
    w = host_weights(inp)
    in_maps = []
    for c in range(NCORES):
        m = {"x": np.ascontiguousarray(inp["x"][2 * c:2 * c + 2].reshape(2 * T, D)),
             "p": np.ascontiguousarray(inp["p"][0, 2 * c:2 * c + 2].reshape(2 * T, 256)),
             "pos": np.ascontiguousarray(inp["positions"][2 * c:2 * c + 2].astype(np.int32))}
        m.update(w)
        for k, v in consts.items():
            m["c_" + k] = v
        in_maps.append(m)
    res = run_bass_kernel_spmd(nc, in_maps, core_ids=list(range(NCORES)))
    out = np.stack([r["out"].reshape(2, T, D) for r in res.results], 0).reshape(16, T, D)
    return out.astype(np.float32)
```

```python
import os
import numpy as np
from contextlib import ExitStack
import concourse.bass as bass
import concourse.mybir as mybir
from concourse.bass_utils import run_bass_kernel_spmd

ACT = mybir.ActivationFunctionType
ALU = mybir.AluOpType
AX = mybir.AxisListType
F32 = mybir.dt.float32
BF16 = mybir.dt.bfloat16
I32 = mybir.dt.int32

NEG = -30000.0
BIG = 1.0e9
EPS = 1e-6
T = 2048
D = 1024
NCORES = 8
DFF = 2816
SC_N = 0.125
SC_M = 96.0 ** -0.5
TWO_PI = 6.283185307179586
PI = 3.141592653589793


_UNIQ = [0]


def _sbt(nc, name, shape, dt):
    _UNIQ[0] += 1
    return nc.sbuf_tensor(f"{name}_u{_UNIQ[0]}", shape, dt)


class Buf:
    __slots__ = ("w", "r", "x")

    def __init__(self, x=False):
        self.w = None
        self.r = {}
        self.x = x


class Sched:
    NDS = 24

    def __init__(self, nc, stack):
        self.nc = nc
        self.engs = {"pe": nc.tensor, "act": nc.scalar, "dve": nc.vector, "pool": nc.gpsimd, "sp": nc.sync}
        self.sem = {k: stack.enter_context(nc.semaphore(k + "_s")) for k in self.engs}
        self.cnt = {k: 0 for k in self.engs}
        self.seen = {k: {} for k in self.engs}
        self.dsem = [stack.enter_context(nc.semaphore(f"dq{i}")) for i in range(self.NDS)]
        self.dcnt = [0] * self.NDS
        self.dnext = 0

    def _semof(self, k):
        return self.dsem[k] if isinstance(k, int) else self.sem[k]

    def _waits(self, eng, reads, writes):
        deps = {}
        for b in reads:
            if b.w is not None and deps.get(b.w[0], 0) < b.w[1]:
                deps[b.w[0]] = b.w[1]
            if b.x:
                for k, v in b.r.items():
                    if k != eng and deps.get(k, 0) < v:
                        deps[k] = v
        for b in writes:
            if b.w is not None and deps.get(b.w[0], 0) < b.w[1]:
                deps[b.w[0]] = b.w[1]
            for k, v in b.r.items():
                if deps.get(k, 0) < v:
                    deps[k] = v
        e = self.engs[eng]
        for k, v in deps.items():
            if k == eng and eng in ("pe", "sp"):
                continue
            if self.seen[eng].get(k, 0) >= v:
                continue
            e.wait_ge(self._semof(k), v)
            self.seen[eng][k] = v

    def _mark(self, key, val, reads, writes):
        for b in writes:
            b.w = (key, val)
            b.r = {}
        for b in reads:
            if b not in writes:
                b.r[key] = val

    def op(self, eng, fn, reads=(), writes=()):
        self._waits(eng, reads, writes)
        ins = fn(self.engs[eng])
        self.cnt[eng] += 1
        ins.then_inc(self.sem[eng], 1)
        self._mark(eng, self.cnt[eng], reads, writes)

    def dma(self, out, in_, reads=(), writes=(), eng="sp", **kw):
        i = self.dnext
        self.dnext = (self.dnext + 1) % self.NDS
        e = self.engs[eng]
        if self.dcnt[i] > 0 and self.seen[eng].get(i, 0) < self.dcnt[i]:
            e.wait_ge(self.dsem[i], self.dcnt[i])
            self.seen[eng][i] = self.dcnt[i]
        self._waits(eng, reads, writes)
        ins = e.dma_start(out=out, in_=in_, **kw)
        self.dcnt[i] += 16
        ins.then_inc(self.dsem[i], 16)
        self._mark(i, self.dcnt[i], reads, writes)

    def barrier(self):
        for en, e in self.engs.items():
            for k in self.engs:
                if k != en and self.cnt[k] > self.seen[en].get(k, 0):
                    e.wait_ge(self.sem[k], self.cnt[k])
                    self.seen[en][k] = self.cnt[k]
            for i in range(self.NDS):
                if self.dcnt[i] > self.seen[en].get(i, 0):
                    e.wait_ge(self.dsem[i], self.dcnt[i])
                    self.seen[en][i] = self.dcnt[i]


def make_consts():
    c = {}
    c["ident"] = np.eye(128, dtype=np.float32)
    ds = np.arange(128)[:, None]
    dt = np.arange(128)[None, :]
    negc = np.where(ds <= dt, 0.0, NEG).astype(np.float32)
    negw = np.where(ds > dt, 0.0, NEG).astype(np.float32)
    c["negc"] = np.tile(negc, (1, 4))
    c["negw"] = np.tile(negw, (1, 4))
    dt5 = np.arange(512)[None, :]
    c["negm"] = np.stack([np.where(ds + 128 * m <= dt5, 0.0, NEG) for m in range(4)], 0).astype(np.float32)
    s = np.arange(T)
    E = (s[None, :] // 64 == np.arange(32)[:, None]).astype(np.float32)
    kal = np.stack([np.ones(T), np.ones(T), s // 128, s % 128], 0).astype(np.float32)
    c["kaug_sel"] = np.concatenate([E, kal], 0)
    c["kaug_win"] = np.concatenate([np.zeros_like(E), kal], 0)
    pj = 16 * np.arange(128) + 31
    kc = np.stack([np.ones(128), np.ones(128), pj // 128, pj % 128], 0).astype(np.float32)
    kc[:, 127] = 0.0
    c["kaug_cmp"] = np.concatenate([np.zeros((32, 128), np.float32), kc], 0)
    slopes = np.array([2.0 ** (-(h + 1)) for h in range(8)], dtype=np.float64)
    qa = np.zeros((2, 36, 16, 4, 128), np.float32)
    for g in range(2):
        for r in range(4):
            sl = slopes[4 * g + r]
            for i in range(16):
                qa[g, 32, i, r, :] = -sl * 128.0 * i
                qa[g, 33, i, r, :] = -sl * np.arange(128)
                qa[g, 34, i, r, :] = sl * 128.0
                qa[g, 35, i, r, :] = sl
    c["qaug"] = qa.reshape(2, 36, 16 * 4 * 128)
    selg = np.zeros((24, 24, 64), np.float32)
    for r in range(24):
        selg[r, r, :] = 1.0
    c["selg"] = selg.reshape(24, 24 * 64)
    cc = np.arange(128)[None, :] - 120
    dtt = np.arange(128)[:, None]
    c["cmptbl"] = np.where(16 * cc + 31 <= dtt, 0.0, NEG).astype(np.float32)
    v12 = np.zeros((128, 2), np.float32)
    v12[:, 0] = np.where(np.arange(128) < 64, -BIG, BIG)
    v12[:, 1] = np.where(np.arange(128) < 64, BIG, 0.0)
    c["v12"] = v12
    invf = (np.float32(10000.0) ** (-np.arange(0, 32, 2, dtype=np.float32) / np.float32(32))).astype(np.float32)
    c["invf"] = np.concatenate([invf, invf])[:, None].astype(np.float32)
    return c


WSHAPES = {
    "g_pre_mix": [1, D], "g_post_mix": [1, D], "g_pre_ffn": [1, D], "g_post_ffn": [1, D], "g_ple": [1, D],
    "w_in": [128, 8, 3768], "posT_k": [64, 32], "posT_v": [64, 32],
    "ck_w1": [64, 32, 64], "cv_w1": [64, 32, 64], "ck_b1": [64, 1], "cv_b1": [64, 1],
    "ck_w2": [64, 64], "cv_w2": [64, 64],
    "g_q": [128, 2], "w_uq": [128, 2, 768], "g_kv": [128, 1], "w_ukv": [128, 1024],
    "w_br_nsa": [128, 4, D], "w_br_mla": [128, 4, D], "w_o": [128, 8, D],
    "w_up": [128, 8, 2 * DFF], "w_conv": [128, 3, 44], "b_conv": [128, 44], "w_down": [128, 22, D],
    "w_ple": [128, 2, D], "w_ple_gate": [128, 8, D],
}


def host_weights(inp):
    w = {}
    f = lambda a: np.ascontiguousarray(a, dtype=np.float32)
    for k in ("g_pre_mix", "g_post_mix", "g_pre_ffn", "g_post_ffn", "g_ple"):
        w[k] = f(inp[k][0][None, :])
    kcp = lambda a, p=128: f(a.reshape(a.shape[0] // p, p, a.shape[1]).transpose(1, 0, 2))
    w["w_in"] = kcp(inp["w_in"][0])
    w["posT_k"] = f(inp["nsa_pos_k"][0].T)
    w["posT_v"] = f(inp["nsa_pos_v"][0].T)
    w["ck_w1"] = f(inp["nsa_ck_w1"][0].reshape(32, 64, 64).transpose(1, 0, 2))
    w["cv_w1"] = f(inp["nsa_cv_w1"][0].reshape(32, 64, 64).transpose(1, 0, 2))
    w["ck_b1"] = f(inp["nsa_ck_b1"][0][:, None])
    w["cv_b1"] = f(inp["nsa_cv_b1"][0][:, None])
    w["ck_w2"] = f(inp["nsa_ck_w2"][0])
    w["cv_w2"] = f(inp["nsa_cv_w2"][0])
    w["g_q"] = f(inp["mla_g_q"][0].reshape(2, 128).T)
    w["w_uq"] = kcp(inp["mla_w_uq"][0])
    w["g_kv"] = f(inp["mla_g_kv"][0][:, None])
    w["w_ukv"] = f(inp["mla_w_ukv"][0])
    w["w_br_nsa"] = kcp(inp["w_br_nsa"][0])
    w["w_br_mla"] = kcp(inp["w_br_mla"][0])
    w["w_o"] = kcp(inp["w_o"][0])
    w["w_up"] = kcp(inp["w_up"][0])
    w["w_conv"] = f(inp["w_conv"][0].reshape(3, 44, 128).transpose(2, 0, 1))
    w["b_conv"] = f(inp["b_conv"][0].reshape(44, 128).T)
    w["w_down"] = kcp(inp["w_down"][0])
    w["w_ple"] = kcp(inp["w_ple"][0])
    w["w_ple_gate"] = kcp(inp["w_ple_gate"][0])
    return w


def build(stages=("mix", "ffn", "ple"), dbg=False, nseq=2):
    nc = bass.Bass("TRN2", target_bir_lowering=False)
    consts = make_consts()
    A = {}
    A["x"] = nc.dram_tensor("x", [2 * T, D], F32, kind="ExternalInput").ap()
    A["p"] = nc.dram_tensor("p", [2 * T, 256], F32, kind="ExternalInput").ap()
    A["pos"] = nc.dram_tensor("pos", [2, T], I32, kind="ExternalInput").ap()
    for k, shp in WSHAPES.items():
        A[k] = nc.dram_tensor(k, shp, F32, kind="ExternalInput").ap()
    for k, v in consts.items():
        A[k] = nc.dram_tensor("c_" + k, list(v.shape), F32, kind="ExternalInput").ap()
    A["out"] = nc.dram_tensor("out", [2 * T, D], F32, kind="ExternalOutput").ap()
    if "mix" in stages:
        A["x1s"] = nc.dram_tensor("x1s", [2 * T, D], F32, kind="ExternalOutput" if dbg else "Internal").ap()
    else:
        A["x1s"] = nc.dram_tensor("x1s", [2 * T, D], F32, kind="ExternalInput").ap()
    if dbg:
        A["d_onsa"] = nc.dram_tensor("d_onsa", [128, 4, T], F32, kind="ExternalOutput").ap()
        A["d_omla"] = nc.dram_tensor("d_omla", [128, 4, T], F32, kind="ExternalOutput").ap()
    x1b = [Buf() for _ in range(32)]

    with ExitStack() as st:
        S = Sched(nc, st)
        sbg = lambda name, shape, dt: st.enter_context(_sbt(nc, name, shape, dt))
        PS = [st.enter_context(nc.psum_tensor(f"ps{i}", [128, 512], F32)) for i in range(7)]
        PSb = [Buf(True) for _ in range(7)]
        PB = st.enter_context(nc.psum_tensor("pb", [128, 1024], BF16))
        PBb = Buf(True)
        ident = sbg("ident", [128, 128], BF16)
        bid = Buf()
        S.dma(ident[:], A["ident"], writes=[bid], eng="pool")

        if "mix" in stages:
            for sq in range(nseq):
                stage_mix(nc, S, A, sq, PS, PSb, PB, PBb, ident, bid, x1b, dbg and sq == 0)
                S.barrier()
        if "ffn" in stages:
            stage_ffn(nc, S, A, PS, PSb, PB, PBb, ident, bid, x1b, nseq)
            S.barrier()
        if "ple" in stages:
            stage_ple(nc, S, A, PS, PSb, PB, PBb, ident, bid, x1b, nseq)
        S.barrier()
        print('COUNTS', S.cnt, max(S.dcnt))
    return nc


def rstd_from_ss(S, ss_ap, rstd_ap, n, rb, wb):
    S.op("act", lambda e: e.activation(out=rstd_ap, in_=ss_ap, func=ACT.Sqrt, scale=1.0 / n, bias=EPS), reads=rb, writes=wb)
    S.op("dve", lambda e: e.reciprocal(out=rstd_ap, in_=rstd_ap), reads=wb, writes=wb)


def stage_ffn(nc, S, A, PS, PSb, PB, PBb, ident, bid, x1b, nseq):
    with ExitStack() as s2:
        sb = lambda name, shape, dt: s2.enter_context(_sbt(nc, name, shape, dt))
        wup = sb("wup", [128, 8, 2 * DFF], BF16)
        bwup = [Buf() for _ in range(11)]
        wdn = sb("wdn", [128, 22, D], BF16)
        bwdn = [Buf() for _ in range(2)]
        wc = sb("wc", [128, 3, 44], F32)
        bc = sb("bc", [128, 44], F32)
        bwc = Buf()
        gpre = sb("gpre", [128, D], F32)
        gpost = sb("gpost", [128, D], F32)
        bg = Buf()
        S.dma(wc[:], A["w_conv"], writes=[bwc])
        S.dma(bc[:], A["b_conv"], writes=[bwc])
        S.dma(gpre[:], A["g_pre_ffn"][0:1, :].to_broadcast([128, D]), writes=[bg])
        S.dma(gpost[:], A["g_post_ffn"][0:1, :].to_broadcast([128, D]), writes=[bg])
        order = []
        for q in range(6):
            order.append(q)
            if q + 6 < 11:
                pass
        for q in range(11):
            S.dma(wup[:, :, q * 512:(q + 1) * 512], A["w_up"][:, :, q * 512:(q + 1) * 512], writes=[bwup[q]], eng="pool")
        for q in range(2):
            S.dma(wdn[:, q * 11:(q + 1) * 11, :], A["w_down"][:, q * 11:(q + 1) * 11, :], writes=[bwdn[q]], eng="pool")
        x1t = [sb(f"x1t{i}", [128, D], F32) for i in range(4)]
        bx1t = [Buf() for _ in range(4)]
        hn = [sb(f"hn{i}", [128, D], BF16) for i in range(2)]
        bhn = [Buf() for _ in range(2)]
        h2T = [sb(f"h2T{i}", [128, 8, 258], BF16) for i in range(2)]
        bh2T = [Buf() for _ in range(2)]
        junk = sb("junk", [128, D], BF16)
        bjunk = Buf()
        ssq = sb("ssq", [128, 8], F32)
        bss = [Buf() for _ in range(4)]
        cA = [sb(f"cA{i}", [128, 256], F32) for i in range(2)]
        cV = [sb(f"cV{i}", [128, 256], F32) for i in range(2)]
        gA = [sb(f"gA{i}", [128, 256], F32) for i in range(2)]
        bcA = [Buf() for _ in range(2)]
        bcV = [Buf() for _ in range(2)]
        bgA = [Buf() for _ in range(2)]
        actT = [sb(f"actT{i}", [128, 22, 256], BF16) for i in range(2)]
        bact = [[Buf() for _ in range(22)] for _ in range(2)]
        tmp = [sb(f"tmp{i}", [128, D], F32) for i in range(2)]
        btmp = [Buf() for _ in range(2)]
        ss2 = sb("ss2", [128, 8], F32)
        bss2 = [Buf() for _ in range(2)]
        uprot = [4, 5, 6]
        upi = 0
        nblk = 8 * nseq
        def front(b):
            r0 = 256 * b
            hT = h2T[b % 2]
            bh = bh2T[b % 2]
            for tt in range(2):
                xi = (2 * b + tt) % 4
                gi = 2 * b + tt
                S.dma(x1t[xi][:], A["x1s"][r0 + 128 * tt: r0 + 128 * (tt + 1), :], reads=[x1b[gi]], writes=[bx1t[xi]])
                S.op("act", lambda e: e.activation(out=junk[:], in_=x1t[xi][:], func=ACT.Square, accum_out=ssq[:, xi:xi + 1]), reads=[bx1t[xi]], writes=[bjunk, bss[xi]])
                rstd_from_ss(S, ssq[:, xi:xi + 1], ssq[:, 4 + xi:5 + xi], D, [bss[xi]], [bss[xi]])
                S.op("dve", lambda e: e.scalar_tensor_tensor(out=hn[tt][:], in0=x1t[xi][:], scalar=ssq[:, 4 + xi:5 + xi], in1=gpre[:], op0=ALU.mult, op1=ALU.mult), reads=[bx1t[xi], bss[xi], bg], writes=[bhn[tt]])
                for kc in range(8):
                    S.op("pe", lambda e: e.transpose(out=PB[:, kc * 128:(kc + 1) * 128], in_=hn[tt][:, kc * 128:(kc + 1) * 128], identity=ident[:]), reads=[bhn[tt], bid], writes=[PBb])
                S.op("act", lambda e: e.copy(out=hT[:, :, 2 + 128 * tt: 2 + 128 * (tt + 1)], in_=PB[:, :].rearrange("p (k t) -> p k t", k=8)), reads=[PBb], writes=[bh])
            if b % 8 == 0:
                S.op("pool", lambda e: e.memset(hT[:, :, 0:2], 0.0), writes=[bh])
            else:
                S.op("pool", lambda e: e.tensor_copy(out=hT[:, :, 0:2], in_=h2T[(b - 1) % 2][:, :, 256:258]), reads=[bh2T[(b - 1) % 2]], writes=[bh])

        def up(b, j):
            nonlocal upi
            hT = h2T[b % 2]
            bh = bh2T[b % 2]
            ia = uprot[upi % 3]
            upi += 1
            iv = uprot[upi % 3]
            upi += 1
            ca, cv, ga = cA[j % 2], cV[j % 2], gA[j % 2]
            bca, bcv, bga = bcA[j % 2], bcV[j % 2], bgA[j % 2]
            for (ib, col0) in ((ia, j * 128), (iv, DFF + j * 128)):
                for kc in range(8):
                    S.op("pe", lambda e: e.matmul(PS[ib][:, 0:258], lhsT=wup[:, kc, col0:col0 + 128], rhs=hT[:, kc, :], start=(kc == 0), stop=(kc == 7)),
                         reads=[bwup[col0 // 512], bh], writes=[PSb[ib]])
            for (ib, cc, bcc, jj) in ((ia, ca, bca, j), (iv, cv, bcv, 22 + j)):
                S.op("act", lambda e: e.activation(out=cc[:], in_=PS[ib][:, 2:258], func=ACT.Identity, scale=wc[:, 2, jj:jj + 1], bias=bc[:, jj:jj + 1]), reads=[PSb[ib], bwc], writes=[bcc])
                S.op("dve", lambda e: e.scalar_tensor_tensor(out=cc[:], in0=PS[ib][:, 1:257], scalar=wc[:, 1, jj:jj + 1], in1=cc[:], op0=ALU.mult, op1=ALU.add), reads=[PSb[ib], bwc], writes=[bcc])
                S.op("dve", lambda e: e.scalar_tensor_tensor(out=cc[:], in0=PS[ib][:, 0:256], scalar=wc[:, 0, jj:jj + 1], in1=cc[:], op0=ALU.mult, op1=ALU.add), reads=[PSb[ib], bwc], writes=[bcc])
            S.op("act", lambda e: e.activation(out=ga[:], in_=ca[:], func=ACT.Gelu_apprx_tanh), reads=[bca], writes=[bga])
            S.op("pool", lambda e: e.tensor_tensor(out=actT[b % 2][:, j, :], in0=ga[:], in1=cv[:], op=ALU.mult), reads=[bga, bcv], writes=[bact[b % 2][j]])

        def down(b):
            r0 = 256 * b
            for tt in range(2):
                xi = (2 * b + tt) % 4
                gi = 2 * b + tt
                for j in range(22):
                    for hf in range(2):
                        S.op("pe", lambda e: e.matmul(PS[2 * tt + hf][:, :], lhsT=actT[b % 2][:, j, 128 * tt:128 * (tt + 1)], rhs=wdn[:, j, hf * 512:(hf + 1) * 512], start=(j == 0), stop=(j == 21)),
                             reads=[bact[b % 2][j], bwdn[j // 11]], writes=[PSb[2 * tt + hf]])
                for hf in range(2):
                    S.op("act", lambda e: e.activation(out=junk[:, 0:512], in_=PS[2 * tt + hf][:, :], func=ACT.Square, accum_out=ss2[:, 4 * tt + hf:4 * tt + hf + 1]), reads=[PSb[2 * tt + hf]], writes=[bjunk, bss2[tt]])
                S.op("dve", lambda e: e.tensor_tensor(out=ss2[:, 4 * tt + 2:4 * tt + 3], in0=ss2[:, 4 * tt:4 * tt + 1], in1=ss2[:, 4 * tt + 1:4 * tt + 2], op=ALU.add), reads=[bss2[tt]], writes=[bss2[tt]])
                rstd_from_ss(S, ss2[:, 4 * tt + 2:4 * tt + 3], ss2[:, 4 * tt + 3:4 * tt + 4], D, [bss2[tt]], [bss2[tt]])
                for hf in range(2):
                    S.op("dve", lambda e: e.scalar_tensor_tensor(out=tmp[tt][:, hf * 512:(hf + 1) * 512], in0=PS[2 * tt + hf][:, :], scalar=ss2[:, 4 * tt + 3:4 * tt + 4], in1=gpost[:, hf * 512:(hf + 1) * 512], op0=ALU.mult, op1=ALU.mult),
                         reads=[PSb[2 * tt + hf], bss2[tt], bg], writes=[btmp[tt]])
                S.op("pool", lambda e: e.tensor_tensor(out=tmp[tt][:], in0=tmp[tt][:], in1=x1t[xi][:], op=ALU.add), reads=[bx1t[xi]], writes=[btmp[tt]])
                S.dma(A["x1s"][r0 + 128 * tt: r0 + 128 * (tt + 1), :], tmp[tt][:], reads=[btmp[tt]], writes=[x1b[gi]], eng="pool")

        front(0)
        for b in range(nblk):
            for j in range(22):
                up(b, j)
                if j == 12 and b + 1 < nblk:
                    front(b + 1)
            down(b)


def stage_ple(nc, S, A, PS, PSb, PB, PBb, ident, bid, x1b, nseq):
    with ExitStack() as s2:
        sb = lambda name, shape, dt: s2.enter_context(_sbt(nc, name, shape, dt))
        wpg = sb("wpg", [128, 8, D], BF16)
        wpl = sb("wpl", [128, 2, D], BF16)
        bw = Buf()
        gple = sb("gple", [128, D], F32)
        S.dma(wpg[:], A["w_ple_gate"], writes=[bw], eng="pool")
        S.dma(wpl[:], A["w_ple"], writes=[bw], eng="pool")
        S.dma(gple[:], A["g_ple"][0:1, :].to_broadcast([128, D]), writes=[bw])
        R = 6
        x2t = [sb(f"x2t{i}", [128, D], F32) for i in range(R)]
        pt = [sb(f"pt{i}", [128, 256], F32) for i in range(R)]
        x2b = [sb(f"x2b{i}", [128, D], BF16) for i in range(R)]
        pbf = [sb(f"pbf{i}", [128, 256], BF16) for i in range(R)]
        x2T = [sb(f"x2T{i}", [128, 8, 128], BF16) for i in range(R)]
        pT = [sb(f"pT{i}", [128, 2, 128], BF16) for i in range(R)]
        sg = [sb(f"sg{i}", [128, D], F32) for i in range(R)]
        eg = [sb(f"eg{i}", [128, D], F32) for i in range(R)]
        junk = sb("junkp", [128, D], BF16)
        ssp = sb("ssp", [128, 2 * R], F32)
        bx2t, bpt, bx2b, bpbf, bx2T, bpT, bsg, beg, bssp = [[Buf() for _ in range(R)] for _ in range(9)]
        bjunk = Buf()
        ntile = 16 * nseq

        def load(i):
            k = i % R
            rows = slice(128 * i, 128 * (i + 1))
            S.dma(x2t[k][:], A["x1s"][rows, :], reads=[x1b[i]], writes=[bx2t[k]])
            S.dma(pt[k][:], A["p"][rows, :], writes=[bpt[k]])

        def front(i):
            k = i % R
            S.op("dve", lambda e: e.tensor_copy(out=x2b[k][:], in_=x2t[k][:]), reads=[bx2t[k]], writes=[bx2b[k]])
            S.op("pool", lambda e: e.tensor_copy(out=pbf[k][:], in_=pt[k][:]), reads=[bpt[k]], writes=[bpbf[k]])
            for kc in range(8):
                S.op("pe", lambda e: e.transpose(out=PB[:, kc * 128:(kc + 1) * 128], in_=x2b[k][:, kc * 128:(kc + 1) * 128], identity=ident[:]), reads=[bx2b[k], bid], writes=[PBb])
            S.op("act", lambda e: e.copy(out=x2T[k][:], in_=PB[:, :].rearrange("p (k t) -> p k t", k=8)), reads=[PBb], writes=[bx2T[k]])
            for kc in range(2):
                S.op("pe", lambda e: e.transpose(out=PB[:, kc * 128:(kc + 1) * 128], in_=pbf[k][:, kc * 128:(kc + 1) * 128], identity=ident[:]), reads=[bpbf[k], bid], writes=[PBb])
            S.op("act", lambda e: e.copy(out=pT[k][:], in_=PB[:, 0:256].rearrange("p (k t) -> p k t", k=2)), reads=[PBb], writes=[bpT[k]])

        def back(i):
            k = i % R
            rows = slice(128 * i, 128 * (i + 1))
            for hf in range(2):
                st_ = (2 * i + hf) % 3
                ig, ie = 2 * st_, 2 * st_ + 1
                cs = slice(hf * 512, (hf + 1) * 512)
                for kc in range(8):
                    S.op("pe", lambda e: e.matmul(PS[ig][:, :], lhsT=x2T[k][:, kc, :], rhs=wpg[:, kc, cs], start=(kc == 0), stop=(kc == 7)), reads=[bx2T[k], bw], writes=[PSb[ig]])
                for kc in range(2):
                    S.op("pe", lambda e: e.matmul(PS[ie][:, :], lhsT=pT[k][:, kc, :], rhs=wpl[:, kc, cs], start=(kc == 0), stop=(kc == 1)), reads=[bpT[k], bw], writes=[PSb[ie]])
                S.op("act", lambda e: e.activation(out=sg[k][:, cs], in_=PS[ig][:, :], func=ACT.Sigmoid), reads=[PSb[ig]], writes=[bsg[k]])
                S.op("dve", lambda e: e.tensor_tensor(out=eg[k][:, cs], in0=PS[ie][:, :], in1=sg[k][:, cs], op=ALU.mult), reads=[PSb[ie], bsg[k]], writes=[beg[k]])
            S.op("act", lambda e: e.activation(out=junk[:], in_=eg[k][:], func=ACT.Square, accum_out=ssp[:, 2 * k:2 * k + 1]), reads=[beg[k]], writes=[bjunk, bssp[k]])

        def back3(i):
            k = i % R
            rows = slice(128 * i, 128 * (i + 1))
            rstd_from_ss(S, ssp[:, 2 * k:2 * k + 1], ssp[:, 2 * k + 1:2 * k + 2], D, [bssp[k]], [bssp[k]])
            S.op("dve", lambda e: e.scalar_tensor_tensor(out=eg[k][:], in0=eg[k][:], scalar=ssp[:, 2 * k + 1:2 * k + 2], in1=gple[:], op0=ALU.mult, op1=ALU.mult), reads=[bssp[k], bw], writes=[beg[k]])
            S.op("pool", lambda e: e.tensor_tensor(out=eg[k][:], in0=eg[k][:], in1=x2t[k][:], op=ALU.add), reads=[bx2t[k]], writes=[beg[k]])
            S.dma(A["out"][rows, :], eg[k][:], reads=[beg[k]], eng="pool")

        for i in range(4):
            load(i)
        front(0)
        front(1)
        for i in range(ntile + 1):
            if i >= 1:
                back3(i - 1)
            if i < ntile:
                back(i)
            if i + 4 < ntile:
                load(i + 4)
            if i + 2 < ntile:
                front(i + 2)


def attn_stream(S, pairs, PS, PSb, PT, bPT, srot=(0, 1, 2), skew=2):
    n = len(pairs)
    NP = len(PT)

    def emitS(k):
        pr = pairs[k]
        ib = srot[k % len(srot)]
        m = len(pr["smm"])
        for q, (l, r, rd) in enumerate(pr["smm"]):
            S.op("pe", lambda e: e.matmul(PS[ib][:, :], lhsT=l, rhs=r, start=(q == 0), stop=(q == m - 1)), reads=rd, writes=[PSb[ib]])

    for k in range(min(skew, n)):
        emitS(k)
    for k in range(n):
        if k + skew < n:
            emitS(k + skew)
        pr = pairs[k]
        ib = srot[k % len(srot)]
        S.op("act", lambda e: e.activation(out=PT[k % NP][:], in_=PS[ib][:, :], func=ACT.Exp), reads=[PSb[ib]], writes=[bPT[k % NP]])
        ob, first, last = pr["O"]
        vl, vrd = pr["v"]
        S.op("pe", lambda e: e.matmul(PS[ob][:, :], lhsT=vl, rhs=PT[k % NP][:], start=first, stop=last), reads=[bPT[k % NP]] + vrd, writes=[PSb[ob]])
        if pr.get("fin") is not None:
            pr["fin"]()


def stage_mix(nc, S, A, sq, PS, PSb, PB, PBb, ident, bid, x1b, dbg):
    r0 = sq * T
    with ExitStack() as s1:
        sb = lambda name, shape, dt: s1.enter_context(_sbt(nc, name, shape, dt))
        nbc = [0]

        def nb():
            nbc[0] = (nbc[0] + 1) % 7
            return nbc[0]

        cst = Buf()
        negc = sb("negc", [128, 512], BF16)
        negw = sb("negw", [128, 512], BF16)
        negm = sb("negm", [128, 4, 512], BF16)
        selg = sb("selg", [24, 24 * 64], BF16)
        cmptbl = sb("cmptbl", [128, 128], F32)
        v12 = sb("v12", [128, 2], F32)
        invf = sb("invf", [32, 1], F32)
        ones = sb("ones", [128, 128], BF16)
        S.dma(negc[:], A["negc"], writes=[cst], eng="pool")
        S.dma(negw[:], A["negw"], writes=[cst], eng="pool")
        S.dma(negm[:], A["negm"].rearrange("m p t -> p m t"), writes=[cst], eng="pool")
        S.dma(selg[:], A["selg"], writes=[cst], eng="pool")
        S.dma(cmptbl[:], A["cmptbl"], writes=[cst])
        S.dma(v12[:], A["v12"], writes=[cst])
        S.dma(invf[:], A["invf"], writes=[cst])
        S.op("pool", lambda e: e.memset(ones[:], 1.0), writes=[cst])
        gpm = sb("gpm", [128, D], F32)
        S.dma(gpm[:], A["g_pre_mix"][0:1, :].to_broadcast([128, D]), writes=[cst])

        hT = sb("hT", [128, 8, T], BF16)
        bhT = [Buf() for _ in range(4)]
        onsa = sb("onsa", [128, 4, T], BF16)
        omla = sb("omla", [128, 4, T], BF16)
        bonsa = Buf()
        bomla = Buf()
        junk = sb("junkm", [128, D], BF16)
        bjunk = Buf()
        sst = sb("sst", [128, 8], F32)

        with ExitStack() as s2:
            sb2 = lambda name, shape, dt: s2.enter_context(_sbt(nc, name, shape, dt))
            xt = [sb2(f"xt{i}", [128, D], F32) for i in range(2)]
            hn = [sb2(f"hnm{i}", [128, D], BF16) for i in range(2)]
            bxt = [Buf() for _ in range(2)]
            bhn = [Buf() for _ in range(2)]
            bs = [Buf() for _ in range(2)]
            for i in range(16):
                k = i % 2
                S.dma(xt[k][:], A["x"][r0 + 128 * i: r0 + 128 * (i + 1), :], writes=[bxt[k]])
                S.op("act", lambda e: e.activation(out=junk[:], in_=xt[k][:], func=ACT.Square, accum_out=sst[:, k:k + 1]), reads=[bxt[k]], writes=[bjunk, bs[k]])
                rstd_from_ss(S, sst[:, k:k + 1], sst[:, 2 + k:3 + k], D, [bs[k]], [bs[k]])
                S.op("dve", lambda e: e.scalar_tensor_tensor(out=hn[k][:], in0=xt[k][:], scalar=sst[:, 2 + k:3 + k], in1=gpm[:], op0=ALU.mult, op1=ALU.mult), reads=[bxt[k], bs[k], cst], writes=[bhn[k]])
                for kc in range(8):
                    S.op("pe", lambda e: e.transpose(out=PB[:, kc * 128:(kc + 1) * 128], in_=hn[k][:, kc * 128:(kc + 1) * 128], identity=ident[:]), reads=[bhn[k], bid], writes=[PBb])
                S.op("act", lambda e: e.copy(out=hT[:, :, i * 128:(i + 1) * 128], in_=PB[:, :].rearrange("p (k t) -> p k t", k=8)), reads=[PBb], writes=[bhT[i // 4]])

        with ExitStack() as s2:
            sb2 = lambda name, shape, dt: s2.enter_context(_sbt(nc, name, shape, dt))
            s3 = ExitStack()
            sb3 = lambda name, shape, dt: s3.enter_context(_sbt(nc, name, shape, dt))
            cos2 = sb2("cos2", [32, T], F32)
            sin2 = sb2("sin2", [32, T], F32)
            brope = Buf()
            krope = sb2("krope", [32, T], BF16)
            cqn = sb2("cqn", [128, 2, T], BF16)
            ckvn = sb2("ckvn", [128, T], BF16)
            t1 = sb2("t1", [32, 512], F32)
            t2 = sb2("t2", [32, 512], F32)
            posi = sb3("posi", [32, T], I32)
            ang = sb3("ang", [32, T], F32)
            tmpa = sb3("tmpa", [32, T], F32)
            S.dma(posi[:], A["pos"][sq:sq + 1, :].to_broadcast([32, T]), writes=[brope])
            S.op("dve", lambda e: e.tensor_copy(out=ang[:], in_=posi[:]), reads=[brope], writes=[brope])
            S.op("dve", lambda e: e.tensor_scalar(out=ang[:], in0=ang[:], scalar1=invf[:, 0:1], scalar2=None, op0=ALU.mult), reads=[cst], writes=[brope])
            qi = sb3("qi", [32, T], I32)
            for (addc, dst) in ((0.0, sin2), (PI / 2, cos2)):
                S.op("dve", lambda e: e.tensor_scalar(out=tmpa[:], in0=ang[:], scalar1=addc, scalar2=1.0 / TWO_PI, op0=ALU.add, op1=ALU.mult), reads=[brope], writes=[brope])
                S.op("dve", lambda e: e.tensor_copy(out=qi[:], in_=tmpa[:]), reads=[brope], writes=[brope])
                S.op("dve", lambda e: e.tensor_copy(out=tmpa[:], in_=qi[:]), reads=[brope], writes=[brope])
                S.op("dve", lambda e: e.scalar_tensor_tensor(out=tmpa[:], in0=tmpa[:], scalar=-TWO_PI, in1=ang[:], op0=ALU.mult, op1=ALU.add), reads=[brope], writes=[brope])
                if addc != 0.0:
                    S.op("dve", lambda e: e.tensor_scalar(out=tmpa[:], in0=tmpa[:], scalar1=addc, scalar2=None, op0=ALU.add), reads=[brope], writes=[brope])
                S.op("dve", lambda e: e.tensor_scalar(out=dst[:], in0=tmpa[:], scalar1=PI, scalar2=-TWO_PI, op0=ALU.is_gt, op1=ALU.mult), reads=[brope], writes=[brope])
                S.op("dve", lambda e: e.tensor_tensor(out=tmpa[:], in0=tmpa[:], in1=dst[:], op=ALU.add), reads=[brope], writes=[brope])
                S.op("dve", lambda e: e.tensor_scalar(out=tmpa[:], in0=tmpa[:], scalar1=-PI, scalar2=PI, op0=ALU.max, op1=ALU.min), reads=[brope], writes=[brope])
                S.op("act", lambda e: e.activation(out=dst[:], in_=tmpa[:], func=ACT.Sin), reads=[brope], writes=[brope])
            wA = sb3("wA", [128, 8, 416], BF16)
            wkrot = sb3("wkrot", [128, 8, 32], BF16)
            bwA = Buf()
            S.dma(wA[:], A["w_in"][:, :, 1304:1720], writes=[bwA], eng="pool")
            S.op("pool", lambda e: e.tensor_scalar(out=wkrot[:, :, 0:16], in0=wA[:, :, 400:416], scalar1=-1.0, scalar2=None, op0=ALU.mult), reads=[bwA], writes=[bwA])
            S.op("pool", lambda e: e.tensor_copy(out=wkrot[:, :, 16:32], in_=wA[:, :, 384:400]), reads=[bwA], writes=[bwA])
            gq = sb3("gq", [128, 2], F32)
            gkv = sb3("gkv", [128, 1], F32)
            S.dma(gq[:], A["g_q"], writes=[bwA])
            S.dma(gkv[:], A["g_kv"], writes=[bwA])
            bcq = Buf()
            cf = [sb3(f"cf{m}", [128, 512], F32) for m in range(3)]
            sqb = [sb3(f"sqb{m}", [128, 512], BF16) for m in range(3)]
            rq = sb3("rq", [128, 512], F32)
            rk = sb3("rk", [128, 512], F32)
            bcf = [Buf() for _ in range(3)]
            bsqb = [Buf() for _ in range(3)]
            brq, brk, bt1, bt2 = Buf(), Buf(), Buf(), Buf()
            for c in range(4):
                cs = slice(c * 512, (c + 1) * 512)
                for m, col0 in enumerate((0, 128, 256)):
                    ib = nb()
                    for kc in range(8):
                        S.op("pe", lambda e: e.matmul(PS[ib][:, :], lhsT=wA[:, kc, col0:col0 + 128], rhs=hT[:, kc, cs], start=(kc == 0), stop=(kc == 7)), reads=[bwA, bhT[c]], writes=[PSb[ib]])
                    S.op("act", lambda e: e.copy(out=cf[m][:], in_=PS[ib][:, :]), reads=[PSb[ib]], writes=[bcf[m]])
                    S.op("act", lambda e: e.activation(out=sqb[m][:], in_=PS[ib][:, :], func=ACT.Square), reads=[PSb[ib]], writes=[bsqb[m]])
                iq = nb()
                S.op("pe", lambda e: e.matmul(PS[iq][:, :], lhsT=ones[:], rhs=sqb[0][:], start=True, stop=False), reads=[cst, bsqb[0]], writes=[PSb[iq]])
                S.op("pe", lambda e: e.matmul(PS[iq][:, :], lhsT=ones[:], rhs=sqb[1][:], start=False, stop=True), reads=[cst, bsqb[1]], writes=[PSb[iq]])
                ik = nb()
                S.op("pe", lambda e: e.matmul(PS[ik][:, :], lhsT=ones[:], rhs=sqb[2][:], start=True, stop=True), reads=[cst, bsqb[2]], writes=[PSb[ik]])
                S.op("act", lambda e: e.activation(out=rq[:], in_=PS[iq][:, :], func=ACT.Sqrt, scale=1.0 / 256, bias=EPS), reads=[PSb[iq]], writes=[brq])
                S.op("dve", lambda e: e.reciprocal(out=rq[:], in_=rq[:]), reads=[brq], writes=[brq])
                S.op("act", lambda e: e.activation(out=rk[:], in_=PS[ik][:, :], func=ACT.Sqrt, scale=1.0 / 128, bias=EPS), reads=[PSb[ik]], writes=[brk])
                S.op("dve", lambda e: e.reciprocal(out=rk[:], in_=rk[:]), reads=[brk], writes=[brk])
                for m in range(2):
                    S.op("dve", lambda e: e.scalar_tensor_tensor(out=cqn[:, m, cs], in0=cf[m][:], scalar=gq[:, m:m + 1], in1=rq[:], op0=ALU.mult, op1=ALU.mult), reads=[bcf[m], brq, bwA], writes=[bcq])
                S.op("dve", lambda e: e.scalar_tensor_tensor(out=ckvn[:, cs], in0=cf[2][:], scalar=gkv[:, 0:1], in1=rk[:], op0=ALU.mult, op1=ALU.mult), reads=[bcf[2], brk, bwA], writes=[bcq])
                i1 = nb()
                i2 = nb()
                for kc in range(8):
                    S.op("pe", lambda e: e.matmul(PS[i1][0:32, :], lhsT=wA[:, kc, 384:416], rhs=hT[:, kc, cs], start=(kc == 0), stop=(kc == 7)), reads=[bwA, bhT[c]], writes=[PSb[i1]])
                for kc in range(8):
                    S.op("pe", lambda e: e.matmul(PS[i2][0:32, :], lhsT=wkrot[:, kc, :], rhs=hT[:, kc, cs], start=(kc == 0), stop=(kc == 7)), reads=[bwA, bhT[c]], writes=[PSb[i2]])
                S.op("dve", lambda e: e.tensor_tensor(out=t1[:], in0=PS[i1][0:32, :], in1=cos2[:, cs], op=ALU.mult), reads=[PSb[i1], brope], writes=[bt1])
                S.op("dve", lambda e: e.tensor_tensor(out=t2[:], in0=PS[i2][0:32, :], in1=sin2[:, cs], op=ALU.mult), reads=[PSb[i2], brope], writes=[bt2])
                S.op("pool", lambda e: e.tensor_tensor(out=krope[:, cs], in0=t1[:], in1=t2[:], op=ALU.add), reads=[bt1, bt2], writes=[bcq])

            S.barrier()
            s3.close()
            wuq = sb2("wuq", [128, 2, 768], BF16)
            wuqr = sb2("wuqr", [128, 2, 8, 32], BF16)
            wukv = sb2("wukv", [128, 1024], BF16)
            bwu = Buf()
            S.dma(wuq[:], A["w_uq"], writes=[bwu], eng="pool")
            S.dma(wukv[:], A["w_ukv"], writes=[bwu], eng="pool")
            wuq4 = wuq[:, :, :].rearrange("p m (h c) -> p m h c", h=8)
            for m in range(2):
                S.op("pool", lambda e: e.tensor_scalar(out=wuqr[:, m, :, 0:16], in0=wuq4[:, m, :, 80:96], scalar1=-1.0, scalar2=None, op0=ALU.mult), reads=[bwu], writes=[bwu])
                S.op("pool", lambda e: e.tensor_copy(out=wuqr[:, m, :, 16:32], in_=wuq4[:, m, :, 64:80]), reads=[bwu], writes=[bwu])
            VAm = sb2("VAm", [128, 16, 8, 128], BF16)
            bVA = Buf()
            S.op("pool", lambda e: e.memset(VAm[:], 1.0), writes=[bVA])
            wukv3 = wukv[:, :].rearrange("p (h c) -> p h c", h=8)
            for kt in range(16):
                ib = nb()
                S.op("pe", lambda e: e.matmul(PS[ib][:, :], lhsT=ckvn[:, kt * 128:(kt + 1) * 128], rhs=wukv3[:, :, 64:128], start=True, stop=True), reads=[bcq, bwu], writes=[PSb[ib]])
                S.op("act", lambda e: e.copy(out=VAm[:, kt, :, 0:64], in_=PS[ib][:, :].rearrange("p (h c) -> p h c", h=8)), reads=[PSb[ib]], writes=[bVA])
            qnt = [sb2(f"qnt{i}", [64, T], BF16) for i in range(2)]
            qrt = [sb2(f"qrt{i}", [32, T], BF16) for i in range(2)]
            knt = [sb2(f"knt{i}", [64, T], BF16) for i in range(2)]
            bqk = [Buf() for _ in range(2)]
            PT = [sb2(f"PT{i}", [128, 512], BF16) for i in range(4)]
            bPT = [Buf() for _ in range(4)]
            rcm = sb2("rcm", [64, 512], F32)
            brcm = Buf()
            for h in range(8):
                hk = h % 2
                qn, qr, kn = qnt[hk], qrt[hk], knt[hk]
                for c in range(4):
                    cs = slice(c * 512, (c + 1) * 512)
                    ib = 3 + (c % 4)
                    for m in range(2):
                        S.op("pe", lambda e: e.matmul(PS[ib][0:64, :], lhsT=wuq[:, m, h * 96:h * 96 + 64], rhs=cqn[:, m, cs], start=(m == 0), stop=(m == 1)), reads=[bwu, bcq], writes=[PSb[ib]])
                    S.op("act", lambda e: e.mul(out=qn[:, cs], in_=PS[ib][0:64, :], mul=SC_M), reads=[PSb[ib]], writes=[bqk[hk]])
                    S.op("pe", lambda e: e.matmul(PS[ib][0:64, :], lhsT=wukv[:, h * 128:h * 128 + 64], rhs=ckvn[:, cs], start=True, stop=True), reads=[bwu, bcq], writes=[PSb[ib]])
                    S.op("act", lambda e: e.copy(out=kn[:, cs], in_=PS[ib][0:64, :]), reads=[PSb[ib]], writes=[bqk[hk]])
                    i1 = 3 + ((c + 1) % 4)
                    i2 = 3 + ((c + 2) % 4)
                    for m in range(2):
                        S.op("pe", lambda e: e.matmul(PS[i1][0:32, :], lhsT=wuq[:, m, h * 96 + 64:h * 96 + 96], rhs=cqn[:, m, cs], start=(m == 0), stop=(m == 1)), reads=[bwu, bcq], writes=[PSb[i1]])
                    for m in range(2):
                        S.op("pe", lambda e: e.matmul(PS[i2][0:32, :], lhsT=wuqr[:, m, h, :], rhs=cqn[:, m, cs], start=(m == 0), stop=(m == 1)), reads=[bwu, bcq], writes=[PSb[i2]])
                    S.op("dve", lambda e: e.tensor_tensor(out=t1[:], in0=PS[i1][0:32, :], in1=cos2[:, cs], op=ALU.mult), reads=[PSb[i1], brope], writes=[bt1])
                    S.op("dve", lambda e: e.tensor_tensor(out=t2[:], in0=PS[i2][0:32, :], in1=sin2[:, cs], op=ALU.mult), reads=[PSb[i2], brope], writes=[bt2])
                    S.op("pool", lambda e: e.tensor_tensor(out=t1[:], in0=t1[:], in1=t2[:], op=ALU.add), reads=[bt2], writes=[bt1])
                    S.op("act", lambda e: e.mul(out=qr[:, cs], in_=t1[:], mul=SC_M), reads=[bt1], writes=[bqk[hk]])
                pairs = []
                for c in range(4):
                    cs = slice(c * 512, (c + 1) * 512)
                    ob = 3 + (c % 3)

                    def fin(c=c, cs=cs, ob=ob):
                        S.op("dve", lambda e: e.reciprocal(out=rcm[:], in_=PS[ob][64:128, :]), reads=[PSb[ob]], writes=[brcm])
                        S.op("dve", lambda e: e.tensor_tensor(out=omla[hk * 64:hk * 64 + 64, h // 2, cs], in0=PS[ob][0:64, :], in1=rcm[:], op=ALU.mult), reads=[PSb[ob], brcm], writes=[bomla])

                    nk = 4 * c + 4
                    for kt in range(nk):
                        ks = slice(kt * 128, (kt + 1) * 128)
                        smm = [(kn[:, ks], qn[:, cs], [bqk[hk]]), (krope[:, ks], qr[:, cs], [bqk[hk], bcq])]
                        if kt >= 4 * c:
                            smm.append((ident[:], negm[:, kt - 4 * c, :], [bid, cst]))
                        pairs.append(dict(smm=smm, v=(VAm[:, kt, h, :], [bVA]), O=(ob, kt == 0, kt == nk - 1), fin=fin if kt == nk - 1 else None))
                attn_stream(S, pairs, PS, PSb, PT, bPT)
            if dbg:
                S.dma(A["d_omla"], omla[:], reads=[bomla], eng="pool")
        S.barrier()
        if os.environ.get('SKIP_NSA') is None:
          stage_nsa(nc, S, A, sq, PS, PSb, PB, PBb, ident, bid, s1, hT, bhT, onsa, bonsa, cst, negc, negw, selg, cmptbl, v12, dbg, nb)
        S.barrier()
        if os.environ.get('SKIP_MERGE') is None:
          stage_merge(nc, S, A, sq, PS, PSb, PB, PBb, ident, bid, hT, bhT, onsa, bonsa, omla, bomla, x1b, nb, junk, bjunk)


def stage_nsa(nc, S, A, sq, PS, PSb, PB, PBb, ident, bid, s1, hT, bhT, onsa, bonsa, cst, negc, negw, selg, cmptbl, v12, dbg, nb):
    with ExitStack() as s2:
        sb = lambda name, shape, dt: s2.enter_context(_sbt(nc, name, shape, dt))
        QA = [sb(f"QA{g}", [128, 16 * 512], BF16) for g in range(2)]
        bQA = [[Buf() for _ in range(16)] for g in range(2)]
        BR = ("sel", "win")
        KA = {(b, g): sb(f"KA{b}{g}", [128, T], BF16) for b in BR for g in range(2)}
        bKA = {k: Buf() for k in KA}
        VA = {(b, g): sb(f"VA{b}{g}", [128, 16, 128], BF16) for b in BR for g in range(2)}
        bVA = {k: Buf() for k in VA}
        gsig = sb("gsig", [24, T], BF16)
        bgs = Buf()
        KC = [sb(f"KC{g}", [128, 128], BF16) for g in range(2)]
        VC = [sb(f"VC{g}", [128, 64], BF16) for g in range(2)]
        bKC = [Buf() for _ in range(2)]
        bVC = [Buf() for _ in range(2)]
        for g in range(2):
            S.op("pool", lambda e: e.memset(QA[g][64:96, :], 0.0), writes=bQA[g])
            S.dma(QA[g][96:100, :], A["qaug"][g, 32:36, :], writes=bQA[g], eng="pool")
            S.op("pool", lambda e: e.memset(KC[g][:], 0.0), writes=[bKC[g]])
            S.dma(KC[g][64:100, :], A["kaug_cmp"], writes=[bKC[g]], eng="pool")
            S.op("pool", lambda e: e.memset(VC[g][:], 0.0), writes=[bVC[g]])
            for b in BR:
                S.dma(KA[(b, g)][64:100, :], A["kaug_" + b], writes=[bKA[(b, g)]], eng="pool")
                S.op("pool", lambda e: e.memset(VA[(b, g)][:], 1.0), writes=[bVA[(b, g)]])
        NSTOP = int(os.environ.get('NSA_STOP', '99'))
        if NSTOP <= 1:
            S.barrier()
            return
        with ExitStack() as s3:
            sb3 = lambda name, shape, dt: s3.enter_context(_sbt(nc, name, shape, dt))
            wN = sb3("wN", [128, 8, 1304], BF16)
            bwN = Buf()
            S.dma(wN[:], A["w_in"][:, :, 0:1304], writes=[bwN], eng="pool")
            kcT = {(kd, g): sb3(f"kcT{kd}{g}", [64, T], BF16) for kd in range(2) for g in range(2)}
            bkc = {k: Buf() for k in kcT}
            for c in range(4):
                cs = slice(c * 512, (c + 1) * 512)
                for n in range(4):
                    ib = nb()
                    for kc in range(8):
                        S.op("pe", lambda e: e.matmul(PS[ib][:, :], lhsT=wN[:, kc, n * 128:(n + 1) * 128], rhs=hT[:, kc, cs], start=(kc == 0), stop=(kc == 7)), reads=[bwN, bhT[c]], writes=[PSb[ib]])
                    g = n // 2
                    rr = (2 * n) % 4
                    QAv = QA[g][:, :].rearrange("p (i r t) -> p i r t", i=16, r=4)
                    S.op("act", lambda e: e.mul(out=QAv[0:64, 4 * c:4 * c + 4, rr, :], in_=PS[ib][0:64, :].rearrange("p (i t) -> p i t", i=4), mul=SC_N), reads=[PSb[ib]], writes=bQA[g][4 * c:4 * c + 4])
                    S.op("dve", lambda e: e.tensor_scalar(out=QAv[0:64, 4 * c:4 * c + 4, rr + 1, :], in0=PS[ib][64:128, :].rearrange("p (i t) -> p i t", i=4), scalar1=SC_N, scalar2=None, op0=ALU.mult), reads=[PSb[ib]], writes=bQA[g][4 * c:4 * c + 4])
                for kind in (0, 1, 2, 4):
                    ib = nb()
                    col0 = 512 + kind * 128
                    for kc in range(8):
                        S.op("pe", lambda e: e.matmul(PS[ib][:, :], lhsT=wN[:, kc, col0:col0 + 128], rhs=hT[:, kc, cs], start=(kc == 0), stop=(kc == 7)), reads=[bwN, bhT[c]], writes=[PSb[ib]])
                    for g in range(2):
                        if kind < 2:
                            dst, bd = kcT[(kind, g)][0:64, cs], bkc[(kind, g)]
                        else:
                            key = ("sel" if kind == 2 else "win", g)
                            dst, bd = KA[key][0:64, cs], bKA[key]
                        if g == 0:
                            S.op("act", lambda e: e.copy(out=dst, in_=PS[ib][0:64, :]), reads=[PSb[ib]], writes=[bd])
                        else:
                            S.op("dve", lambda e: e.tensor_copy(out=dst, in_=PS[ib][64:128, :]), reads=[PSb[ib]], writes=[bd])
                ib = nb()
                for kc in range(8):
                    S.op("pe", lambda e: e.matmul(PS[ib][0:24, :], lhsT=wN[:, kc, 1280:1304], rhs=hT[:, kc, cs], start=(kc == 0), stop=(kc == 7)), reads=[bwN, bhT[c]], writes=[PSb[ib]])
                S.op("act", lambda e: e.activation(out=gsig[:, cs], in_=PS[ib][0:24, :], func=ACT.Sigmoid), reads=[PSb[ib]], writes=[bgs])
            for kt in range(int(os.environ.get('NKT', '16')) if NSTOP > 2 else 0):
                ib = nb()
                ts = slice(kt * 128, (kt + 1) * 128)
                for q, col0 in enumerate((896, 1152)):
                    for kc in range(8):
                        S.op("pe", lambda e: e.matmul(PS[ib][:, q * 128:(q + 1) * 128], lhsT=hT[:, kc, ts], rhs=wN[:, kc, col0:col0 + 128], start=(kc == 0), stop=(kc == 7)), reads=[bwN, bhT[kt // 4]], writes=[PSb[ib]])
                for q, b in enumerate(BR):
                    for g in range(int(os.environ.get('VCOPY', '2'))):
                        if g == 0:
                            S.op("act", lambda e: e.copy(out=VA[(b, g)][:, kt, 0:64], in_=PS[ib][:, q * 128 + g * 64: q * 128 + g * 64 + 64]), reads=[PSb[ib]], writes=[bVA[(b, g)]])
                        else:
                            S.op("dve", lambda e: e.tensor_copy(out=VA[(b, g)][:, kt, 0:64], in_=PS[ib][:, q * 128 + g * 64: q * 128 + g * 64 + 64]), reads=[PSb[ib]], writes=[bVA[(b, g)]])
            for kd, (w1n, b1n, w2n, posn) in enumerate((("ck_w1", "ck_b1", "ck_w2", "posT_k"), ("cv_w1", "cv_b1", "cv_w2", "posT_v"))[:int(os.environ.get('NCMP', '2'))]):
                W1 = sb3(f"W1{kd}", [64, 32, 64], BF16)
                posT = sb3(f"posT{kd}", [64, 32], BF16)
                b1 = sb3(f"b1{kd}", [64, 1], F32)
                W2 = sb3(f"W2{kd}", [64, 64], BF16)
                bias = sb3(f"bias{kd}", [64, 1], F32)
                bcw = Buf()
                bbias = Buf()
                S.dma(W1[:], A[w1n], writes=[bcw], eng="pool")
                S.dma(posT[:], A[posn], writes=[bcw], eng="pool")
                S.dma(W2[:], A[w2n], writes=[bcw], eng="pool")
                S.dma(b1[:], A[b1n], writes=[bcw])
                ip = nb()
                for l in range(32):
                    S.op("pe", lambda e: e.matmul(PS[ip][0:64, 0:1], lhsT=W1[:, l, :], rhs=posT[:, l:l + 1], start=(l == 0), stop=(l == 31)), reads=[bcw], writes=[PSb[ip]])
                S.op("dve", lambda e: e.tensor_tensor(out=bias[:], in0=PS[ip][0:64, 0:1], in1=b1[:], op=ALU.add), reads=[PSb[ip], bcw], writes=[bbias])
                for g in range(2):
                    G = sb3(f"G{kd}{g}", [64, 128], BF16)
                    bG = Buf()
                    S.op("pool", lambda e: e.memset(G[:], 0.0), writes=[bG])
                    ia = nb()
                    for l in range(32):
                        S.op("pe", lambda e: e.matmul(PS[ia][0:64, 0:127], lhsT=W1[:, l, :], rhs=kcT[(kd, g)][0:64, l:l + 2017:16], start=(l == 0), stop=(l == 31)), reads=[bcw, bkc[(kd, g)]], writes=[PSb[ia]])
                    S.op("act", lambda e: e.activation(out=G[:, 0:127], in_=PS[ia][0:64, 0:127], func=ACT.Gelu_apprx_tanh, bias=bias[:, 0:1]), reads=[PSb[ia], bbias], writes=[bG])
                    io = nb()
                    if kd == 0:
                        S.op("pe", lambda e: e.matmul(PS[io][0:64, 0:127], lhsT=W2[:, :], rhs=G[:, 0:127], start=True, stop=True), reads=[bcw, bG], writes=[PSb[io]])
                        S.op("act", lambda e: e.copy(out=KC[g][0:64, 0:127], in_=PS[io][0:64, 0:127]), reads=[PSb[io]], writes=[bKC[g]])
                    else:
                        S.op("pe", lambda e: e.matmul(PS[io][0:127, 0:64], lhsT=G[:, 0:127], rhs=W2[:, :], start=True, stop=True), reads=[bcw, bG], writes=[PSb[io]])
                        S.op("act", lambda e: e.copy(out=VC[g][0:127, :], in_=PS[io][0:127, 0:64]), reads=[PSb[io]], writes=[bVC[g]])
            S.barrier()
        f32t = lambda name, shape: sb(name, shape, F32)
        sc = f32t("sc", [128, 512])
        ex = f32t("ex", [128, 512])
        pp = f32t("pp", [128, 512])
        pbf = sb("pbf", [128, 512], BF16)
        PTc = sb("PTc", [128, 512], BF16)
        sm = f32t("sm", [128, 8])
        t1i = f32t("t1i", [128, 4, 32])
        imp = f32t("imp", [128, 32])
        score = f32t("score", [128, 32])
        sc2 = f32t("sc2", [128, 32])
        m8 = f32t("m8", [128, 16])
        Z = sb("Z", [128, 128], BF16)
        rcs = f32t("rcs", [64, 512])
        gc = f32t("gc", [64, 512])
        cfm = f32t("cfm", [64, 512])
        tb = f32t("tb", [64, 512])
        acc = f32t("acc", [64, 512])
        bsc, bex, bpp, bpbf, bPTc, bsm, bsel, bZ, brcs, bgc, bcfm, btb, bacc = [Buf() for _ in range(13)]
        PT = [sb(f"PTn{i}", [128, 512], BF16) for i in range(4)]
        bPT = [Buf() for _ in range(4)]
        S.op("pool", lambda e: e.memset(Z[:], 0.0), writes=[bZ])
        v3 = lambda t: t[:, :].rearrange("p (r j) -> p r j", r=4)
        for g in range(2):
            QAv = QA[g][:, :].rearrange("p (i r t) -> p i r t", i=16, r=4)
            for i in range(int(os.environ.get('NSA_NI', '16'))):
                nj = 8 * i + 8
                ts = slice(i * 128, (i + 1) * 128)
                rhsQ = QA[g][0:100, i * 512:(i + 1) * 512]
                for r in range(4):
                    S.op("pe", lambda e: e.matmul(PS[6][:, r * 128:r * 128 + nj], lhsT=QA[g][0:100, (4 * i + r) * 128:(4 * i + r + 1) * 128], rhs=KC[g][0:100, 0:nj], start=True, stop=True), reads=[bQA[g][i], bKC[g]], writes=[PSb[6]])
                S.op("dve", lambda e: e.tensor_tensor(out=v3(sc)[:, :, 0:nj], in0=v3(PS[6])[:, :, 0:nj], in1=cmptbl[:, 120 - 8 * i:120 - 8 * i + nj].unsqueeze(1).to_broadcast([128, 4, nj]), op=ALU.add), reads=[PSb[6], cst], writes=[bsc])
                S.op("act", lambda e: e.activation(out=v3(ex)[:, :, 0:nj], in_=v3(sc)[:, :, 0:nj], func=ACT.Exp), reads=[bsc], writes=[bex])
                S.op("dve", lambda e: e.tensor_reduce(out=sm[:, 0:4], in_=v3(ex)[:, :, 0:nj], axis=AX.X, op=ALU.add), reads=[bex], writes=[bsm])
                S.op("dve", lambda e: e.tensor_scalar(out=sm[:, 0:4], in0=sm[:, 0:4], scalar1=1e-30, scalar2=None, op0=ALU.max), reads=[bsm], writes=[bsm])
                S.op("dve", lambda e: e.reciprocal(out=sm[:, 4:8], in_=sm[:, 0:4]), reads=[bsm], writes=[bsm])
                S.op("dve", lambda e: e.tensor_tensor(out=v3(pp)[:, :, 0:nj], in0=v3(ex)[:, :, 0:nj], in1=sm[:, 4:8].unsqueeze(2).to_broadcast([128, 4, nj]), op=ALU.mult), reads=[bex, bsm], writes=[bpp])
                S.op("pool", lambda e: e.tensor_copy(out=v3(pbf)[:, :, 0:nj], in_=v3(pp)[:, :, 0:nj]), reads=[bpp], writes=[bpbf])
                for r in range(4):
                    S.op("pe", lambda e: e.transpose(out=PB[0:nj, r * 128:(r + 1) * 128], in_=pbf[:, r * 128:r * 128 + nj], identity=ident[:]), reads=[bpbf, bid], writes=[PBb])
                S.op("act", lambda e: e.copy(out=PTc[0:nj, :], in_=PB[0:nj, 0:512]), reads=[PBb], writes=[bPTc])
                S.op("pe", lambda e: e.matmul(PS[5][0:64, :], lhsT=VC[g][0:nj, 0:64], rhs=PTc[0:nj, :], start=True, stop=True), reads=[bVC[g], bPTc], writes=[PSb[5]])
                pairs = []
                k0 = max(0, i - 4)
                for kt in range(k0, i + 1):
                    ks = slice(kt * 128, (kt + 1) * 128)
                    smm = [(KA[("win", g)][0:100, ks], rhsQ, [bKA[("win", g)], bQA[g][i]])]
                    if kt == i:
                        smm.append((ident[:], negc[:], [bid, cst]))
                    elif kt == i - 4:
                        smm.append((ident[:], negw[:], [bid, cst]))
                    pairs.append(dict(smm=smm, v=(VA[("win", g)][:, kt, :], [bVA[("win", g)]]), O=(4, kt == k0, kt == i), fin=None))
                attn_stream(S, pairs, PS, PSb, PT, bPT)
                if i >= 8:
                    nsb = nj // 4
                    S.op("dve", lambda e: e.tensor_reduce(out=t1i[:, :, 0:nsb], in_=v3(pp)[:, :, 0:nj].rearrange("p r (s q) -> p r s q", q=4), axis=AX.X, op=ALU.add), reads=[bpp], writes=[bsel])
                    S.op("dve", lambda e: e.tensor_reduce(out=imp[:, 0:nsb], in_=t1i[:, :, 0:nsb].rearrange("p r s -> p s r"), axis=AX.X, op=ALU.add), reads=[bsel], writes=[bsel])
                    S.op("pool", lambda e: e.memset(score[:], -BIG), reads=[bsel], writes=[bsel])
                    S.op("dve", lambda e: e.tensor_copy(out=score[:, 0:2 * i], in_=imp[:, 0:2 * i]), reads=[bsel], writes=[bsel])
                    S.op("dve", lambda e: e.tensor_scalar(out=score[:, 2 * i - 1:2 * i], in0=imp[:, 2 * i - 1:2 * i], scalar1=v12[:, 1:2], scalar2=None, op0=ALU.add), reads=[bsel, cst], writes=[bsel])
                    S.op("pool", lambda e: e.memset(score[:, 0:1], BIG), reads=[bsel], writes=[bsel])
                    S.op("pool", lambda e: e.memset(score[:, 2 * i:2 * i + 1], BIG), reads=[bsel], writes=[bsel])
                    S.op("dve", lambda e: e.tensor_copy(out=score[:, 2 * i + 1:2 * i + 2], in_=v12[:, 0:1]), reads=[bsel, cst], writes=[bsel])
                    S.op("dve", lambda e: e.max(out=m8[:, 0:8], in_=score[:]), reads=[bsel], writes=[bsel])
                    S.op("dve", lambda e: e.match_replace(out=sc2[:], in_to_replace=m8[:, 0:8], in_values=score[:], imm_value=-3.0e9), reads=[bsel], writes=[bsel])
                    S.op("dve", lambda e: e.max(out=m8[:, 8:16], in_=sc2[:]), reads=[bsel], writes=[bsel])
                    S.op("dve", lambda e: e.tensor_scalar(out=Z[:, 64:96], in0=score[:], scalar1=m8[:, 15:16], scalar2=NEG, op0=ALU.is_lt, op1=ALU.mult), reads=[bsel], writes=[bZ])
                    S.op("pe", lambda e: e.transpose(out=PB[:, 512:640], in_=Z[:, :], identity=ident[:]), reads=[bZ, bid], writes=[PBb])
                    S.op("act", lambda e: e.copy(out=QAv[64:96, i, :, :], in_=PB[64:96, 512:640].unsqueeze(1).to_broadcast([32, 4, 128])), reads=[PBb], writes=[bQA[g][i]])
                pairs = []
                for kt in range(0, i + 1):
                    ks = slice(kt * 128, (kt + 1) * 128)
                    smm = [(KA[("sel", g)][0:100, ks], rhsQ, [bKA[("sel", g)], bQA[g][i]])]
                    if kt == i:
                        smm.append((ident[:], negc[:], [bid, cst]))
                    pairs.append(dict(smm=smm, v=(VA[("sel", g)][:, kt, :], [bVA[("sel", g)]]), O=(3, kt == 0, kt == i), fin=None))
                attn_stream(S, pairs, PS, PSb, PT, bPT)
                for b in range(3):
                    for r in range(4):
                        col = (b * 8 + 4 * g + r) * 64
                        S.op("pe", lambda e: e.matmul(PS[6][0:64, r * 128:(r + 1) * 128], lhsT=selg[0:24, col:col + 64], rhs=gsig[0:24, ts], start=True, stop=True), reads=[cst, bgs], writes=[PSb[6]])
                    if b == 0:
                        S.op("act", lambda e: e.copy(out=gc[:], in_=PS[6][0:64, :]), reads=[PSb[6]], writes=[bgc])
                        S.op("dve", lambda e: e.tensor_tensor(out=acc[:], in0=PS[5][0:64, :], in1=gc[:], op=ALU.mult), reads=[PSb[5], bgc], writes=[bacc])
                    else:
                        ob = 3 if b == 1 else 4
                        S.op("dve", lambda e: e.reciprocal(out=rcs[:], in_=PS[ob][64:128, :]), reads=[PSb[ob]], writes=[brcs])
                        S.op("dve", lambda e: e.tensor_tensor(out=cfm[:], in0=PS[6][0:64, :], in1=rcs[:], op=ALU.mult), reads=[PSb[6], brcs], writes=[bcfm])
                        S.op("dve", lambda e: e.tensor_tensor(out=tb[:], in0=PS[ob][0:64, :], in1=cfm[:], op=ALU.mult), reads=[PSb[ob], bcfm], writes=[btb])
                        S.op("pool", lambda e: e.tensor_tensor(out=acc[:], in0=acc[:], in1=tb[:], op=ALU.add), reads=[btb], writes=[bacc])
                a3 = acc[:, :].rearrange("p (r t) -> p r t", r=4)
                S.op("pool", lambda e: e.tensor_copy(out=onsa[0:64, 2 * g:2 * g + 2, ts], in_=a3[:, 0::2, :]), reads=[bacc], writes=[bonsa])
                S.op("act", lambda e: e.copy(out=onsa[64:128, 2 * g:2 * g + 2, ts], in_=a3[:, 1::2, :]), reads=[bacc], writes=[bonsa])
        if dbg:
            S.dma(A["d_onsa"], onsa[:], reads=[bonsa], eng="pool")
        S.barrier()


def stage_merge(nc, S, A, sq, PS, PSb, PB, PBb, ident, bid, hT, bhT, onsa, bonsa, omla, bomla, x1b, nb, junk, bjunk):
    r0 = sq * T
    with ExitStack() as s2:
        sb = lambda name, shape, dt: s2.enter_context(_sbt(nc, name, shape, dt))
        wbn = sb("wbn", [128, 4, D], BF16)
        wbm = sb("wbm", [128, 4, D], BF16)
        wo = sb("wo", [128, 8, D], BF16)
        wmg = sb("wmg", [128, 8, 2048], BF16)
        gpo = sb("gpo", [128, D], F32)
        bw = Buf()
        S.dma(wmg[:], A["w_in"][:, :, 1720:3768], writes=[bw], eng="pool")
        S.dma(wbn[:], A["w_br_nsa"], writes=[bw], eng="pool")
        S.dma(wbm[:], A["w_br_mla"], writes=[bw], eng="pool")
        S.dma(wo[:], A["w_o"], writes=[bw], eng="pool")
        S.dma(gpo[:], A["g_post_mix"][0:1, :].to_broadcast([128, D]), writes=[bw])
        R = 4
        xt = [sb(f"xtm{i}", [128, D], F32) for i in range(R)]
        s0 = [sb(f"s0{i}", [128, 512], F32) for i in range(2)]
        m0 = [sb(f"m0{i}", [128, 512], F32) for i in range(2)]
        mgb = [sb(f"mgb{i}", [128, D], BF16) for i in range(R)]
        mgT = [sb(f"mgT{i}", [128, 8, 128], BF16) for i in range(R)]
        tmp = [sb(f"tmpm{i}", [128, D], F32) for i in range(R)]
        ssm = sb("ssm", [128, 16], F32)
        bxt, bmgb, bmgT, btmp, bssm = [[Buf() for _ in range(R)] for _ in range(5)]
        bs0 = [Buf() for _ in range(2)]
        bm0 = [Buf() for _ in range(2)]
        nb5c = [0]

        def nb5():
            nb5c[0] = (nb5c[0] + 1) % 5
            return nb5c[0]

        def partA(i):
            k = i % R
            ts = slice(i * 128, (i + 1) * 128)
            S.dma(xt[k][:], A["x"][r0 + 128 * i: r0 + 128 * (i + 1), :], writes=[bxt[k]])
            for hf in range(2):
                cs = slice(hf * 512, (hf + 1) * 512)
                for br, (wb_, osrc, bo) in enumerate(((wbn, onsa, bonsa), (wbm, omla, bomla))):
                    ig = nb5()
                    for kc in range(8):
                        S.op("pe", lambda e: e.matmul(PS[ig][:, :], lhsT=hT[:, kc, ts], rhs=wmg[:, kc, br * 1024 + hf * 512: br * 1024 + (hf + 1) * 512], start=(kc == 0), stop=(kc == 7)), reads=[bhT[i // 4], bw], writes=[PSb[ig]])
                    ibr = nb5()
                    for c in range(4):
                        S.op("pe", lambda e: e.matmul(PS[ibr][:, :], lhsT=osrc[:, c, ts], rhs=wb_[:, c, cs], start=(c == 0), stop=(c == 3)), reads=[bo, bw], writes=[PSb[ibr]])
                    S.op("act", lambda e: e.activation(out=s0[br][:], in_=PS[ig][:, :], func=ACT.Sigmoid), reads=[PSb[ig]], writes=[bs0[br]])
                    S.op("dve", lambda e: e.tensor_tensor(out=m0[br][:], in0=PS[ibr][:, :], in1=s0[br][:], op=ALU.mult), reads=[PSb[ibr], bs0[br]], writes=[bm0[br]])
                S.op("pool", lambda e: e.tensor_tensor(out=mgb[k][:, cs], in0=m0[0][:], in1=m0[1][:], op=ALU.add), reads=[bm0[0], bm0[1]], writes=[bmgb[k]])

        iy = [5, 6]

        def partB1(i):
            k = i % R
            for kc in range(8):
                S.op("pe", lambda e: e.transpose(out=PB[:, kc * 128:(kc + 1) * 128], in_=mgb[k][:, kc * 128:(kc + 1) * 128], identity=ident[:]), reads=[bmgb[k], bid], writes=[PBb])
            S.op("act", lambda e: e.copy(out=mgT[k][:], in_=PB[:, :].rearrange("p (k t) -> p k t", k=8)), reads=[PBb], writes=[bmgT[k]])
            for hf in range(2):
                for kc in range(8):
                    S.op("pe", lambda e: e.matmul(PS[iy[hf]][:, :], lhsT=mgT[k][:, kc, :], rhs=wo[:, kc, hf * 512:(hf + 1) * 512], start=(kc == 0), stop=(kc == 7)), reads=[bmgT[k], bw], writes=[PSb[iy[hf]]])
                S.op("act", lambda e: e.activation(out=junk[:, 0:512], in_=PS[iy[hf]][:, :], func=ACT.Square, accum_out=ssm[:, 4 * k + hf:4 * k + hf + 1]), reads=[PSb[iy[hf]]], writes=[bjunk, bssm[k]])

        def partB2(i):
            k = i % R
            S.op("dve", lambda e: e.tensor_tensor(out=ssm[:, 4 * k + 2:4 * k + 3], in0=ssm[:, 4 * k:4 * k + 1], in1=ssm[:, 4 * k + 1:4 * k + 2], op=ALU.add), reads=[bssm[k]], writes=[bssm[k]])
            rstd_from_ss(S, ssm[:, 4 * k + 2:4 * k + 3], ssm[:, 4 * k + 3:4 * k + 4], D, [bssm[k]], [bssm[k]])
            for hf in range(2):
                S.op("dve", lambda e: e.scalar_tensor_tensor(out=tmp[k][:, hf * 512:(hf + 1) * 512], in0=PS[iy[hf]][:, :], scalar=ssm[:, 4 * k + 3:4 * k + 4], in1=gpo[:, hf * 512:(hf + 1) * 512], op0=ALU.mult, op1=ALU.mult), reads=[PSb[iy[hf]], bssm[k], bw], writes=[btmp[k]])
            S.op("pool", lambda e: e.tensor_tensor(out=tmp[k][:], in0=tmp[k][:], in1=xt[k][:], op=ALU.add), reads=[bxt[k]], writes=[btmp[k]])
            S.dma(A["x1s"][r0 + 128 * i: r0 + 128 * (i + 1), :], tmp[k][:], reads=[btmp[k]], writes=[x1b[16 * sq + i]], eng="pool")

        for t in range(16 + 2):
            if 0 <= t - 2 < 16:
                partB2(t - 2)
            if 0 <= t - 1 < 16:
                partB1(t - 1)
            if t < 16:
                partA(t)


def kernel(**inp):
    inp = {k: np.asarray(v) for k, v in inp.items()}
    nc = build()
    consts = make_consts()
    w = host_weights(inp)
    in_maps = []
    for c in range(NCORES):
        m = {"x": np.ascontiguousarray(inp["x"][2 * c:2 * c + 2].reshape(2 * T, D)),
             "p": np.ascontiguousarray(inp["p"][0, 2 * c:2 * c + 2].reshape(2 * T, 256)),
             "pos": np.ascontiguousarray(inp["positions"][2 * c:2 * c + 2].astype(np.int32))}
        m.update(w)
        for k, v in consts.items():
            m["c_" + k] = v
        in_maps.append(m)
    res = run_bass_kernel_spmd(nc, in_maps, core_ids=list(range(NCORES)))
    out = np.stack([r["out"].reshape(2, T, D) for r in res.results], 0).reshape(16, T, D)
    return out.astype(np.float32)
```

```python
import os
import numpy as np
from contextlib import ExitStack
import concourse.bass as bass
import concourse.mybir as mybir
from concourse.bass_utils import run_bass_kernel_spmd

ACT = mybir.ActivationFunctionType
ALU = mybir.AluOpType
AX = mybir.AxisListType
F32 = mybir.dt.float32
BF16 = mybir.dt.bfloat16
I32 = mybir.dt.int32

NEG = -30000.0
BIG = 1.0e9
EPS = 1e-6
T = 2048
D = 1024
NCORES = 8
DFF = 2816
SC_N = 0.125
SC_M = 96.0 ** -0.5
TWO_PI = 6.283185307179586
PI = 3.141592653589793


_UNIQ = [0]


def _sbt(nc, name, shape, dt):
    _UNIQ[0] += 1
    return nc.sbuf_tensor(f"{name}_u{_UNIQ[0]}", shape, dt)


class Buf:
    __slots__ = ("w", "r", "x")

    def __init__(self, x=False):
        self.w = None
        self.r = {}
        self.x = x


class Sched:
    NDS = 24

    def __init__(self, nc, stack):
        self.nc = nc
        self.engs = {"pe": nc.tensor, "act": nc.scalar, "dve": nc.vector, "pool": nc.gpsimd, "sp": nc.sync}
        self.sem = {k: stack.enter_context(nc.semaphore(k + "_s")) for k in self.engs}
        self.cnt = {k: 0 for k in self.engs}
        self.seen = {k: {} for k in self.engs}
        self.dsem = [stack.enter_context(nc.semaphore(f"dq{i}")) for i in range(self.NDS)]
        self.dcnt = [0] * self.NDS
        self.dnext = 0

    def _semof(self, k):
        return self.dsem[k] if isinstance(k, int) else self.sem[k]

    def _waits(self, eng, reads, writes):
        deps = {}
        for b in reads:
            if b.w is not None and deps.get(b.w[0], 0) < b.w[1]:
                deps[b.w[0]] = b.w[1]
            if b.x:
                for k, v in b.r.items():
                    if k != eng and deps.get(k, 0) < v:
                        deps[k] = v
        for b in writes:
            if b.w is not None and deps.get(b.w[0], 0) < b.w[1]:
                deps[b.w[0]] = b.w[1]
            for k, v in b.r.items():
                if deps.get(k, 0) < v:
                    deps[k] = v
        e = self.engs[eng]
        for k, v in deps.items():
            if k == eng and eng in ("pe", "sp"):
                continue
            if self.seen[eng].get(k, 0) >= v:
                continue
            e.wait_ge(self._semof(k), v)
            self.seen[eng][k] = v

    def _mark(self, key, val, reads, writes):
        for b in writes:
            b.w = (key, val)
            b.r = {}
        for b in reads:
            if b not in writes:
                b.r[key] = val

    def op(self, eng, fn, reads=(), writes=()):
        self._waits(eng, reads, writes)
        ins = fn(self.engs[eng])
        self.cnt[eng] += 1
        ins.then_inc(self.sem[eng], 1)
        self._mark(eng, self.cnt[eng], reads, writes)

    def dma(self, out, in_, reads=(), writes=(), eng="sp", **kw):
        i = self.dnext
        self.dnext = (self.dnext + 1) % self.NDS
        e = self.engs[eng]
        if self.dcnt[i] > 0 and self.seen[eng].get(i, 0) < self.dcnt[i]:
            e.wait_ge(self.dsem[i], self.dcnt[i])
            self.seen[eng][i] = self.dcnt[i]
        self._waits(eng, reads, writes)
        ins = e.dma_start(out=out, in_=in_, **kw)
        self.dcnt[i] += 16
        ins.then_inc(self.dsem[i], 16)
        self._mark(i, self.dcnt[i], reads, writes)

    def barrier(self):
        for en, e in self.engs.items():
            for k in self.engs:
                if k != en and self.cnt[k] > self.seen[en].get(k, 0):
                    e.wait_ge(self.sem[k], self.cnt[k])
                    self.seen[en][k] = self.cnt[k]
            for i in range(self.NDS):
                if self.dcnt[i] > self.seen[en].get(i, 0):
                    e.wait_ge(self.dsem[i], self.dcnt[i])
                    self.seen[en][i] = self.dcnt[i]


def make_consts():
    c = {}
    c["ident"] = np.eye(128, dtype=np.float32)
    ds = np.arange(128)[:, None]
    dt = np.arange(128)[None, :]
    negc = np.where(ds <= dt, 0.0, NEG).astype(np.float32)
    negw = np.where(ds > dt, 0.0, NEG).astype(np.float32)
    c["negc"] = np.tile(negc, (1, 4))
    c["negw"] = np.tile(negw, (1, 4))
    dt5 = np.arange(512)[None, :]
    c["negm"] = np.stack([np.where(ds + 128 * m <= dt5, 0.0, NEG) for m in range(4)], 0).astype(np.float32)
    s = np.arange(T)
    E = (s[None, :] // 64 == np.arange(32)[:, None]).astype(np.float32)
    kal = np.stack([np.ones(T), np.ones(T), s // 128, s % 128], 0).astype(np.float32)
    c["kaug_sel"] = np.concatenate([E, kal], 0)
    c["kaug_win"] = np.concatenate([np.zeros_like(E), kal], 0)
    pj = 16 * np.arange(128) + 31
    kc = np.stack([np.ones(128), np.ones(128), pj // 128, pj % 128], 0).astype(np.float32)
    kc[:, 127] = 0.0
    c["kaug_cmp"] = np.concatenate([np.zeros((32, 128), np.float32), kc], 0)
    slopes = np.array([2.0 ** (-(h + 1)) for h in range(8)], dtype=np.float64)
    qa = np.zeros((2, 36, 16, 4, 128), np.float32)
    for g in range(2):
        for r in range(4):
            sl = slopes[4 * g + r]
            for i in range(16):
                qa[g, 32, i, r, :] = -sl * 128.0 * i
                qa[g, 33, i, r, :] = -sl * np.arange(128)
                qa[g, 34, i, r, :] = sl * 128.0
                qa[g, 35, i, r, :] = sl
    c["qaug"] = qa.reshape(2, 36, 16 * 4 * 128)
    selg = np.zeros((24, 24, 64), np.float32)
    for r in range(24):
        selg[r, r, :] = 1.0
    c["selg"] = selg.reshape(24, 24 * 64)
    cc = np.arange(128)[None, :] - 120
    dtt = np.arange(128)[:, None]
    c["cmptbl"] = np.where(16 * cc + 31 <= dtt, 0.0, NEG).astype(np.float32)
    v12 = np.zeros((128, 2), np.float32)
    v12[:, 0] = np.where(np.arange(128) < 64, -BIG, BIG)
    v12[:, 1] = np.where(np.arange(128) < 64, BIG, 0.0)
    c["v12"] = v12
    invf = (np.float32(10000.0) ** (-np.arange(0, 32, 2, dtype=np.float32) / np.float32(32))).astype(np.float32)
    c["invf"] = np.concatenate([invf, invf])[:, None].astype(np.float32)
    return c


WSHAPES = {
    "g_pre_mix": [1, D], "g_post_mix": [1, D], "g_pre_ffn": [1, D], "g_post_ffn": [1, D], "g_ple": [1, D],
    "w_in": [128, 8, 3768], "posT_k": [64, 32], "posT_v": [64, 32],
    "ck_w1": [64, 32, 64], "cv_w1": [64, 32, 64], "ck_b1": [64, 1], "cv_b1": [64, 1],
    "ck_w2": [64, 64], "cv_w2": [64, 64],
    "g_q": [128, 2], "w_uq": [128, 2, 768], "g_kv": [128, 1], "w_ukv": [128, 1024],
    "w_br_nsa": [128, 4, D], "w_br_mla": [128, 4, D], "w_o": [128, 8, D],
    "w_up": [128, 8, 2 * DFF], "w_conv": [128, 3, 44], "b_conv": [128, 44], "w_down": [128, 22, D],
    "w_ple": [128, 2, D], "w_ple_gate": [128, 8, D],
}


def host_weights(inp):
    w = {}
    f = lambda a: np.ascontiguousarray(a, dtype=np.float32)
    for k in ("g_pre_mix", "g_post_mix", "g_pre_ffn", "g_post_ffn", "g_ple"):
        w[k] = f(inp[k][0][None, :])
    kcp = lambda a, p=128: f(a.reshape(a.shape[0] // p, p, a.shape[1]).transpose(1, 0, 2))
    w["w_in"] = kcp(inp["w_in"][0])
    w["posT_k"] = f(inp["nsa_pos_k"][0].T)
    w["posT_v"] = f(inp["nsa_pos_v"][0].T)
    w["ck_w1"] = f(inp["nsa_ck_w1"][0].reshape(32, 64, 64).transpose(1, 0, 2))
    w["cv_w1"] = f(inp["nsa_cv_w1"][0].reshape(32, 64, 64).transpose(1, 0, 2))
    w["ck_b1"] = f(inp["nsa_ck_b1"][0][:, None])
    w["cv_b1"] = f(inp["nsa_cv_b1"][0][:, None])
    w["ck_w2"] = f(inp["nsa_ck_w2"][0])
    w["cv_w2"] = f(inp["nsa_cv_w2"][0])
    w["g_q"] = f(inp["mla_g_q"][0].reshape(2, 128).T)
    w["w_uq"] = kcp(inp["mla_w_uq"][0])
    w["g_kv"] = f(inp["mla_g_kv"][0][:, None])
    w["w_ukv"] = f(inp["mla_w_ukv"][0])
    w["w_br_nsa"] = kcp(inp["w_br_nsa"][0])
    w["w_br_mla"] = kcp(inp["w_br_mla"][0])
    w["w_o"] = kcp(inp["w_o"][0])
    w["w_up"] = kcp(inp["w_up"][0])
    w["w_conv"] = f(inp["w_conv"][0].reshape(3, 44, 128).transpose(2, 0, 1))
    w["b_conv"] = f(inp["b_conv"][0].reshape(44, 128).T)
    w["w_down"] = kcp(inp["w_down"][0])
    w["w_ple"] = kcp(inp["w_ple"][0])
    w["w_ple_gate"] = kcp(inp["w_ple_gate"][0])
    return w


def build(stages=("mix", "ffn", "ple"), dbg=False, nseq=2):
    nc = bass.Bass("TRN2", target_bir_lowering=False)
    consts = make_consts()
    A = {}
    A["x"] = nc.dram_tensor("x", [2 * T, D], F32, kind="ExternalInput").ap()
    A["p"] = nc.dram_tensor("p", [2 * T, 256], F32, kind="ExternalInput").ap()
    A["pos"] = nc.dram_tensor("pos", [2, T], I32, kind="ExternalInput").ap()
    for k, shp in WSHAPES.items():
        A[k] = nc.dram_tensor(k, shp, F32, kind="ExternalInput").ap()
    for k, v in consts.items():
        A[k] = nc.dram_tensor("c_" + k, list(v.shape), F32, kind="ExternalInput").ap()
    A["out"] = nc.dram_tensor("out", [2 * T, D], F32, kind="ExternalOutput").ap()
    if "mix" in stages:
        A["x1s"] = nc.dram_tensor("x1s", [2 * T, D], F32, kind="ExternalOutput" if dbg else "Internal").ap()
    else:
        A["x1s"] = nc.dram_tensor("x1s", [2 * T, D], F32, kind="ExternalInput").ap()
    if dbg:
        A["d_onsa"] = nc.dram_tensor("d_onsa", [128, 4, T], F32, kind="ExternalOutput").ap()
        A["d_omla"] = nc.dram_tensor("d_omla", [128, 4, T], F32, kind="ExternalOutput").ap()
    x1b = [Buf() for _ in range(32)]

    with ExitStack() as st:
        S = Sched(nc, st)
        sbg = lambda name, shape, dt: st.enter_context(_sbt(nc, name, shape, dt))
        PS = [st.enter_context(nc.psum_tensor(f"ps{i}", [128, 512], F32)) for i in range(7)]
        PSb = [Buf(True) for _ in range(7)]
        PB = st.enter_context(nc.psum_tensor("pb", [128, 1024], BF16))
        PBb = Buf(True)
        ident = sbg("ident", [128, 128], BF16)
        bid = Buf()
        S.dma(ident[:], A["ident"], writes=[bid], eng="pool")

        if "mix" in stages:
            for sq in range(nseq):
                stage_mix(nc, S, A, sq, PS, PSb, PB, PBb, ident, bid, x1b, dbg and sq == 0)
                S.barrier()
        if "ffn" in stages:
            stage_ffn(nc, S, A, PS, PSb, PB, PBb, ident, bid, x1b, nseq)
            S.barrier()
        if "ple" in stages:
            stage_ple(nc, S, A, PS, PSb, PB, PBb, ident, bid, x1b, nseq)
        S.barrier()
        print('COUNTS', S.cnt, max(S.dcnt))
    return nc


def rstd_from_ss(S, ss_ap, rstd_ap, n, rb, wb):
    S.op("act", lambda e: e.activation(out=rstd_ap, in_=ss_ap, func=ACT.Sqrt, scale=1.0 / n, bias=EPS), reads=rb, writes=wb)
    S.op("dve", lambda e: e.reciprocal(out=rstd_ap, in_=rstd_ap), reads=wb, writes=wb)


def stage_ffn(nc, S, A, PS, PSb, PB, PBb, ident, bid, x1b, nseq):
    with ExitStack() as s2:
        sb = lambda name, shape, dt: s2.enter_context(_sbt(nc, name, shape, dt))
        wup = sb("wup", [128, 8, 2 * DFF], BF16)
        bwup = [Buf() for _ in range(11)]
        wdn = sb("wdn", [128, 22, D], BF16)
        bwdn = [Buf() for _ in range(2)]
        wc = sb("wc", [128, 3, 44], F32)
        bc = sb("bc", [128, 44], F32)
        bwc = Buf()
        gpre = sb("gpre", [128, D], F32)
        gpost = sb("gpost", [128, D], F32)
        bg = Buf()
        S.dma(wc[:], A["w_conv"], writes=[bwc])
        S.dma(bc[:], A["b_conv"], writes=[bwc])
        S.dma(gpre[:], A["g_pre_ffn"][0:1, :].to_broadcast([128, D]), writes=[bg])
        S.dma(gpost[:], A["g_post_ffn"][0:1, :].to_broadcast([128, D]), writes=[bg])
        order = []
        for q in range(6):
            order.append(q)
            if q + 6 < 11:
                pass
        for q in range(11):
            S.dma(wup[:, :, q * 512:(q + 1) * 512], A["w_up"][:, :, q * 512:(q + 1) * 512], writes=[bwup[q]], eng="pool")
        for q in range(2):
            S.dma(wdn[:, q * 11:(q + 1) * 11, :], A["w_down"][:, q * 11:(q + 1) * 11, :], writes=[bwdn[q]], eng="pool")
        x1t = [sb(f"x1t{i}", [128, D], F32) for i in range(4)]
        bx1t = [Buf() for _ in range(4)]
        hn = [sb(f"hn{i}", [128, D], BF16) for i in range(2)]
        bhn = [Buf() for _ in range(2)]
        h2T = [sb(f"h2T{i}", [128, 8, 258], BF16) for i in range(2)]
        bh2T = [Buf() for _ in range(2)]
        junk = sb("junk", [128, D], BF16)
        bjunk = Buf()
        ssq = sb("ssq", [128, 8], F32)
        bss = [Buf() for _ in range(4)]
        cA = [sb(f"cA{i}", [128, 256], F32) for i in range(2)]
        cV = [sb(f"cV{i}", [128, 256], F32) for i in range(2)]
        gA = [sb(f"gA{i}", [128, 256], F32) for i in range(2)]
        bcA = [Buf() for _ in range(2)]
        bcV = [Buf() for _ in range(2)]
        bgA = [Buf() for _ in range(2)]
        actT = [sb(f"actT{i}", [128, 22, 256], BF16) for i in range(2)]
        bact = [[Buf() for _ in range(22)] for _ in range(2)]
        tmp = [sb(f"tmp{i}", [128, D], F32) for i in range(2)]
        btmp = [Buf() for _ in range(2)]
        ss2 = sb("ss2", [128, 8], F32)
        bss2 = [Buf() for _ in range(2)]
        uprot = [4, 5, 6]
        upi = 0
        nblk = 8 * nseq
        def front(b):
            r0 = 256 * b
            hT = h2T[b % 2]
            bh = bh2T[b % 2]
            for tt in range(2):
                xi = (2 * b + tt) % 4
                gi = 2 * b + tt
                S.dma(x1t[xi][:], A["x1s"][r0 + 128 * tt: r0 + 128 * (tt + 1), :], reads=[x1b[gi]], writes=[bx1t[xi]])
                S.op("act", lambda e: e.activation(out=junk[:], in_=x1t[xi][:], func=ACT.Square, accum_out=ssq[:, xi:xi + 1]), reads=[bx1t[xi]], writes=[bjunk, bss[xi]])
                rstd_from_ss(S, ssq[:, xi:xi + 1], ssq[:, 4 + xi:5 + xi], D, [bss[xi]], [bss[xi]])
                S.op("dve", lambda e: e.scalar_tensor_tensor(out=hn[tt][:], in0=x1t[xi][:], scalar=ssq[:, 4 + xi:5 + xi], in1=gpre[:], op0=ALU.mult, op1=ALU.mult), reads=[bx1t[xi], bss[xi], bg], writes=[bhn[tt]])
                for kc in range(8):
                    S.op("pe", lambda e: e.transpose(out=PB[:, kc * 128:(kc + 1) * 128], in_=hn[tt][:, kc * 128:(kc + 1) * 128], identity=ident[:]), reads=[bhn[tt], bid], writes=[PBb])
                S.op("act", lambda e: e.copy(out=hT[:, :, 2 + 128 * tt: 2 + 128 * (tt + 1)], in_=PB[:, :].rearrange("p (k t) -> p k t", k=8)), reads=[PBb], writes=[bh])
            if b % 8 == 0:
                S.op("pool", lambda e: e.memset(hT[:, :, 0:2], 0.0), writes=[bh])
            else:
                S.op("pool", lambda e: e.tensor_copy(out=hT[:, :, 0:2], in_=h2T[(b - 1) % 2][:, :, 256:258]), reads=[bh2T[(b - 1) % 2]], writes=[bh])

        def up(b, j):
            nonlocal upi
            hT = h2T[b % 2]
            bh = bh2T[b % 2]
            ia = uprot[upi % 3]
            upi += 1
            iv = uprot[upi % 3]
            upi += 1
            ca, cv, ga = cA[j % 2], cV[j % 2], gA[j % 2]
            bca, bcv, bga = bcA[j % 2], bcV[j % 2], bgA[j % 2]
            for (ib, col0) in ((ia, j * 128), (iv, DFF + j * 128)):
                for kc in range(8):
                    S.op("pe", lambda e: e.matmul(PS[ib][:, 0:258], lhsT=wup[:, kc, col0:col0 + 128], rhs=hT[:, kc, :], start=(kc == 0), stop=(kc == 7)),
                         reads=[bwup[col0 // 512], bh], writes=[PSb[ib]])
            for (ib, cc, bcc, jj) in ((ia, ca, bca, j), (iv, cv, bcv, 22 + j)):
                S.op("act", lambda e: e.activation(out=cc[:], in_=PS[ib][:, 2:258], func=ACT.Identity, scale=wc[:, 2, jj:jj + 1], bias=bc[:, jj:jj + 1]), reads=[PSb[ib], bwc], writes=[bcc])
                S.op("dve", lambda e: e.scalar_tensor_tensor(out=cc[:], in0=PS[ib][:, 1:257], scalar=wc[:, 1, jj:jj + 1], in1=cc[:], op0=ALU.mult, op1=ALU.add), reads=[PSb[ib], bwc], writes=[bcc])
                S.op("dve", lambda e: e.scalar_tensor_tensor(out=cc[:], in0=PS[ib][:, 0:256], scalar=wc[:, 0, jj:jj + 1], in1=cc[:], op0=ALU.mult, op1=ALU.add), reads=[PSb[ib], bwc], writes=[bcc])
            S.op("act", lambda e: e.activation(out=ga[:], in_=ca[:], func=ACT.Gelu_apprx_tanh), reads=[bca], writes=[bga])
            S.op("pool", lambda e: e.tensor_tensor(out=actT[b % 2][:, j, :], in0=ga[:], in1=cv[:], op=ALU.mult), reads=[bga, bcv], writes=[bact[b % 2][j]])

        def down(b):
            r0 = 256 * b
            for tt in range(2):
                xi = (2 * b + tt) % 4
                gi = 2 * b + tt
                for j in range(22):
                    for hf in range(2):
                        S.op("pe", lambda e: e.matmul(PS[2 * tt + hf][:, :], lhsT=actT[b % 2][:, j, 128 * tt:128 * (tt + 1)], rhs=wdn[:, j, hf * 512:(hf + 1) * 512], start=(j == 0), stop=(j == 21)),
                             reads=[bact[b % 2][j], bwdn[j // 11]], writes=[PSb[2 * tt + hf]])
                for hf in range(2):
                    S.op("act", lambda e: e.activation(out=junk[:, 0:512], in_=PS[2 * tt + hf][:, :], func=ACT.Square, accum_out=ss2[:, 4 * tt + hf:4 * tt + hf + 1]), reads=[PSb[2 * tt + hf]], writes=[bjunk, bss2[tt]])
                S.op("dve", lambda e: e.tensor_tensor(out=ss2[:, 4 * tt + 2:4 * tt + 3], in0=ss2[:, 4 * tt:4 * tt + 1], in1=ss2[:, 4 * tt + 1:4 * tt + 2], op=ALU.add), reads=[bss2[tt]], writes=[bss2[tt]])
                rstd_from_ss(S, ss2[:, 4 * tt + 2:4 * tt + 3], ss2[:, 4 * tt + 3:4 * tt + 4], D, [bss2[tt]], [bss2[tt]])
                for hf in range(2):
                    S.op("dve", lambda e: e.scalar_tensor_tensor(out=tmp[tt][:, hf * 512:(hf + 1) * 512], in0=PS[2 * tt + hf][:, :], scalar=ss2[:, 4 * tt + 3:4 * tt + 4], in1=gpost[:, hf * 512:(hf + 1) * 512], op0=ALU.mult, op1=ALU.mult),
                         reads=[PSb[2 * tt + hf], bss2[tt], bg], writes=[btmp[tt]])
                S.op("pool", lambda e: e.tensor_tensor(out=tmp[tt][:], in0=tmp[tt][:], in1=x1t[xi][:], op=ALU.add), reads=[bx1t[xi]], writes=[btmp[tt]])
                S.dma(A["x1s"][r0 + 128 * tt: r0 + 128 * (tt + 1), :], tmp[tt][:], reads=[btmp[tt]], writes=[x1b[gi]], eng="pool")

        front(0)
        for b in range(nblk):
            for j in range(22):
                up(b, j)
                if j == 12 and b + 1 < nblk:
                    front(b + 1)
            down(b)


def stage_ple(nc, S, A, PS, PSb, PB, PBb, ident, bid, x1b, nseq):
    with ExitStack() as s2:
        sb = lambda name, shape, dt: s2.enter_context(_sbt(nc, name, shape, dt))
        wpg = sb("wpg", [128, 8, D], BF16)
        wpl = sb("wpl", [128, 2, D], BF16)
        bw = Buf()
        gple = sb("gple", [128, D], F32)
        S.dma(wpg[:], A["w_ple_gate"], writes=[bw], eng="pool")
        S.dma(wpl[:], A["w_ple"], writes=[bw], eng="pool")
        S.dma(gple[:], A["g_ple"][0:1, :].to_broadcast([128, D]), writes=[bw])
        R = 6
        x2t = [sb(f"x2t{i}", [128, D], F32) for i in range(R)]
        pt = [sb(f"pt{i}", [128, 256], F32) for i in range(R)]
        x2b = [sb(f"x2b{i}", [128, D], BF16) for i in range(R)]
        pbf = [sb(f"pbf{i}", [128, 256], BF16) for i in range(R)]
        x2T = [sb(f"x2T{i}", [128, 8, 128], BF16) for i in range(R)]
        pT = [sb(f"pT{i}", [128, 2, 128], BF16) for i in range(R)]
        sg = [sb(f"sg{i}", [128, D], F32) for i in range(R)]
        eg = [sb(f"eg{i}", [128, D], F32) for i in range(R)]
        junk = sb("junkp", [128, D], BF16)
        ssp = sb("ssp", [128, 2 * R], F32)
        bx2t, bpt, bx2b, bpbf, bx2T, bpT, bsg, beg, bssp = [[Buf() for _ in range(R)] for _ in range(9)]
        bjunk = Buf()
        ntile = 16 * nseq

        def load(i):
            k = i % R
            rows = slice(128 * i, 128 * (i + 1))
            S.dma(x2t[k][:], A["x1s"][rows, :], reads=[x1b[i]], writes=[bx2t[k]])
            S.dma(pt[k][:], A["p"][rows, :], writes=[bpt[k]])

        def front(i):
            k = i % R
            S.op("dve", lambda e: e.tensor_copy(out=x2b[k][:], in_=x2t[k][:]), reads=[bx2t[k]], writes=[bx2b[k]])
            S.op("pool", lambda e: e.tensor_copy(out=pbf[k][:], in_=pt[k][:]), reads=[bpt[k]], writes=[bpbf[k]])
            for kc in range(8):
                S.op("pe", lambda e: e.transpose(out=PB[:, kc * 128:(kc + 1) * 128], in_=x2b[k][:, kc * 128:(kc + 1) * 128], identity=ident[:]), reads=[bx2b[k], bid], writes=[PBb])
            S.op("act", lambda e: e.copy(out=x2T[k][:], in_=PB[:, :].rearrange("p (k t) -> p k t", k=8)), reads=[PBb], writes=[bx2T[k]])
            for kc in range(2):
                S.op("pe", lambda e: e.transpose(out=PB[:, kc * 128:(kc + 1) * 128], in_=pbf[k][:, kc * 128:(kc + 1) * 128], identity=ident[:]), reads=[bpbf[k], bid], writes=[PBb])
            S.op("act", lambda e: e.copy(out=pT[k][:], in_=PB[:, 0:256].rearrange("p (k t) -> p k t", k=2)), reads=[PBb], writes=[bpT[k]])

        def back(i):
            k = i % R
            rows = slice(128 * i, 128 * (i + 1))
            for hf in range(2):
                st_ = (2 * i + hf) % 3
                ig, ie = 2 * st_, 2 * st_ + 1
                cs = slice(hf * 512, (hf + 1) * 512)
                for kc in range(8):
                    S.op("pe", lambda e: e.matmul(PS[ig][:, :], lhsT=x2T[k][:, kc, :], rhs=wpg[:, kc, cs], start=(kc == 0), stop=(kc == 7)), reads=[bx2T[k], bw], writes=[PSb[ig]])
                for kc in range(2):
                    S.op("pe", lambda e: e.matmul(PS[ie][:, :], lhsT=pT[k][:, kc, :], rhs=wpl[:, kc, cs], start=(kc == 0), stop=(kc == 1)), reads=[bpT[k], bw], writes=[PSb[ie]])
                S.op("act", lambda e: e.activation(out=sg[k][:, cs], in_=PS[ig][:, :], func=ACT.Sigmoid), reads=[PSb[ig]], writes=[bsg[k]])
                S.op("dve", lambda e: e.tensor_tensor(out=eg[k][:, cs], in0=PS[ie][:, :], in1=sg[k][:, cs], op=ALU.mult), reads=[PSb[ie], bsg[k]], writes=[beg[k]])
            S.op("act", lambda e: e.activation(out=junk[:], in_=eg[k][:], func=ACT.Square, accum_out=ssp[:, 2 * k:2 * k + 1]), reads=[beg[k]], writes=[bjunk, bssp[k]])

        def back3(i):
            k = i % R
            rows = slice(128 * i, 128 * (i + 1))
            rstd_from_ss(S, ssp[:, 2 * k:2 * k + 1], ssp[:, 2 * k + 1:2 * k + 2], D, [bssp[k]], [bssp[k]])
            S.op("dve", lambda e: e.scalar_tensor_tensor(out=eg[k][:], in0=eg[k][:], scalar=ssp[:, 2 * k + 1:2 * k + 2], in1=gple[:], op0=ALU.mult, op1=ALU.mult), reads=[bssp[k], bw], writes=[beg[k]])
            S.op("pool", lambda e: e.tensor_tensor(out=eg[k][:], in0=eg[k][:], in1=x2t[k][:], op=ALU.add), reads=[bx2t[k]], writes=[beg[k]])
            S.dma(A["out"][rows, :], eg[k][:], reads=[beg[k]], eng="pool")

        for i in range(4):
            load(i)
        front(0)
        front(1)
        for i in range(ntile + 1):
            if i >= 1:
                back3(i - 1)
            if i < ntile:
                back(i)
            if i + 4 < ntile:
                load(i + 4)
            if i + 2 < ntile:
                front(i + 2)


def attn_stream(S, pairs, PS, PSb, PT, bPT, srot=(0, 1, 2), skew=2):
    n = len(pairs)
    NP = len(PT)

    def emitS(k):
        pr = pairs[k]
        ib = srot[k % len(srot)]
        m = len(pr["smm"])
        for q, (l, r, rd, lo, hi) in enumerate(pr["smm"]):
            S.op("pe", lambda e: e.matmul(PS[ib][:, lo:hi], lhsT=l, rhs=r, start=(q == 0), stop=(q == m - 1), skip_group_check=True), reads=rd, writes=[PSb[ib]])

    for k in range(min(skew, n)):
        emitS(k)
    for k in range(n):
        if k + skew < n:
            emitS(k + skew)
        pr = pairs[k]
        ib = srot[k % len(srot)]
        lo, hi = pr.get("cols", (0, 512))
        S.op("act", lambda e: e.activation(out=PT[k % NP][:, lo:hi], in_=PS[ib][:, lo:hi], func=ACT.Exp), reads=[PSb[ib]], writes=[bPT[k % NP]])
        ob, first, last = pr["O"]
        vl, vrd = pr["v"]
        S.op("pe", lambda e: e.matmul(PS[ob][:, lo:hi], lhsT=vl, rhs=PT[k % NP][:, lo:hi], start=first, stop=last, skip_group_check=True), reads=[bPT[k % NP]] + vrd, writes=[PSb[ob]])
        if pr.get("fin") is not None:
            pr["fin"]()


def stage_mix(nc, S, A, sq, PS, PSb, PB, PBb, ident, bid, x1b, dbg):
    r0 = sq * T
    with ExitStack() as s1:
        sb = lambda name, shape, dt: s1.enter_context(_sbt(nc, name, shape, dt))
        nbc = [0]

        def nb():
            nbc[0] = (nbc[0] + 1) % 7
            return nbc[0]

        cst = Buf()
        negc = sb("negc", [128, 512], BF16)
        negw = sb("negw", [128, 512], BF16)
        negm = sb("negm", [128, 4, 512], BF16)
        selg = sb("selg", [24, 24 * 64], BF16)
        cmptbl = sb("cmptbl", [128, 128], F32)
        v12 = sb("v12", [128, 2], F32)
        invf = sb("invf", [128, 1], F32)
        ones = sb("ones", [128, 128], BF16)
        S.dma(negc[:], A["negc"], writes=[cst], eng="pool")
        S.dma(negw[:], A["negw"], writes=[cst], eng="pool")
        S.dma(negm[:], A["negm"].rearrange("m p t -> p m t"), writes=[cst], eng="pool")
        S.dma(selg[:], A["selg"], writes=[cst], eng="pool")
        S.dma(cmptbl[:], A["cmptbl"], writes=[cst])
        S.dma(v12[:], A["v12"], writes=[cst])
        S.dma(invf[64:96, :], A["invf"], writes=[cst])
        S.op("pool", lambda e: e.memset(ones[:], 1.0), writes=[cst])
        onesf = sb("onesf", [64, 512], F32)
        S.op("pool", lambda e: e.memset(onesf[:], -1.0), writes=[cst])
        gpm = sb("gpm", [128, D], F32)
        S.dma(gpm[:], A["g_pre_mix"][0:1, :].to_broadcast([128, D]), writes=[cst])

        hT = sb("hT", [128, 8, T], BF16)
        bhT = [Buf() for _ in range(4)]
        onsa = sb("onsa", [128, 4, T], BF16)
        omla = sb("omla", [128, 4, T], BF16)
        bonsa = Buf()
        bomla = Buf()
        junk = sb("junkm", [128, D], BF16)
        bjunk = Buf()
        sst = sb("sst", [128, 8], F32)

        with ExitStack() as s2:
            sb2 = lambda name, shape, dt: s2.enter_context(_sbt(nc, name, shape, dt))
            xt = [sb2(f"xt{i}", [128, D], F32) for i in range(2)]
            hn = [sb2(f"hnm{i}", [128, D], BF16) for i in range(2)]
            bxt = [Buf() for _ in range(2)]
            bhn = [Buf() for _ in range(2)]
            bs = [Buf() for _ in range(2)]
            for i in range(16):
                k = i % 2
                S.dma(xt[k][:], A["x"][r0 + 128 * i: r0 + 128 * (i + 1), :], writes=[bxt[k]])
                S.op("act", lambda e: e.activation(out=junk[:], in_=xt[k][:], func=ACT.Square, accum_out=sst[:, k:k + 1]), reads=[bxt[k]], writes=[bjunk, bs[k]])
                rstd_from_ss(S, sst[:, k:k + 1], sst[:, 2 + k:3 + k], D, [bs[k]], [bs[k]])
                S.op("dve", lambda e: e.scalar_tensor_tensor(out=hn[k][:], in0=xt[k][:], scalar=sst[:, 2 + k:3 + k], in1=gpm[:], op0=ALU.mult, op1=ALU.mult), reads=[bxt[k], bs[k], cst], writes=[bhn[k]])
                for kc in range(8):
                    S.op("pe", lambda e: e.transpose(out=PB[:, kc * 128:(kc + 1) * 128], in_=hn[k][:, kc * 128:(kc + 1) * 128], identity=ident[:]), reads=[bhn[k], bid], writes=[PBb])
                S.op("act", lambda e: e.copy(out=hT[:, :, i * 128:(i + 1) * 128], in_=PB[:, :].rearrange("p (k t) -> p k t", k=8)), reads=[PBb], writes=[bhT[i // 4]])

        with ExitStack() as s2:
            sb2 = lambda name, shape, dt: s2.enter_context(_sbt(nc, name, shape, dt))
            s3 = ExitStack()
            sb3 = lambda name, shape, dt: s3.enter_context(_sbt(nc, name, shape, dt))
            cos2 = sb2("cos2", [128, T], F32)
            sin2 = sb2("sin2", [128, T], F32)
            RP = slice(64, 96)
            brope = Buf()
            krope = sb2("krope", [128, T], BF16)
            cqn = sb2("cqn", [128, 2, T], BF16)
            ckvn = sb2("ckvn", [128, T], BF16)
            t1 = sb2("t1", [128, 512], F32)
            t2 = sb2("t2", [128, 512], F32)
            posi = sb3("posi", [128, T], I32)
            ang = sb3("ang", [128, T], F32)
            tmpa = sb3("tmpa", [128, T], F32)
            S.dma(posi[RP, :], A["pos"][sq:sq + 1, :].to_broadcast([32, T]), writes=[brope])
            S.op("dve", lambda e: e.tensor_copy(out=ang[RP, :], in_=posi[RP, :]), reads=[brope], writes=[brope])
            S.op("dve", lambda e: e.tensor_scalar(out=ang[RP, :], in0=ang[RP, :], scalar1=invf[RP, 0:1], scalar2=None, op0=ALU.mult), reads=[cst], writes=[brope])
            qi = sb3("qi", [128, T], I32)
            for (addc, dst) in ((0.0, sin2), (PI / 2, cos2)):
                S.op("dve", lambda e: e.tensor_scalar(out=tmpa[RP, :], in0=ang[RP, :], scalar1=addc, scalar2=1.0 / TWO_PI, op0=ALU.add, op1=ALU.mult), reads=[brope], writes=[brope])
                S.op("dve", lambda e: e.tensor_copy(out=qi[RP, :], in_=tmpa[RP, :]), reads=[brope], writes=[brope])
                S.op("dve", lambda e: e.tensor_copy(out=tmpa[RP, :], in_=qi[RP, :]), reads=[brope], writes=[brope])
                S.op("dve", lambda e: e.scalar_tensor_tensor(out=tmpa[RP, :], in0=tmpa[RP, :], scalar=-TWO_PI, in1=ang[RP, :], op0=ALU.mult, op1=ALU.add), reads=[brope], writes=[brope])
                if addc != 0.0:
                    S.op("dve", lambda e: e.tensor_scalar(out=tmpa[RP, :], in0=tmpa[RP, :], scalar1=addc, scalar2=None, op0=ALU.add), reads=[brope], writes=[brope])
                S.op("dve", lambda e: e.tensor_scalar(out=dst[RP, :], in0=tmpa[RP, :], scalar1=PI, scalar2=-TWO_PI, op0=ALU.is_gt, op1=ALU.mult), reads=[brope], writes=[brope])
                S.op("dve", lambda e: e.tensor_tensor(out=tmpa[RP, :], in0=tmpa[RP, :], in1=dst[RP, :], op=ALU.add), reads=[brope], writes=[brope])
                S.op("dve", lambda e: e.tensor_scalar(out=tmpa[RP, :], in0=tmpa[RP, :], scalar1=-PI, scalar2=PI, op0=ALU.max, op1=ALU.min), reads=[brope], writes=[brope])
                S.op("act", lambda e: e.activation(out=dst[RP, :], in_=tmpa[RP, :], func=ACT.Sin), reads=[brope], writes=[brope])
            wA = sb3("wA", [128, 8, 416], BF16)
            wkrot = sb3("wkrot", [128, 8, 32], BF16)
            bwA = Buf()
            S.dma(wA[:], A["w_in"][:, :, 1304:1720], writes=[bwA], eng="pool")
            S.op("pool", lambda e: e.tensor_scalar(out=wkrot[:, :, 0:16], in0=wA[:, :, 400:416], scalar1=-1.0, scalar2=None, op0=ALU.mult), reads=[bwA], writes=[bwA])
            S.op("pool", lambda e: e.tensor_copy(out=wkrot[:, :, 16:32], in_=wA[:, :, 384:400]), reads=[bwA], writes=[bwA])
            gq = sb3("gq", [128, 2], F32)
            gkv = sb3("gkv", [128, 1], F32)
            S.dma(gq[:], A["g_q"], writes=[bwA])
            S.dma(gkv[:], A["g_kv"], writes=[bwA])
            bcq = Buf()
            cf = [sb3(f"cf{m}", [128, 512], F32) for m in range(3)]
            sqb = [sb3(f"sqb{m}", [128, 512], BF16) for m in range(3)]
            rq = sb3("rq", [128, 512], F32)
            rk = sb3("rk", [128, 512], F32)
            bcf = [Buf() for _ in range(3)]
            bsqb = [Buf() for _ in range(3)]
            brq, brk, bt1, bt2 = Buf(), Buf(), Buf(), Buf()
            for c in range(4):
                cs = slice(c * 512, (c + 1) * 512)
                for m, col0 in enumerate((0, 128, 256)):
                    ib = nb()
                    for kc in range(8):
                        S.op("pe", lambda e: e.matmul(PS[ib][:, :], lhsT=wA[:, kc, col0:col0 + 128], rhs=hT[:, kc, cs], start=(kc == 0), stop=(kc == 7)), reads=[bwA, bhT[c]], writes=[PSb[ib]])
                    S.op("act", lambda e: e.copy(out=cf[m][:], in_=PS[ib][:, :]), reads=[PSb[ib]], writes=[bcf[m]])
                    S.op("act", lambda e: e.activation(out=sqb[m][:], in_=PS[ib][:, :], func=ACT.Square), reads=[PSb[ib]], writes=[bsqb[m]])
                iq = nb()
                S.op("pe", lambda e: e.matmul(PS[iq][:, :], lhsT=ones[:], rhs=sqb[0][:], start=True, stop=False), reads=[cst, bsqb[0]], writes=[PSb[iq]])
                S.op("pe", lambda e: e.matmul(PS[iq][:, :], lhsT=ones[:], rhs=sqb[1][:], start=False, stop=True), reads=[cst, bsqb[1]], writes=[PSb[iq]])
                ik = nb()
                S.op("pe", lambda e: e.matmul(PS[ik][:, :], lhsT=ones[:], rhs=sqb[2][:], start=True, stop=True), reads=[cst, bsqb[2]], writes=[PSb[ik]])
                S.op("act", lambda e: e.activation(out=rq[:], in_=PS[iq][:, :], func=ACT.Sqrt, scale=1.0 / 256, bias=EPS), reads=[PSb[iq]], writes=[brq])
                S.op("dve", lambda e: e.reciprocal(out=rq[:], in_=rq[:]), reads=[brq], writes=[brq])
                S.op("act", lambda e: e.activation(out=rk[:], in_=PS[ik][:, :], func=ACT.Sqrt, scale=1.0 / 128, bias=EPS), reads=[PSb[ik]], writes=[brk])
                S.op("dve", lambda e: e.reciprocal(out=rk[:], in_=rk[:]), reads=[brk], writes=[brk])
                for m in range(2):
                    S.op("dve", lambda e: e.scalar_tensor_tensor(out=cqn[:, m, cs], in0=cf[m][:], scalar=gq[:, m:m + 1], in1=rq[:], op0=ALU.mult, op1=ALU.mult), reads=[bcf[m], brq, bwA], writes=[bcq])
                S.op("dve", lambda e: e.scalar_tensor_tensor(out=ckvn[:, cs], in0=cf[2][:], scalar=gkv[:, 0:1], in1=rk[:], op0=ALU.mult, op1=ALU.mult), reads=[bcf[2], brk, bwA], writes=[bcq])
                i1 = nb()
                i2 = nb()
                for kc in range(8):
                    S.op("pe", lambda e: e.matmul(PS[i1][64:96, :], lhsT=wA[:, kc, 384:416], rhs=hT[:, kc, cs], start=(kc == 0), stop=(kc == 7), tile_position=(0, 64)), reads=[bwA, bhT[c]], writes=[PSb[i1]])
                for kc in range(8):
                    S.op("pe", lambda e: e.matmul(PS[i2][64:96, :], lhsT=wkrot[:, kc, :], rhs=hT[:, kc, cs], start=(kc == 0), stop=(kc == 7), tile_position=(0, 64)), reads=[bwA, bhT[c]], writes=[PSb[i2]])
                S.op("dve", lambda e: e.tensor_tensor(out=t1[RP, :], in0=PS[i1][64:96, :], in1=cos2[RP, cs], op=ALU.mult), reads=[PSb[i1], brope], writes=[bt1])
                S.op("dve", lambda e: e.tensor_tensor(out=t2[RP, :], in0=PS[i2][64:96, :], in1=sin2[RP, cs], op=ALU.mult), reads=[PSb[i2], brope], writes=[bt2])
                S.op("pool", lambda e: e.tensor_tensor(out=krope[RP, cs], in0=t1[RP, :], in1=t2[RP, :], op=ALU.add), reads=[bt1, bt2], writes=[bcq])

            S.barrier()
            s3.close()
            wuq = sb2("wuq", [128, 2, 768], BF16)
            wuqr = sb2("wuqr", [128, 2, 8, 32], BF16)
            wukv = sb2("wukv", [128, 1024], BF16)
            bwu = Buf()
            S.dma(wuq[:], A["w_uq"], writes=[bwu], eng="pool")
            S.dma(wukv[:], A["w_ukv"], writes=[bwu], eng="pool")
            wuq4 = wuq[:, :, :].rearrange("p m (h c) -> p m h c", h=8)
            for m in range(2):
                S.op("pool", lambda e: e.tensor_scalar(out=wuqr[:, m, :, 0:16], in0=wuq4[:, m, :, 80:96], scalar1=-1.0, scalar2=None, op0=ALU.mult), reads=[bwu], writes=[bwu])
                S.op("pool", lambda e: e.tensor_copy(out=wuqr[:, m, :, 16:32], in_=wuq4[:, m, :, 64:80]), reads=[bwu], writes=[bwu])
            VAm = sb2("VAm", [128, 16, 8, 128], BF16)
            bVA = Buf()
            S.op("pool", lambda e: e.memset(VAm[:], 1.0), writes=[bVA])
            wukv3 = wukv[:, :].rearrange("p (h c) -> p h c", h=8)
            for kt in range(16):
                ib = nb()
                S.op("pe", lambda e: e.matmul(PS[ib][:, :], lhsT=ckvn[:, kt * 128:(kt + 1) * 128], rhs=wukv3[:, :, 64:128], start=True, stop=True), reads=[bcq, bwu], writes=[PSb[ib]])
                S.op("act", lambda e: e.copy(out=VAm[:, kt, :, 0:64], in_=PS[ib][:, :].rearrange("p (h c) -> p h c", h=8)), reads=[PSb[ib]], writes=[bVA])
            qnt = [sb2(f"qnt{i}", [128, T], BF16) for i in range(2)]
            knt = [sb2(f"knt{i}", [128, T], BF16) for i in range(2)]
            negc1 = sb2("negc1", [128, 128], BF16)
            S.dma(negc1[:], A["negc"][:, 0:128], writes=[cst], eng="pool")
            bqk = [Buf() for _ in range(2)]
            PT = [sb2(f"PT{i}", [128, 512], BF16) for i in range(4)]
            bPT = [Buf() for _ in range(4)]
            rcm = sb2("rcm", [64, 512], F32)
            brcm = Buf()
            def proj(h, c):
                hk = h % 2
                qn, kn = qnt[hk], knt[hk]
                cs = slice(c * 512, (c + 1) * 512)
                if c == 0:
                    S.op("pool", lambda e: e.tensor_copy(out=kn[RP, :], in_=krope[RP, :]), reads=[bcq], writes=[bqk[hk]])
                for m in range(2):
                    S.op("pe", lambda e: e.matmul(PS[5][0:96, :], lhsT=wuq[:, m, h * 96:h * 96 + 96], rhs=cqn[:, m, cs], start=(m == 0), stop=(m == 1)), reads=[bwu, bcq], writes=[PSb[5]])
                S.op("act", lambda e: e.mul(out=qn[0:64, cs], in_=PS[5][0:64, :], mul=SC_M), reads=[PSb[5]], writes=[bqk[hk]])
                S.op("dve", lambda e: e.scalar_tensor_tensor(out=t1[RP, :], in0=PS[5][64:96, :], scalar=SC_M, in1=cos2[RP, cs], op0=ALU.mult, op1=ALU.mult), reads=[PSb[5], brope], writes=[bt1])
                S.op("pe", lambda e: e.matmul(PS[6][0:64, :], lhsT=wukv[:, h * 128:h * 128 + 64], rhs=ckvn[:, cs], start=True, stop=True), reads=[bwu, bcq], writes=[PSb[6]])
                for m in range(2):
                    S.op("pe", lambda e: e.matmul(PS[6][64:96, :], lhsT=wuqr[:, m, h, :], rhs=cqn[:, m, cs], start=(m == 0), stop=(m == 1), tile_position=(0, 64)), reads=[bwu, bcq], writes=[PSb[6]])
                S.op("dve", lambda e: e.tensor_copy(out=kn[0:64, cs], in_=PS[6][0:64, :]), reads=[PSb[6]], writes=[bqk[hk]])
                S.op("dve", lambda e: e.scalar_tensor_tensor(out=t2[RP, :], in0=PS[6][64:96, :], scalar=SC_M, in1=sin2[RP, cs], op0=ALU.mult, op1=ALU.mult), reads=[PSb[6], brope], writes=[bt2])
                S.op("pool", lambda e: e.tensor_tensor(out=qn[RP, cs], in0=t1[RP, :], in1=t2[RP, :], op=ALU.add), reads=[bt1, bt2], writes=[bqk[hk]])

            def attn(h):
                hk = h % 2
                qn, kn = qnt[hk], knt[hk]
                pairs = []
                for c in range(4):
                    cs = slice(c * 512, (c + 1) * 512)
                    ob = 3 + (c % 2)

                    def fin(c=c, cs=cs, ob=ob):
                        S.op("dve", lambda e: e.reciprocal(out=rcm[:], in_=PS[ob][64:128, :]), reads=[PSb[ob]], writes=[brcm])
                        S.op("dve", lambda e: e.tensor_tensor(out=omla[hk * 64:hk * 64 + 64, h // 2, cs], in0=PS[ob][0:64, :], in1=rcm[:], op=ALU.mult), reads=[PSb[ob], brcm], writes=[bomla])
                        if h + 1 < 8:
                            proj(h + 1, c)

                    nk = 4 * c + 4
                    for kt in range(nk):
                        ks = slice(kt * 128, (kt + 1) * 128)
                        lo = 128 * (kt - 4 * c) if kt >= 4 * c else 0
                        smm = [(kn[0:96, ks], qn[0:96, c * 512 + lo:(c + 1) * 512], [bqk[hk]], lo, 512)]
                        if kt >= 4 * c:
                            smm.append((ident[:], negc1[:], [bid, cst], lo, lo + 128))
                        pairs.append(dict(smm=smm, cols=(lo, 512), v=(VAm[:, kt, h, :], [bVA]), O=(ob, kt == 0, kt == nk - 1), fin=fin if kt == nk - 1 else None))
                attn_stream(S, pairs, PS, PSb, PT, bPT)

            for c in range(4):
                proj(0, c)
            for h in range(8):
                attn(h)
            if dbg:
                S.dma(A["d_omla"], omla[:], reads=[bomla], eng="pool")
        S.barrier()
        if os.environ.get('SKIP_NSA') is None:
          stage_nsa(nc, S, A, sq, PS, PSb, PB, PBb, ident, bid, s1, hT, bhT, onsa, bonsa, cst, negc, negw, selg, cmptbl, v12, dbg, nb, onesf)
        S.barrier()
        if os.environ.get('SKIP_MERGE') is None:
          stage_merge(nc, S, A, sq, PS, PSb, PB, PBb, ident, bid, hT, bhT, onsa, bonsa, omla, bomla, x1b, nb, junk, bjunk)


def stage_nsa(nc, S, A, sq, PS, PSb, PB, PBb, ident, bid, s1, hT, bhT, onsa, bonsa, cst, negc, negw, selg, cmptbl, v12, dbg, nb, onesf):
    with ExitStack() as s2:
        sb = lambda name, shape, dt: s2.enter_context(_sbt(nc, name, shape, dt))
        QA = [sb(f"QA{g}", [128, 16 * 512], BF16) for g in range(2)]
        bQA = [[Buf() for _ in range(16)] for g in range(2)]
        BR = ("sel", "win")
        KA = {(b, g): sb(f"KA{b}{g}", [128, T], BF16) for b in BR for g in range(2)}
        bKA = {k: Buf() for k in KA}
        VA = {(b, g): sb(f"VA{b}{g}", [128, 16, 128], BF16) for b in BR for g in range(2)}
        bVA = {k: Buf() for k in VA}
        gsig = sb("gsig", [24, T], BF16)
        bgs = Buf()
        KC = [sb(f"KC{g}", [128, 128], BF16) for g in range(2)]
        VC = [sb(f"VC{g}", [128, 64], BF16) for g in range(2)]
        bKC = [Buf() for _ in range(2)]
        bVC = [Buf() for _ in range(2)]
        for g in range(2):
            S.op("pool", lambda e: e.memset(QA[g][64:96, :], 0.0), writes=bQA[g])
            S.dma(QA[g][96:100, :], A["qaug"][g, 32:36, :], writes=bQA[g], eng="pool")
            S.op("pool", lambda e: e.memset(KC[g][:], 0.0), writes=[bKC[g]])
            S.dma(KC[g][64:100, :], A["kaug_cmp"], writes=[bKC[g]], eng="pool")
            S.op("pool", lambda e: e.memset(VC[g][:], 0.0), writes=[bVC[g]])
            for b in BR:
                S.dma(KA[(b, g)][64:100, :], A["kaug_" + b], writes=[bKA[(b, g)]], eng="pool")
                S.op("pool", lambda e: e.memset(VA[(b, g)][:], 1.0), writes=[bVA[(b, g)]])
        NSTOP = int(os.environ.get('NSA_STOP', '99'))
        if NSTOP <= 1:
            S.barrier()
            return
        with ExitStack() as s3:
            sb3 = lambda name, shape, dt: s3.enter_context(_sbt(nc, name, shape, dt))
            wN = sb3("wN", [128, 8, 1304], BF16)
            bwN = Buf()
            S.dma(wN[:], A["w_in"][:, :, 0:1304], writes=[bwN], eng="pool")
            kcT = {(kd, g): sb3(f"kcT{kd}{g}", [64, T], BF16) for kd in range(2) for g in range(2)}
            bkc = {k: Buf() for k in kcT}
            for c in range(4):
                cs = slice(c * 512, (c + 1) * 512)
                for n in range(4):
                    ib = nb()
                    for kc in range(8):
                        S.op("pe", lambda e: e.matmul(PS[ib][:, :], lhsT=wN[:, kc, n * 128:(n + 1) * 128], rhs=hT[:, kc, cs], start=(kc == 0), stop=(kc == 7)), reads=[bwN, bhT[c]], writes=[PSb[ib]])
                    g = n // 2
                    rr = (2 * n) % 4
                    QAv = QA[g][:, :].rearrange("p (i r t) -> p i r t", i=16, r=4)
                    S.op("act", lambda e: e.mul(out=QAv[0:64, 4 * c:4 * c + 4, rr, :], in_=PS[ib][0:64, :].rearrange("p (i t) -> p i t", i=4), mul=SC_N), reads=[PSb[ib]], writes=bQA[g][4 * c:4 * c + 4])
                    S.op("dve", lambda e: e.tensor_scalar(out=QAv[0:64, 4 * c:4 * c + 4, rr + 1, :], in0=PS[ib][64:128, :].rearrange("p (i t) -> p i t", i=4), scalar1=SC_N, scalar2=None, op0=ALU.mult), reads=[PSb[ib]], writes=bQA[g][4 * c:4 * c + 4])
                for kind in (0, 1, 2, 4):
                    ib = nb()
                    col0 = 512 + kind * 128
                    for kc in range(8):
                        S.op("pe", lambda e: e.matmul(PS[ib][:, :], lhsT=wN[:, kc, col0:col0 + 128], rhs=hT[:, kc, cs], start=(kc == 0), stop=(kc == 7)), reads=[bwN, bhT[c]], writes=[PSb[ib]])
                    for g in range(2):
                        if kind < 2:
                            dst, bd = kcT[(kind, g)][0:64, cs], bkc[(kind, g)]
                        else:
                            key = ("sel" if kind == 2 else "win", g)
                            dst, bd = KA[key][0:64, cs], bKA[key]
                        if g == 0:
                            S.op("act", lambda e: e.copy(out=dst, in_=PS[ib][0:64, :]), reads=[PSb[ib]], writes=[bd])
                        else:
                            S.op("dve", lambda e: e.tensor_copy(out=dst, in_=PS[ib][64:128, :]), reads=[PSb[ib]], writes=[bd])
                ib = nb()
                for kc in range(8):
                    S.op("pe", lambda e: e.matmul(PS[ib][0:24, :], lhsT=wN[:, kc, 1280:1304], rhs=hT[:, kc, cs], start=(kc == 0), stop=(kc == 7)), reads=[bwN, bhT[c]], writes=[PSb[ib]])
                S.op("act", lambda e: e.activation(out=gsig[:, cs], in_=PS[ib][0:24, :], func=ACT.Sigmoid), reads=[PSb[ib]], writes=[bgs])
            for kt in range(int(os.environ.get('NKT', '16')) if NSTOP > 2 else 0):
                ib = nb()
                ts = slice(kt * 128, (kt + 1) * 128)
                for q, col0 in enumerate((896, 1152)):
                    for kc in range(8):
                        S.op("pe", lambda e: e.matmul(PS[ib][:, q * 128:(q + 1) * 128], lhsT=hT[:, kc, ts], rhs=wN[:, kc, col0:col0 + 128], start=(kc == 0), stop=(kc == 7)), reads=[bwN, bhT[kt // 4]], writes=[PSb[ib]])
                for q, b in enumerate(BR):
                    for g in range(int(os.environ.get('VCOPY', '2'))):
                        if g == 0:
                            S.op("act", lambda e: e.copy(out=VA[(b, g)][:, kt, 0:64], in_=PS[ib][:, q * 128 + g * 64: q * 128 + g * 64 + 64]), reads=[PSb[ib]], writes=[bVA[(b, g)]])
                        else:
                            S.op("dve", lambda e: e.tensor_copy(out=VA[(b, g)][:, kt, 0:64], in_=PS[ib][:, q * 128 + g * 64: q * 128 + g * 64 + 64]), reads=[PSb[ib]], writes=[bVA[(b, g)]])
            for kd, (w1n, b1n, w2n, posn) in enumerate((("ck_w1", "ck_b1", "ck_w2", "posT_k"), ("cv_w1", "cv_b1", "cv_w2", "posT_v"))[:int(os.environ.get('NCMP', '2'))]):
                W1 = sb3(f"W1{kd}", [64, 32, 64], BF16)
                posT = sb3(f"posT{kd}", [64, 32], BF16)
                b1 = sb3(f"b1{kd}", [64, 1], F32)
                W2 = sb3(f"W2{kd}", [64, 64], BF16)
                bias = sb3(f"bias{kd}", [64, 1], F32)
                bcw = Buf()
                bbias = Buf()
                S.dma(W1[:], A[w1n], writes=[bcw], eng="pool")
                S.dma(posT[:], A[posn], writes=[bcw], eng="pool")
                S.dma(W2[:], A[w2n], writes=[bcw], eng="pool")
                S.dma(b1[:], A[b1n], writes=[bcw])
                ip = nb()
                for l in range(32):
                    S.op("pe", lambda e: e.matmul(PS[ip][0:64, 0:1], lhsT=W1[:, l, :], rhs=posT[:, l:l + 1], start=(l == 0), stop=(l == 31)), reads=[bcw], writes=[PSb[ip]])
                S.op("dve", lambda e: e.tensor_tensor(out=bias[:], in0=PS[ip][0:64, 0:1], in1=b1[:], op=ALU.add), reads=[PSb[ip], bcw], writes=[bbias])
                for g in range(2):
                    G = sb3(f"G{kd}{g}", [64, 128], BF16)
                    bG = Buf()
                    S.op("pool", lambda e: e.memset(G[:], 0.0), writes=[bG])
                    ia = nb()
                    for l in range(32):
                        S.op("pe", lambda e: e.matmul(PS[ia][0:64, 0:127], lhsT=W1[:, l, :], rhs=kcT[(kd, g)][0:64, l:l + 2017:16], start=(l == 0), stop=(l == 31)), reads=[bcw, bkc[(kd, g)]], writes=[PSb[ia]])
                    S.op("act", lambda e: e.activation(out=G[:, 0:127], in_=PS[ia][0:64, 0:127], func=ACT.Gelu_apprx_tanh, bias=bias[:, 0:1]), reads=[PSb[ia], bbias], writes=[bG])
                    io = nb()
                    if kd == 0:
                        S.op("pe", lambda e: e.matmul(PS[io][0:64, 0:127], lhsT=W2[:, :], rhs=G[:, 0:127], start=True, stop=True), reads=[bcw, bG], writes=[PSb[io]])
                        S.op("act", lambda e: e.copy(out=KC[g][0:64, 0:127], in_=PS[io][0:64, 0:127]), reads=[PSb[io]], writes=[bKC[g]])
                    else:
                        S.op("pe", lambda e: e.matmul(PS[io][0:127, 0:64], lhsT=G[:, 0:127], rhs=W2[:, :], start=True, stop=True), reads=[bcw, bG], writes=[PSb[io]])
                        S.op("act", lambda e: e.copy(out=VC[g][0:127, :], in_=PS[io][0:127, 0:64]), reads=[PSb[io]], writes=[bVC[g]])
            S.barrier()
        f32t = lambda name, shape: sb(name, shape, F32)
        sc = f32t("sc", [128, 512])
        ex = f32t("ex", [128, 512])
        pp = f32t("pp", [128, 512])
        pbf = sb("pbf", [128, 512], BF16)
        PTc = sb("PTc", [128, 512], BF16)
        sm = f32t("sm", [128, 8])
        t1i = f32t("t1i", [128, 4, 32])
        imp = f32t("imp", [128, 32])
        score = f32t("score", [128, 32])
        sc2 = f32t("sc2", [128, 32])
        m8 = f32t("m8", [128, 16])
        Z = sb("Z", [128, 128], BF16)
        rcs = f32t("rcs", [64, 512])
        gc = f32t("gc", [64, 512])
        cfm = f32t("cfm", [64, 512])
        tb = f32t("tb", [64, 512])
        acc = f32t("acc", [64, 512])
        bsc, bex, bpp, bpbf, bPTc, bsm, bsel, bZ, brcs, bgc, bcfm, btb, bacc = [Buf() for _ in range(13)]
        PT = [sb(f"PTn{i}", [128, 512], BF16) for i in range(4)]
        bPT = [Buf() for _ in range(4)]
        S.op("pool", lambda e: e.memset(Z[:], 0.0), writes=[bZ])
        v3 = lambda t: t[:, :].rearrange("p (r j) -> p r j", r=4)
        for g in range(2):
            QAv = QA[g][:, :].rearrange("p (i r t) -> p i r t", i=16, r=4)
            for i in range(int(os.environ.get('NSA_NI', '16'))):
                nj = 8 * i + 8
                ts = slice(i * 128, (i + 1) * 128)
                rhsQ = QA[g][0:100, i * 512:(i + 1) * 512]
                for r in range(4):
                    S.op("pe", lambda e: e.matmul(PS[6][:, r * 128:r * 128 + nj], lhsT=QA[g][0:100, (4 * i + r) * 128:(4 * i + r + 1) * 128], rhs=KC[g][0:100, 0:nj], start=True, stop=True), reads=[bQA[g][i], bKC[g]], writes=[PSb[6]])
                S.op("dve", lambda e: e.tensor_tensor(out=v3(sc)[:, :, 0:nj], in0=v3(PS[6])[:, :, 0:nj], in1=cmptbl[:, 120 - 8 * i:120 - 8 * i + nj].unsqueeze(1).to_broadcast([128, 4, nj]), op=ALU.add), reads=[PSb[6], cst], writes=[bsc])
                S.op("act", lambda e: e.activation(out=v3(ex)[:, :, 0:nj], in_=v3(sc)[:, :, 0:nj], func=ACT.Exp), reads=[bsc], writes=[bex])
                S.op("dve", lambda e: e.tensor_reduce(out=sm[:, 0:4], in_=v3(ex)[:, :, 0:nj], axis=AX.X, op=ALU.add), reads=[bex], writes=[bsm])
                S.op("dve", lambda e: e.tensor_scalar(out=sm[:, 0:4], in0=sm[:, 0:4], scalar1=1e-30, scalar2=None, op0=ALU.max), reads=[bsm], writes=[bsm])
                S.op("dve", lambda e: e.reciprocal(out=sm[:, 4:8], in_=sm[:, 0:4]), reads=[bsm], writes=[bsm])
                S.op("dve", lambda e: e.tensor_tensor(out=v3(pp)[:, :, 0:nj], in0=v3(ex)[:, :, 0:nj], in1=sm[:, 4:8].unsqueeze(2).to_broadcast([128, 4, nj]), op=ALU.mult), reads=[bex, bsm], writes=[bpp])
                S.op("pool", lambda e: e.tensor_copy(out=v3(pbf)[:, :, 0:nj], in_=v3(pp)[:, :, 0:nj]), reads=[bpp], writes=[bpbf])
                for r in range(4):
                    S.op("pe", lambda e: e.transpose(out=PB[0:nj, r * 128:(r + 1) * 128], in_=pbf[:, r * 128:r * 128 + nj], identity=ident[:]), reads=[bpbf, bid], writes=[PBb])
                S.op("act", lambda e: e.copy(out=PTc[0:nj, :], in_=PB[0:nj, 0:512]), reads=[PBb], writes=[bPTc])
                S.op("pe", lambda e: e.matmul(PS[5][0:64, :], lhsT=VC[g][0:nj, 0:64], rhs=PTc[0:nj, :], start=True, stop=True), reads=[bVC[g], bPTc], writes=[PSb[5]])
                pairs = []
                k0 = max(0, i - 4)
                for kt in range(k0, i + 1):
                    ks = slice(kt * 128, (kt + 1) * 128)
                    smm = [(KA[("win", g)][0:100, ks], rhsQ, [bKA[("win", g)], bQA[g][i]], 0, 512)]
                    if kt == i:
                        smm.append((ident[:], negc[:], [bid, cst], 0, 512))
                    elif kt == i - 4:
                        smm.append((ident[:], negw[:], [bid, cst], 0, 512))
                    pairs.append(dict(smm=smm, v=(VA[("win", g)][:, kt, :], [bVA[("win", g)]]), O=(4, kt == k0, kt == i), fin=None))
                attn_stream(S, pairs, PS, PSb, PT, bPT)
                if i >= 8:
                    nsb = nj // 4
                    S.op("dve", lambda e: e.tensor_reduce(out=t1i[:, :, 0:nsb], in_=v3(pp)[:, :, 0:nj].rearrange("p r (s q) -> p r s q", q=4), axis=AX.X, op=ALU.add), reads=[bpp], writes=[bsel])
                    S.op("dve", lambda e: e.tensor_reduce(out=imp[:, 0:nsb], in_=t1i[:, :, 0:nsb].rearrange("p r s -> p s r"), axis=AX.X, op=ALU.add), reads=[bsel], writes=[bsel])
                    S.op("pool", lambda e: e.memset(score[:], -BIG), reads=[bsel], writes=[bsel])
                    S.op("dve", lambda e: e.tensor_copy(out=score[:, 0:2 * i], in_=imp[:, 0:2 * i]), reads=[bsel], writes=[bsel])
                    S.op("dve", lambda e: e.tensor_scalar(out=score[:, 2 * i - 1:2 * i], in0=imp[:, 2 * i - 1:2 * i], scalar1=v12[:, 1:2], scalar2=None, op0=ALU.add), reads=[bsel, cst], writes=[bsel])
                    S.op("pool", lambda e: e.memset(score[:, 0:1], BIG), reads=[bsel], writes=[bsel])
                    S.op("pool", lambda e: e.memset(score[:, 2 * i:2 * i + 1], BIG), reads=[bsel], writes=[bsel])
                    S.op("dve", lambda e: e.tensor_copy(out=score[:, 2 * i + 1:2 * i + 2], in_=v12[:, 0:1]), reads=[bsel, cst], writes=[bsel])
                    S.op("dve", lambda e: e.max(out=m8[:, 0:8], in_=score[:]), reads=[bsel], writes=[bsel])
                    S.op("dve", lambda e: e.match_replace(out=sc2[:], in_to_replace=m8[:, 0:8], in_values=score[:], imm_value=-3.0e9), reads=[bsel], writes=[bsel])
                    S.op("dve", lambda e: e.max(out=m8[:, 8:16], in_=sc2[:]), reads=[bsel], writes=[bsel])
                    S.op("dve", lambda e: e.tensor_scalar(out=Z[:, 64:96], in0=score[:], scalar1=m8[:, 15:16], scalar2=NEG, op0=ALU.is_lt, op1=ALU.mult), reads=[bsel], writes=[bZ])
                    S.op("pe", lambda e: e.transpose(out=PB[:, 512:640], in_=Z[:, :], identity=ident[:]), reads=[bZ, bid], writes=[PBb])
                    S.op("act", lambda e: e.copy(out=QAv[64:96, i, :, :], in_=PB[64:96, 512:640].unsqueeze(1).to_broadcast([32, 4, 128])), reads=[PBb], writes=[bQA[g][i]])
                pairs = []
                for kt in range(0, i + 1):
                    ks = slice(kt * 128, (kt + 1) * 128)
                    smm = [(KA[("sel", g)][0:100, ks], rhsQ, [bKA[("sel", g)], bQA[g][i]], 0, 512)]
                    if kt == i:
                        smm.append((ident[:], negc[:], [bid, cst], 0, 512))
                    pairs.append(dict(smm=smm, v=(VA[("sel", g)][:, kt, :], [bVA[("sel", g)]]), O=(3, kt == 0, kt == i), fin=None))
                attn_stream(S, pairs, PS, PSb, PT, bPT)
                for b in range(3):
                    for r in range(4):
                        col = (b * 8 + 4 * g + r) * 64
                        S.op("pe", lambda e: e.matmul(PS[6][0:64, r * 128:(r + 1) * 128], lhsT=selg[0:24, col:col + 64], rhs=gsig[0:24, ts], start=True, stop=True), reads=[cst, bgs], writes=[PSb[6]])
                    if b == 0:
                        S.op("act", lambda e: e.copy(out=gc[:], in_=PS[6][0:64, :]), reads=[PSb[6]], writes=[bgc])
                        S.op("dve", lambda e: e.tensor_tensor(out=acc[:], in0=PS[5][0:64, :], in1=gc[:], op=ALU.mult), reads=[PSb[5], bgc], writes=[bacc])
                    else:
                        ob = 3 if b == 1 else 4
                        S.op("dve", lambda e: e.reciprocal(out=rcs[:], in_=PS[ob][64:128, :]), reads=[PSb[ob]], writes=[brcs])
                        S.op("dve", lambda e: e.tensor_tensor(out=cfm[:], in0=PS[6][0:64, :], in1=rcs[:], op=ALU.mult), reads=[PSb[6], brcs], writes=[bcfm])
                        S.op("dve", lambda e: e.tensor_tensor(out=tb[:], in0=PS[ob][0:64, :], in1=cfm[:], op=ALU.mult), reads=[PSb[ob], bcfm], writes=[btb])
                        S.op("pool", lambda e: e.tensor_tensor(out=acc[:], in0=acc[:], in1=tb[:], op=ALU.add), reads=[btb], writes=[bacc])
                a3 = acc[:, :].rearrange("p (r t) -> p r t", r=4)
                S.op("pool", lambda e: e.tensor_copy(out=onsa[0:64, 2 * g:2 * g + 2, ts], in_=a3[:, 0::2, :]), reads=[bacc], writes=[bonsa])
                S.op("act", lambda e: e.copy(out=onsa[64:128, 2 * g:2 * g + 2, ts], in_=a3[:, 1::2, :]), reads=[bacc], writes=[bonsa])
        if dbg:
            S.dma(A["d_onsa"], onsa[:], reads=[bonsa], eng="pool")
        S.barrier()


def stage_merge(nc, S, A, sq, PS, PSb, PB, PBb, ident, bid, hT, bhT, onsa, bonsa, omla, bomla, x1b, nb, junk, bjunk):
    r0 = sq * T
    with ExitStack() as s2:
        sb = lambda name, shape, dt: s2.enter_context(_sbt(nc, name, shape, dt))
        wbn = sb("wbn", [128, 4, D], BF16)
        wbm = sb("wbm", [128, 4, D], BF16)
        wo = sb("wo", [128, 8, D], BF16)
        wmg = sb("wmg", [128, 8, 2048], BF16)
        gpo = sb("gpo", [128, D], F32)
        bw = Buf()
        S.dma(wmg[:], A["w_in"][:, :, 1720:3768], writes=[bw], eng="pool")
        S.dma(wbn[:], A["w_br_nsa"], writes=[bw], eng="pool")
        S.dma(wbm[:], A["w_br_mla"], writes=[bw], eng="pool")
        S.dma(wo[:], A["w_o"], writes=[bw], eng="pool")
        S.dma(gpo[:], A["g_post_mix"][0:1, :].to_broadcast([128, D]), writes=[bw])
        R = 4
        xt = [sb(f"xtm{i}", [128, D], F32) for i in range(R)]
        s0 = [sb(f"s0{i}", [128, 512], F32) for i in range(2)]
        m0 = [sb(f"m0{i}", [128, 512], F32) for i in range(2)]
        mgb = [sb(f"mgb{i}", [128, D], BF16) for i in range(R)]
        mgT = [sb(f"mgT{i}", [128, 8, 128], BF16) for i in range(R)]
        tmp = [sb(f"tmpm{i}", [128, D], F32) for i in range(R)]
        ssm = sb("ssm", [128, 16], F32)
        bxt, bmgb, bmgT, btmp, bssm = [[Buf() for _ in range(R)] for _ in range(5)]
        bs0 = [Buf() for _ in range(2)]
        bm0 = [Buf() for _ in range(2)]
        nb5c = [0]

        def nb5():
            nb5c[0] = (nb5c[0] + 1) % 5
            return nb5c[0]

        def partA(i):
            k = i % R
            ts = slice(i * 128, (i + 1) * 128)
            S.dma(xt[k][:], A["x"][r0 + 128 * i: r0 + 128 * (i + 1), :], writes=[bxt[k]])
            for hf in range(2):
                cs = slice(hf * 512, (hf + 1) * 512)
                for br, (wb_, osrc, bo) in enumerate(((wbn, onsa, bonsa), (wbm, omla, bomla))):
                    ig = nb5()
                    for kc in range(8):
                        S.op("pe", lambda e: e.matmul(PS[ig][:, :], lhsT=hT[:, kc, ts], rhs=wmg[:, kc, br * 1024 + hf * 512: br * 1024 + (hf + 1) * 512], start=(kc == 0), stop=(kc == 7)), reads=[bhT[i // 4], bw], writes=[PSb[ig]])
                    ibr = nb5()
                    for c in range(4):
                        S.op("pe", lambda e: e.matmul(PS[ibr][:, :], lhsT=osrc[:, c, ts], rhs=wb_[:, c, cs], start=(c == 0), stop=(c == 3)), reads=[bo, bw], writes=[PSb[ibr]])
                    S.op("act", lambda e: e.activation(out=s0[br][:], in_=PS[ig][:, :], func=ACT.Sigmoid), reads=[PSb[ig]], writes=[bs0[br]])
                    S.op("dve", lambda e: e.tensor_tensor(out=m0[br][:], in0=PS[ibr][:, :], in1=s0[br][:], op=ALU.mult), reads=[PSb[ibr], bs0[br]], writes=[bm0[br]])
                S.op("pool", lambda e: e.tensor_tensor(out=mgb[k][:, cs], in0=m0[0][:], in1=m0[1][:], op=ALU.add), reads=[bm0[0], bm0[1]], writes=[bmgb[k]])

        iy = [5, 6]

        def partB1(i):
            k = i % R
            for kc in range(8):
                S.op("pe", lambda e: e.transpose(out=PB[:, kc * 128:(kc + 1) * 128], in_=mgb[k][:, kc * 128:(kc + 1) * 128], identity=ident[:]), reads=[bmgb[k], bid], writes=[PBb])
            S.op("act", lambda e: e.copy(out=mgT[k][:], in_=PB[:, :].rearrange("p (k t) -> p k t", k=8)), reads=[PBb], writes=[bmgT[k]])
            for hf in range(2):
                for kc in range(8):
                    S.op("pe", lambda e: e.matmul(PS[iy[hf]][:, :], lhsT=mgT[k][:, kc, :], rhs=wo[:, kc, hf * 512:(hf + 1) * 512], start=(kc == 0), stop=(kc == 7)), reads=[bmgT[k], bw], writes=[PSb[iy[hf]]])
                S.op("act", lambda e: e.activation(out=junk[:, 0:512], in_=PS[iy[hf]][:, :], func=ACT.Square, accum_out=ssm[:, 4 * k + hf:4 * k + hf + 1]), reads=[PSb[iy[hf]]], writes=[bjunk, bssm[k]])

        def partB2(i):
            k = i % R
            S.op("dve", lambda e: e.tensor_tensor(out=ssm[:, 4 * k + 2:4 * k + 3], in0=ssm[:, 4 * k:4 * k + 1], in1=ssm[:, 4 * k + 1:4 * k + 2], op=ALU.add), reads=[bssm[k]], writes=[bssm[k]])
            rstd_from_ss(S, ssm[:, 4 * k + 2:4 * k + 3], ssm[:, 4 * k + 3:4 * k + 4], D, [bssm[k]], [bssm[k]])
            for hf in range(2):
                S.op("dve", lambda e: e.scalar_tensor_tensor(out=tmp[k][:, hf * 512:(hf + 1) * 512], in0=PS[iy[hf]][:, :], scalar=ssm[:, 4 * k + 3:4 * k + 4], in1=gpo[:, hf * 512:(hf + 1) * 512], op0=ALU.mult, op1=ALU.mult), reads=[PSb[iy[hf]], bssm[k], bw], writes=[btmp[k]])
            S.op("pool", lambda e: e.tensor_tensor(out=tmp[k][:], in0=tmp[k][:], in1=xt[k][:], op=ALU.add), reads=[bxt[k]], writes=[btmp[k]])
            S.dma(A["x1s"][r0 + 128 * i: r0 + 128 * (i + 1), :], tmp[k][:], reads=[btmp[k]], writes=[x1b[16 * sq + i]], eng="pool")

        for t in range(16 + 2):
            if 0 <= t - 2 < 16:
                partB2(t - 2)
            if 0 <= t - 1 < 16:
                partB1(t - 1)
            if t < 16:
                partA(t)


def kernel(**inp):
    inp = {k: np.asarray(v) for k, v in inp.items()}
    nc = build()
    consts = make_consts()
    w = host_weights(inp)
    in_maps = []
    for c in range(NCORES):
        m = {"x": np.ascontiguousarray(inp["x"][2 * c:2 * c + 2].reshape(2 * T, D)),
             "p": np.ascontiguousarray(inp["p"][0, 2 * c:2 * c + 2].reshape(2 * T, 256)),
             "pos": np.ascontiguousarray(inp["positions"][2 * c:2 * c + 2].astype(np.int32))}
        m.update(w)
        for k, v in consts.items():
            m["c_" + k] = v
        in_maps.append(m)
    res = run_bass_kernel_spmd(nc, in_maps, core_ids=list(range(NCORES)))
    out = np.stack([r["out"].reshape(2, T, D) for r in res.results], 0).reshape(16, T, D)
    return out.astype(np.float32)
```

```python
import os
import numpy as np
from contextlib import ExitStack
import concourse.bass as bass
import concourse.mybir as mybir
from concourse.bass_utils import run_bass_kernel_spmd

ACT = mybir.ActivationFunctionType
ALU = mybir.AluOpType
AX = mybir.AxisListType
F32 = mybir.dt.float32
BF16 = mybir.dt.bfloat16
I32 = mybir.dt.int32

NEG = -30000.0
BIG = 1.0e9
EPS = 1e-6
T = 2048
D = 1024
NCORES = 8
DFF = 2816
SC_N = 0.125
SC_M = 96.0 ** -0.5
TWO_PI = 6.283185307179586
PI = 3.141592653589793


_UNIQ = [0]


def _sbt(nc, name, shape, dt):
    _UNIQ[0] += 1
    return nc.sbuf_tensor(f"{name}_u{_UNIQ[0]}", shape, dt)


class Buf:
    __slots__ = ("w", "r", "x")

    def __init__(self, x=False):
        self.w = None
        self.r = {}
        self.x = x


class Sched:
    NDS = 24

    def __init__(self, nc, stack):
        self.nc = nc
        self.engs = {"pe": nc.tensor, "act": nc.scalar, "dve": nc.vector, "pool": nc.gpsimd, "sp": nc.sync}
        self.sem = {k: stack.enter_context(nc.semaphore(k + "_s")) for k in self.engs}
        self.cnt = {k: 0 for k in self.engs}
        self.seen = {k: {} for k in self.engs}
        self.dsem = [stack.enter_context(nc.semaphore(f"dq{i}")) for i in range(self.NDS)]
        self.dcnt = [0] * self.NDS
        self.dnext = 0

    def _semof(self, k):
        return self.dsem[k] if isinstance(k, int) else self.sem[k]

    def _waits(self, eng, reads, writes):
        deps = {}
        for b in reads:
            if b.w is not None and deps.get(b.w[0], 0) < b.w[1]:
                deps[b.w[0]] = b.w[1]
            if b.x:
                for k, v in b.r.items():
                    if k != eng and deps.get(k, 0) < v:
                        deps[k] = v
        for b in writes:
            if b.w is not None and deps.get(b.w[0], 0) < b.w[1]:
                deps[b.w[0]] = b.w[1]
            for k, v in b.r.items():
                if deps.get(k, 0) < v:
                    deps[k] = v
        e = self.engs[eng]
        for k, v in deps.items():
            if k == eng and eng in ("pe", "sp"):
                continue
            if self.seen[eng].get(k, 0) >= v:
                continue
            e.wait_ge(self._semof(k), v)
            self.seen[eng][k] = v

    def _mark(self, key, val, reads, writes):
        for b in writes:
            b.w = (key, val)
            b.r = {}
        for b in reads:
            if b not in writes:
                b.r[key] = val

    def op(self, eng, fn, reads=(), writes=()):
        self._waits(eng, reads, writes)
        ins = fn(self.engs[eng])
        self.cnt[eng] += 1
        ins.then_inc(self.sem[eng], 1)
        self._mark(eng, self.cnt[eng], reads, writes)

    def dma(self, out, in_, reads=(), writes=(), eng="sp", **kw):
        i = self.dnext
        self.dnext = (self.dnext + 1) % self.NDS
        e = self.engs[eng]
        if self.dcnt[i] > 0 and self.seen[eng].get(i, 0) < self.dcnt[i]:
            e.wait_ge(self.dsem[i], self.dcnt[i])
            self.seen[eng][i] = self.dcnt[i]
        self._waits(eng, reads, writes)
        ins = e.dma_start(out=out, in_=in_, **kw)
        self.dcnt[i] += 16
        ins.then_inc(self.dsem[i], 16)
        self._mark(i, self.dcnt[i], reads, writes)

    def barrier(self):
        for en, e in self.engs.items():
            for k in self.engs:
                if k != en and self.cnt[k] > self.seen[en].get(k, 0):
                    e.wait_ge(self.sem[k], self.cnt[k])
                    self.seen[en][k] = self.cnt[k]
            for i in range(self.NDS):
                if self.dcnt[i] > self.seen[en].get(i, 0):
                    e.wait_ge(self.dsem[i], self.dcnt[i])
                    self.seen[en][i] = self.dcnt[i]


def make_consts():
    c = {}
    c["ident"] = np.eye(128, dtype=np.float32)
    ds = np.arange(128)[:, None]
    dt = np.arange(128)[None, :]
    negc = np.where(ds <= dt, 0.0, NEG).astype(np.float32)
    negw = np.where(ds > dt, 0.0, NEG).astype(np.float32)
    c["negc"] = np.tile(negc, (1, 4))
    c["negw"] = np.tile(negw, (1, 4))
    dt5 = np.arange(512)[None, :]
    c["negm"] = np.stack([np.where(ds + 128 * m <= dt5, 0.0, NEG) for m in range(4)], 0).astype(np.float32)
    s = np.arange(T)
    E = (s[None, :] // 64 == np.arange(32)[:, None]).astype(np.float32)
    kal = np.stack([np.ones(T), np.ones(T), s // 128, s % 128], 0).astype(np.float32)
    c["kaug_sel"] = np.concatenate([E, kal], 0)
    c["kaug_win"] = np.concatenate([np.zeros_like(E), kal], 0)
    pj = 16 * np.arange(128) + 31
    kc = np.stack([np.ones(128), np.ones(128), pj // 128, pj % 128], 0).astype(np.float32)
    kc[:, 127] = 0.0
    c["kaug_cmp"] = np.concatenate([np.zeros((32, 128), np.float32), kc], 0)
    slopes = np.array([2.0 ** (-(h + 1)) for h in range(8)], dtype=np.float64)
    qa = np.zeros((2, 36, 16, 4, 128), np.float32)
    for g in range(2):
        for r in range(4):
            sl = slopes[4 * g + r]
            for i in range(16):
                qa[g, 32, i, r, :] = -sl * 128.0 * i
                qa[g, 33, i, r, :] = -sl * np.arange(128)
                qa[g, 34, i, r, :] = sl * 128.0
                qa[g, 35, i, r, :] = sl
    c["qaug"] = qa.reshape(2, 36, 16 * 4 * 128)
    selg = np.zeros((24, 24, 64), np.float32)
    for r in range(24):
        selg[r, r, :] = 1.0
    c["selg"] = selg.reshape(24, 24 * 64)
    cc = np.arange(128)[None, :] - 120
    dtt = np.arange(128)[:, None]
    c["cmptbl"] = np.where(16 * cc + 31 <= dtt, 0.0, NEG).astype(np.float32)
    v12 = np.zeros((128, 2), np.float32)
    v12[:, 0] = np.where(np.arange(128) < 64, -BIG, BIG)
    v12[:, 1] = np.where(np.arange(128) < 64, BIG, 0.0)
    c["v12"] = v12
    invf = (np.float32(10000.0) ** (-np.arange(0, 32, 2, dtype=np.float32) / np.float32(32))).astype(np.float32)
    c["invf"] = np.concatenate([invf, invf])[:, None].astype(np.float32)
    return c


WSHAPES = {
    "g_pre_mix": [1, D], "g_post_mix": [1, D], "g_pre_ffn": [1, D], "g_post_ffn": [1, D], "g_ple": [1, D],
    "w_in": [128, 8, 3768], "posT_k": [64, 32], "posT_v": [64, 32],
    "ck_w1": [64, 32, 64], "cv_w1": [64, 32, 64], "ck_b1": [64, 1], "cv_b1": [64, 1],
    "ck_w2": [64, 64], "cv_w2": [64, 64],
    "g_q": [128, 2], "w_uq": [128, 2, 768], "g_kv": [128, 1], "w_ukv": [128, 1024],
    "w_br_nsa": [128, 4, D], "w_br_mla": [128, 4, D], "w_o": [128, 8, D],
    "w_up": [128, 8, 2 * DFF], "w_conv": [128, 3, 44], "b_conv": [128, 44], "w_down": [128, 22, D],
    "w_ple": [128, 2, D], "w_ple_gate": [128, 8, D],
}


def host_weights(inp):
    w = {}
    f = lambda a: np.ascontiguousarray(a, dtype=np.float32)
    for k in ("g_pre_mix", "g_post_mix", "g_pre_ffn", "g_post_ffn", "g_ple"):
        w[k] = f(inp[k][0][None, :])
    kcp = lambda a, p=128: f(a.reshape(a.shape[0] // p, p, a.shape[1]).transpose(1, 0, 2))
    w["w_in"] = kcp(inp["w_in"][0])
    w["posT_k"] = f(inp["nsa_pos_k"][0].T)
    w["posT_v"] = f(inp["nsa_pos_v"][0].T)
    w["ck_w1"] = f(inp["nsa_ck_w1"][0].reshape(32, 64, 64).transpose(1, 0, 2))
    w["cv_w1"] = f(inp["nsa_cv_w1"][0].reshape(32, 64, 64).transpose(1, 0, 2))
    w["ck_b1"] = f(inp["nsa_ck_b1"][0][:, None])
    w["cv_b1"] = f(inp["nsa_cv_b1"][0][:, None])
    w["ck_w2"] = f(inp["nsa_ck_w2"][0])
    w["cv_w2"] = f(inp["nsa_cv_w2"][0])
    w["g_q"] = f(inp["mla_g_q"][0].reshape(2, 128).T)
    w["w_uq"] = kcp(inp["mla_w_uq"][0])
    w["g_kv"] = f(inp["mla_g_kv"][0][:, None])
    w["w_ukv"] = f(inp["mla_w_ukv"][0])
    w["w_br_nsa"] = kcp(inp["w_br_nsa"][0])
    w["w_br_mla"] = kcp(inp["w_br_mla"][0])
    w["w_o"] = kcp(inp["w_o"][0])
    w["w_up"] = kcp(inp["w_up"][0])
    w["w_conv"] = f(inp["w_conv"][0].reshape(3, 44, 128).transpose(2, 0, 1))
    w["b_conv"] = f(inp["b_conv"][0].reshape(44, 128).T)
    w["w_down"] = kcp(inp["w_down"][0])
    w["w_ple"] = kcp(inp["w_ple"][0])
    w["w_ple_gate"] = kcp(inp["w_ple_gate"][0])
    return w


def build(stages=("mix", "ffn", "ple"), dbg=False, nseq=2):
    nc = bass.Bass("TRN2", target_bir_lowering=False)
    consts = make_consts()
    A = {}
    A["x"] = nc.dram_tensor("x", [2 * T, D], F32, kind="ExternalInput").ap()
    A["p"] = nc.dram_tensor("p", [2 * T, 256], F32, kind="ExternalInput").ap()
    A["pos"] = nc.dram_tensor("pos", [2, T], I32, kind="ExternalInput").ap()
    for k, shp in WSHAPES.items():
        A[k] = nc.dram_tensor(k, shp, F32, kind="ExternalInput").ap()
    for k, v in consts.items():
        A[k] = nc.dram_tensor("c_" + k, list(v.shape), F32, kind="ExternalInput").ap()
    A["out"] = nc.dram_tensor("out", [2 * T, D], F32, kind="ExternalOutput").ap()
    if "mix" in stages:
        A["x1s"] = nc.dram_tensor("x1s", [2 * T, D], F32, kind="ExternalOutput" if dbg else "Internal").ap()
    else:
        A["x1s"] = nc.dram_tensor("x1s", [2 * T, D], F32, kind="ExternalInput").ap()
    if dbg:
        A["d_onsa"] = nc.dram_tensor("d_onsa", [128, 4, T], F32, kind="ExternalOutput").ap()
        A["d_omla"] = nc.dram_tensor("d_omla", [128, 4, T], F32, kind="ExternalOutput").ap()
    x1b = [Buf() for _ in range(32)]
    A["gsd"] = nc.dram_tensor("gsd", [24, T], BF16).ap()
    A["gsd_buf"] = Buf()

    with ExitStack() as st:
        S = Sched(nc, st)
        sbg = lambda name, shape, dt: st.enter_context(_sbt(nc, name, shape, dt))
        PS = [st.enter_context(nc.psum_tensor(f"ps{i}", [128, 512], F32)) for i in range(7)]
        PSb = [Buf(True) for _ in range(7)]
        PB = st.enter_context(nc.psum_tensor("pb", [128, 1024], BF16))
        PBb = Buf(True)
        ident = sbg("ident", [128, 128], BF16)
        bid = Buf()
        S.dma(ident[:], A["ident"], writes=[bid], eng="pool")

        if "mix" in stages:
            for sq in range(nseq):
                stage_mix(nc, S, A, sq, PS, PSb, PB, PBb, ident, bid, x1b, dbg and sq == 0)
                S.barrier()
        if "ffn" in stages:
            stage_ffn(nc, S, A, PS, PSb, PB, PBb, ident, bid, x1b, nseq)
            S.barrier()
        if "ple" in stages:
            stage_ple(nc, S, A, PS, PSb, PB, PBb, ident, bid, x1b, nseq)
        S.barrier()
        print('COUNTS', S.cnt, max(S.dcnt))
    return nc


def rstd_from_ss(S, ss_ap, rstd_ap, n, rb, wb):
    S.op("act", lambda e: e.activation(out=rstd_ap, in_=ss_ap, func=ACT.Sqrt, scale=1.0 / n, bias=EPS), reads=rb, writes=wb)
    S.op("dve", lambda e: e.reciprocal(out=rstd_ap, in_=rstd_ap), reads=wb, writes=wb)


def stage_ffn(nc, S, A, PS, PSb, PB, PBb, ident, bid, x1b, nseq):
    with ExitStack() as s2:
        sb = lambda name, shape, dt: s2.enter_context(_sbt(nc, name, shape, dt))
        wup = sb("wup", [128, 8, 2 * DFF], BF16)
        bwup = [Buf() for _ in range(11)]
        wdn = sb("wdn", [128, 22, D], BF16)
        bwdn = [Buf() for _ in range(2)]
        wc = sb("wc", [128, 3, 44], F32)
        bc = sb("bc", [128, 44], F32)
        bwc = Buf()
        gpre = sb("gpre", [128, D], F32)
        gpost = sb("gpost", [128, D], F32)
        bg = Buf()
        S.dma(wc[:], A["w_conv"], writes=[bwc])
        S.dma(bc[:], A["b_conv"], writes=[bwc])
        S.dma(gpre[:], A["g_pre_ffn"][0:1, :].to_broadcast([128, D]), writes=[bg])
        S.dma(gpost[:], A["g_post_ffn"][0:1, :].to_broadcast([128, D]), writes=[bg])
        order = []
        for q in range(6):
            order.append(q)
            if q + 6 < 11:
                pass
        for q in range(11):
            S.dma(wup[:, :, q * 512:(q + 1) * 512], A["w_up"][:, :, q * 512:(q + 1) * 512], writes=[bwup[q]], eng="pool")
        for q in range(2):
            S.dma(wdn[:, q * 11:(q + 1) * 11, :], A["w_down"][:, q * 11:(q + 1) * 11, :], writes=[bwdn[q]], eng="pool")
        x1t = [sb(f"x1t{i}", [128, D], F32) for i in range(4)]
        bx1t = [Buf() for _ in range(4)]
        hn = [sb(f"hn{i}", [128, D], BF16) for i in range(2)]
        bhn = [Buf() for _ in range(2)]
        h2T = [sb(f"h2T{i}", [128, 8, 258], BF16) for i in range(2)]
        bh2T = [Buf() for _ in range(2)]
        junk = sb("junk", [128, D], BF16)
        bjunk = Buf()
        ssq = sb("ssq", [128, 8], F32)
        bss = [Buf() for _ in range(4)]
        cA = [sb(f"cA{i}", [128, 256], F32) for i in range(2)]
        cV = [sb(f"cV{i}", [128, 256], F32) for i in range(2)]
        gA = [sb(f"gA{i}", [128, 256], F32) for i in range(2)]
        bcA = [Buf() for _ in range(2)]
        bcV = [Buf() for _ in range(2)]
        bgA = [Buf() for _ in range(2)]
        actT = [sb(f"actT{i}", [128, 22, 256], BF16) for i in range(2)]
        bact = [[Buf() for _ in range(22)] for _ in range(2)]
        tmp = [sb(f"tmp{i}", [128, D], F32) for i in range(2)]
        btmp = [Buf() for _ in range(2)]
        ss2 = sb("ss2", [128, 8], F32)
        bss2 = [Buf() for _ in range(2)]
        uprot = [4, 5, 6]
        upi = 0
        nblk = 8 * nseq
        def front(b):
            r0 = 256 * b
            hT = h2T[b % 2]
            bh = bh2T[b % 2]
            for tt in range(2):
                xi = (2 * b + tt) % 4
                gi = 2 * b + tt
                S.dma(x1t[xi][:], A["x1s"][r0 + 128 * tt: r0 + 128 * (tt + 1), :], reads=[x1b[gi]], writes=[bx1t[xi]])
                S.op("act", lambda e: e.activation(out=junk[:], in_=x1t[xi][:], func=ACT.Square, accum_out=ssq[:, xi:xi + 1]), reads=[bx1t[xi]], writes=[bjunk, bss[xi]])
                rstd_from_ss(S, ssq[:, xi:xi + 1], ssq[:, 4 + xi:5 + xi], D, [bss[xi]], [bss[xi]])
                S.op("dve", lambda e: e.scalar_tensor_tensor(out=hn[tt][:], in0=x1t[xi][:], scalar=ssq[:, 4 + xi:5 + xi], in1=gpre[:], op0=ALU.mult, op1=ALU.mult), reads=[bx1t[xi], bss[xi], bg], writes=[bhn[tt]])
                for kc in range(8):
                    S.op("pe", lambda e: e.transpose(out=PB[:, kc * 128:(kc + 1) * 128], in_=hn[tt][:, kc * 128:(kc + 1) * 128], identity=ident[:]), reads=[bhn[tt], bid], writes=[PBb])
                S.op("act", lambda e: e.copy(out=hT[:, :, 2 + 128 * tt: 2 + 128 * (tt + 1)], in_=PB[:, :].rearrange("p (k t) -> p k t", k=8)), reads=[PBb], writes=[bh])
            if b % 8 == 0:
                S.op("pool", lambda e: e.memset(hT[:, :, 0:2], 0.0), writes=[bh])
            else:
                S.op("pool", lambda e: e.tensor_copy(out=hT[:, :, 0:2], in_=h2T[(b - 1) % 2][:, :, 256:258]), reads=[bh2T[(b - 1) % 2]], writes=[bh])

        def up(b, j):
            nonlocal upi
            hT = h2T[b % 2]
            bh = bh2T[b % 2]
            ia = uprot[upi % 3]
            upi += 1
            iv = uprot[upi % 3]
            upi += 1
            ca, cv, ga = cA[j % 2], cV[j % 2], gA[j % 2]
            bca, bcv, bga = bcA[j % 2], bcV[j % 2], bgA[j % 2]
            for (ib, col0) in ((ia, j * 128), (iv, DFF + j * 128)):
                for kc in range(8):
                    S.op("pe", lambda e: e.matmul(PS[ib][:, 0:258], lhsT=wup[:, kc, col0:col0 + 128], rhs=hT[:, kc, :], start=(kc == 0), stop=(kc == 7)),
                         reads=[bwup[col0 // 512], bh], writes=[PSb[ib]])
            for (ib, cc, bcc, jj) in ((ia, ca, bca, j), (iv, cv, bcv, 22 + j)):
                S.op("act", lambda e: e.activation(out=cc[:], in_=PS[ib][:, 2:258], func=ACT.Identity, scale=wc[:, 2, jj:jj + 1], bias=bc[:, jj:jj + 1]), reads=[PSb[ib], bwc], writes=[bcc])
                S.op("dve", lambda e: e.scalar_tensor_tensor(out=cc[:], in0=PS[ib][:, 1:257], scalar=wc[:, 1, jj:jj + 1], in1=cc[:], op0=ALU.mult, op1=ALU.add), reads=[PSb[ib], bwc], writes=[bcc])
                S.op("dve", lambda e: e.scalar_tensor_tensor(out=cc[:], in0=PS[ib][:, 0:256], scalar=wc[:, 0, jj:jj + 1], in1=cc[:], op0=ALU.mult, op1=ALU.add), reads=[PSb[ib], bwc], writes=[bcc])
            S.op("act", lambda e: e.activation(out=ga[:], in_=ca[:], func=ACT.Gelu_apprx_tanh), reads=[bca], writes=[bga])
            S.op("pool", lambda e: e.tensor_tensor(out=actT[b % 2][:, j, :], in0=ga[:], in1=cv[:], op=ALU.mult), reads=[bga, bcv], writes=[bact[b % 2][j]])

        def down(b):
            r0 = 256 * b
            for tt in range(2):
                xi = (2 * b + tt) % 4
                gi = 2 * b + tt
                for j in range(22):
                    for hf in range(2):
                        S.op("pe", lambda e: e.matmul(PS[2 * tt + hf][:, :], lhsT=actT[b % 2][:, j, 128 * tt:128 * (tt + 1)], rhs=wdn[:, j, hf * 512:(hf + 1) * 512], start=(j == 0), stop=(j == 21)),
                             reads=[bact[b % 2][j], bwdn[j // 11]], writes=[PSb[2 * tt + hf]])
                for hf in range(2):
                    S.op("act", lambda e: e.activation(out=junk[:, 0:512], in_=PS[2 * tt + hf][:, :], func=ACT.Square, accum_out=ss2[:, 4 * tt + hf:4 * tt + hf + 1]), reads=[PSb[2 * tt + hf]], writes=[bjunk, bss2[tt]])
                S.op("dve", lambda e: e.tensor_tensor(out=ss2[:, 4 * tt + 2:4 * tt + 3], in0=ss2[:, 4 * tt:4 * tt + 1], in1=ss2[:, 4 * tt + 1:4 * tt + 2], op=ALU.add), reads=[bss2[tt]], writes=[bss2[tt]])
                rstd_from_ss(S, ss2[:, 4 * tt + 2:4 * tt + 3], ss2[:, 4 * tt + 3:4 * tt + 4], D, [bss2[tt]], [bss2[tt]])
                for hf in range(2):
                    S.op("dve", lambda e: e.scalar_tensor_tensor(out=tmp[tt][:, hf * 512:(hf + 1) * 512], in0=PS[2 * tt + hf][:, :], scalar=ss2[:, 4 * tt + 3:4 * tt + 4], in1=gpost[:, hf * 512:(hf + 1) * 512], op0=ALU.mult, op1=ALU.mult),
                         reads=[PSb[2 * tt + hf], bss2[tt], bg], writes=[btmp[tt]])
                S.op("pool", lambda e: e.tensor_tensor(out=tmp[tt][:], in0=tmp[tt][:], in1=x1t[xi][:], op=ALU.add), reads=[bx1t[xi]], writes=[btmp[tt]])
                S.dma(A["x1s"][r0 + 128 * tt: r0 + 128 * (tt + 1), :], tmp[tt][:], reads=[btmp[tt]], writes=[x1b[gi]], eng="pool")

        front(0)
        for b in range(nblk):
            for j in range(22):
                up(b, j)
                if j == 12 and b + 1 < nblk:
                    front(b + 1)
            down(b)


def stage_ple(nc, S, A, PS, PSb, PB, PBb, ident, bid, x1b, nseq):
    with ExitStack() as s2:
        sb = lambda name, shape, dt: s2.enter_context(_sbt(nc, name, shape, dt))
        wpg = sb("wpg", [128, 8, D], BF16)
        wpl = sb("wpl", [128, 2, D], BF16)
        bw = Buf()
        gple = sb("gple", [128, D], F32)
        S.dma(wpg[:], A["w_ple_gate"], writes=[bw], eng="pool")
        S.dma(wpl[:], A["w_ple"], writes=[bw], eng="pool")
        S.dma(gple[:], A["g_ple"][0:1, :].to_broadcast([128, D]), writes=[bw])
        R = 6
        x2t = [sb(f"x2t{i}", [128, D], F32) for i in range(R)]
        pt = [sb(f"pt{i}", [128, 256], F32) for i in range(R)]
        x2b = [sb(f"x2b{i}", [128, D], BF16) for i in range(R)]
        pbf = [sb(f"pbf{i}", [128, 256], BF16) for i in range(R)]
        x2T = [sb(f"x2T{i}", [128, 8, 128], BF16) for i in range(R)]
        pT = [sb(f"pT{i}", [128, 2, 128], BF16) for i in range(R)]
        sg = [sb(f"sg{i}", [128, D], F32) for i in range(R)]
        eg = [sb(f"eg{i}", [128, D], F32) for i in range(R)]
        junk = sb("junkp", [128, D], BF16)
        ssp = sb("ssp", [128, 2 * R], F32)
        bx2t, bpt, bx2b, bpbf, bx2T, bpT, bsg, beg, bssp = [[Buf() for _ in range(R)] for _ in range(9)]
        bjunk = Buf()
        ntile = 16 * nseq

        def load(i):
            k = i % R
            rows = slice(128 * i, 128 * (i + 1))
            S.dma(x2t[k][:], A["x1s"][rows, :], reads=[x1b[i]], writes=[bx2t[k]])
            S.dma(pt[k][:], A["p"][rows, :], writes=[bpt[k]])

        def front(i):
            k = i % R
            S.op("dve", lambda e: e.tensor_copy(out=x2b[k][:], in_=x2t[k][:]), reads=[bx2t[k]], writes=[bx2b[k]])
            S.op("pool", lambda e: e.tensor_copy(out=pbf[k][:], in_=pt[k][:]), reads=[bpt[k]], writes=[bpbf[k]])
            for kc in range(8):
                S.op("pe", lambda e: e.transpose(out=PB[:, kc * 128:(kc + 1) * 128], in_=x2b[k][:, kc * 128:(kc + 1) * 128], identity=ident[:]), reads=[bx2b[k], bid], writes=[PBb])
            S.op("act", lambda e: e.copy(out=x2T[k][:], in_=PB[:, :].rearrange("p (k t) -> p k t", k=8)), reads=[PBb], writes=[bx2T[k]])
            for kc in range(2):
                S.op("pe", lambda e: e.transpose(out=PB[:, kc * 128:(kc + 1) * 128], in_=pbf[k][:, kc * 128:(kc + 1) * 128], identity=ident[:]), reads=[bpbf[k], bid], writes=[PBb])
            S.op("act", lambda e: e.copy(out=pT[k][:], in_=PB[:, 0:256].rearrange("p (k t) -> p k t", k=2)), reads=[PBb], writes=[bpT[k]])

        def back(i):
            k = i % R
            rows = slice(128 * i, 128 * (i + 1))
            for hf in range(2):
                st_ = (2 * i + hf) % 3
                ig, ie = 2 * st_, 2 * st_ + 1
                cs = slice(hf * 512, (hf + 1) * 512)
                for kc in range(8):
                    S.op("pe", lambda e: e.matmul(PS[ig][:, :], lhsT=x2T[k][:, kc, :], rhs=wpg[:, kc, cs], start=(kc == 0), stop=(kc == 7)), reads=[bx2T[k], bw], writes=[PSb[ig]])
                for kc in range(2):
                    S.op("pe", lambda e: e.matmul(PS[ie][:, :], lhsT=pT[k][:, kc, :], rhs=wpl[:, kc, cs], start=(kc == 0), stop=(kc == 1)), reads=[bpT[k], bw], writes=[PSb[ie]])
                S.op("act", lambda e: e.activation(out=sg[k][:, cs], in_=PS[ig][:, :], func=ACT.Sigmoid), reads=[PSb[ig]], writes=[bsg[k]])
                S.op("dve", lambda e: e.tensor_tensor(out=eg[k][:, cs], in0=PS[ie][:, :], in1=sg[k][:, cs], op=ALU.mult), reads=[PSb[ie], bsg[k]], writes=[beg[k]])
            S.op("act", lambda e: e.activation(out=junk[:], in_=eg[k][:], func=ACT.Square, accum_out=ssp[:, 2 * k:2 * k + 1]), reads=[beg[k]], writes=[bjunk, bssp[k]])

        def back3(i):
            k = i % R
            rows = slice(128 * i, 128 * (i + 1))
            rstd_from_ss(S, ssp[:, 2 * k:2 * k + 1], ssp[:, 2 * k + 1:2 * k + 2], D, [bssp[k]], [bssp[k]])
            S.op("dve", lambda e: e.scalar_tensor_tensor(out=eg[k][:], in0=eg[k][:], scalar=ssp[:, 2 * k + 1:2 * k + 2], in1=gple[:], op0=ALU.mult, op1=ALU.mult), reads=[bssp[k], bw], writes=[beg[k]])
            S.op("pool", lambda e: e.tensor_tensor(out=eg[k][:], in0=eg[k][:], in1=x2t[k][:], op=ALU.add), reads=[bx2t[k]], writes=[beg[k]])
            S.dma(A["out"][rows, :], eg[k][:], reads=[beg[k]], eng="pool")

        for i in range(4):
            load(i)
        front(0)
        front(1)
        for i in range(ntile + 1):
            if i >= 1:
                back3(i - 1)
            if i < ntile:
                back(i)
            if i + 4 < ntile:
                load(i + 4)
            if i + 2 < ntile:
                front(i + 2)


def attn_stream(S, pairs, PS, PSb, PT, bPT, srot=(0, 1, 2), skew=2):
    n = len(pairs)
    NP = len(PT)

    def emitS(k):
        pr = pairs[k]
        ib = srot[k % len(srot)]
        m = len(pr["smm"])
        for q, (l, r, rd, lo, hi) in enumerate(pr["smm"]):
            S.op("pe", lambda e: e.matmul(PS[ib][:, lo:hi], lhsT=l, rhs=r, start=(q == 0), stop=(q == m - 1), skip_group_check=True), reads=rd, writes=[PSb[ib]])

    for k in range(min(skew, n)):
        emitS(k)
    for k in range(n):
        if k + skew < n:
            emitS(k + skew)
        pr = pairs[k]
        ib = srot[k % len(srot)]
        lo, hi = pr.get("cols", (0, 512))
        S.op("act", lambda e: e.activation(out=PT[k % NP][:, lo:hi], in_=PS[ib][:, lo:hi], func=ACT.Exp), reads=[PSb[ib]], writes=[bPT[k % NP]])
        ob, first, last = pr["O"]
        vl, vrd = pr["v"]
        S.op("pe", lambda e: e.matmul(PS[ob][:, lo:hi], lhsT=vl, rhs=PT[k % NP][:, lo:hi], start=first, stop=last, skip_group_check=True), reads=[bPT[k % NP]] + vrd, writes=[PSb[ob]])
        if pr.get("fin") is not None:
            pr["fin"]()


def stage_mix(nc, S, A, sq, PS, PSb, PB, PBb, ident, bid, x1b, dbg):
    r0 = sq * T
    with ExitStack() as s1:
        sb = lambda name, shape, dt: s1.enter_context(_sbt(nc, name, shape, dt))
        nbc = [0]

        def nb():
            nbc[0] = (nbc[0] + 1) % 7
            return nbc[0]

        cst = Buf()
        negc = sb("negc", [128, 512], BF16)
        negw = sb("negw", [128, 512], BF16)
        negm = sb("negm", [128, 4, 512], BF16)
        selg = sb("selg", [24, 24 * 64], BF16)
        cmptbl = sb("cmptbl", [128, 128], F32)
        v12 = sb("v12", [128, 2], F32)
        invf = sb("invf", [128, 1], F32)
        ones = sb("ones", [128, 128], BF16)
        S.dma(negc[:], A["negc"], writes=[cst], eng="pool")
        S.dma(negw[:], A["negw"], writes=[cst], eng="pool")
        S.dma(negm[:], A["negm"].rearrange("m p t -> p m t"), writes=[cst], eng="pool")
        S.dma(selg[:], A["selg"], writes=[cst], eng="pool")
        S.dma(cmptbl[:], A["cmptbl"], writes=[cst])
        S.dma(v12[:], A["v12"], writes=[cst])
        S.dma(invf[64:96, :], A["invf"], writes=[cst])
        S.op("pool", lambda e: e.memset(ones[:], 1.0), writes=[cst])
        onesf = sb("onesf", [64, 512], F32)
        S.op("pool", lambda e: e.memset(onesf[:], -1.0), writes=[cst])
        gpm = sb("gpm", [128, D], F32)
        S.dma(gpm[:], A["g_pre_mix"][0:1, :].to_broadcast([128, D]), writes=[cst])

        hT = sb("hT", [128, 8, T], BF16)
        bhT = [Buf() for _ in range(4)]
        onsa = sb("onsa", [128, 4, T], BF16)
        omla = sb("omla", [128, 4, T], BF16)
        bonsa = Buf()
        bomla = Buf()
        junk = sb("junkm", [128, D], BF16)
        bjunk = Buf()
        sst = sb("sst", [128, 8], F32)

        with ExitStack() as s2:
            sb2 = lambda name, shape, dt: s2.enter_context(_sbt(nc, name, shape, dt))
            RA = 4
            xt = [sb2(f"xt{i}", [128, D], F32) for i in range(RA)]
            hn = [sb2(f"hnm{i}", [128, D], BF16) for i in range(RA)]
            ssA = sb2("ssA", [128, 32], F32)
            bxt = [Buf() for _ in range(RA)]
            bhn = [Buf() for _ in range(RA)]
            bs = [Buf() for _ in range(16)]

            def A_load(i):
                k = i % RA
                S.dma(xt[k][:], A["x"][r0 + 128 * i: r0 + 128 * (i + 1), :], writes=[bxt[k]])

            def A_n1(i):
                k = i % RA
                S.op("act", lambda e: e.activation(out=junk[:], in_=xt[k][:], func=ACT.Square, accum_out=ssA[:, i:i + 1]), reads=[bxt[k]], writes=[bjunk, bs[i]])

            def A_n2(i):
                k = i % RA
                rstd_from_ss(S, ssA[:, i:i + 1], ssA[:, 16 + i:17 + i], D, [bs[i]], [bs[i]])
                S.op("dve", lambda e: e.scalar_tensor_tensor(out=hn[k][:], in0=xt[k][:], scalar=ssA[:, 16 + i:17 + i], in1=gpm[:], op0=ALU.mult, op1=ALU.mult), reads=[bxt[k], bs[i], cst], writes=[bhn[k]])

            def A_t(i):
                k = i % RA
                for kc in range(8):
                    S.op("pe", lambda e: e.transpose(out=PB[:, kc * 128:(kc + 1) * 128], in_=hn[k][:, kc * 128:(kc + 1) * 128], identity=ident[:]), reads=[bhn[k], bid], writes=[PBb])
                S.op("act", lambda e: e.copy(out=hT[:, :, i * 128:(i + 1) * 128], in_=PB[:, :].rearrange("p (k t) -> p k t", k=8)), reads=[PBb], writes=[bhT[i // 4]])

            A_load(0)
            A_load(1)
            for t in range(16 + 2):
                if 0 <= t - 2 < 16:
                    A_t(t - 2)
                if 0 <= t - 1 < 16:
                    A_n2(t - 1)
                if t < 16:
                    A_n1(t)
                if t + 2 < 16:
                    A_load(t + 2)

        with ExitStack() as s2:
            sb2 = lambda name, shape, dt: s2.enter_context(_sbt(nc, name, shape, dt))
            s3 = ExitStack()
            sb3 = lambda name, shape, dt: s3.enter_context(_sbt(nc, name, shape, dt))
            cos2 = sb2("cos2", [128, T], F32)
            sin2 = sb2("sin2", [128, T], F32)
            RP = slice(64, 96)
            brope = Buf()
            krope = sb2("krope", [128, T], BF16)
            cqn = sb2("cqn", [128, 2, T], BF16)
            ckvn = sb2("ckvn", [128, T], BF16)
            t1 = sb2("t1", [128, 512], F32)
            t2 = sb2("t2", [128, 512], F32)
            posi = sb3("posi", [128, T], I32)
            ang = sb3("ang", [128, T], F32)
            tmpa = sb3("tmpa", [128, T], F32)
            S.dma(posi[RP, :], A["pos"][sq:sq + 1, :].to_broadcast([32, T]), writes=[brope])
            S.op("dve", lambda e: e.tensor_copy(out=ang[RP, :], in_=posi[RP, :]), reads=[brope], writes=[brope])
            S.op("dve", lambda e: e.tensor_scalar(out=ang[RP, :], in0=ang[RP, :], scalar1=invf[RP, 0:1], scalar2=None, op0=ALU.mult), reads=[cst], writes=[brope])
            qi = sb3("qi", [128, T], I32)
            for (addc, dst) in ((0.0, sin2), (PI / 2, cos2)):
                S.op("dve", lambda e: e.tensor_scalar(out=tmpa[RP, :], in0=ang[RP, :], scalar1=addc, scalar2=1.0 / TWO_PI, op0=ALU.add, op1=ALU.mult), reads=[brope], writes=[brope])
                S.op("dve", lambda e: e.tensor_copy(out=qi[RP, :], in_=tmpa[RP, :]), reads=[brope], writes=[brope])
                S.op("dve", lambda e: e.tensor_copy(out=tmpa[RP, :], in_=qi[RP, :]), reads=[brope], writes=[brope])
                S.op("dve", lambda e: e.scalar_tensor_tensor(out=tmpa[RP, :], in0=tmpa[RP, :], scalar=-TWO_PI, in1=ang[RP, :], op0=ALU.mult, op1=ALU.add), reads=[brope], writes=[brope])
                if addc != 0.0:
                    S.op("dve", lambda e: e.tensor_scalar(out=tmpa[RP, :], in0=tmpa[RP, :], scalar1=addc, scalar2=None, op0=ALU.add), reads=[brope], writes=[brope])
                S.op("dve", lambda e: e.tensor_scalar(out=dst[RP, :], in0=tmpa[RP, :], scalar1=PI, scalar2=-TWO_PI, op0=ALU.is_gt, op1=ALU.mult), reads=[brope], writes=[brope])
                S.op("dve", lambda e: e.tensor_tensor(out=tmpa[RP, :], in0=tmpa[RP, :], in1=dst[RP, :], op=ALU.add), reads=[brope], writes=[brope])
                S.op("dve", lambda e: e.tensor_scalar(out=tmpa[RP, :], in0=tmpa[RP, :], scalar1=-PI, scalar2=PI, op0=ALU.max, op1=ALU.min), reads=[brope], writes=[brope])
                S.op("act", lambda e: e.activation(out=dst[RP, :], in_=tmpa[RP, :], func=ACT.Sin), reads=[brope], writes=[brope])
            wA = sb3("wA", [128, 8, 416], BF16)
            wkrot = sb3("wkrot", [128, 8, 32], BF16)
            bwA = Buf()
            S.dma(wA[:], A["w_in"][:, :, 1304:1720], writes=[bwA], eng="pool")
            S.op("pool", lambda e: e.tensor_scalar(out=wkrot[:, :, 0:16], in0=wA[:, :, 400:416], scalar1=-1.0, scalar2=None, op0=ALU.mult), reads=[bwA], writes=[bwA])
            S.op("pool", lambda e: e.tensor_copy(out=wkrot[:, :, 16:32], in_=wA[:, :, 384:400]), reads=[bwA], writes=[bwA])
            gq = sb3("gq", [128, 2], F32)
            gkv = sb3("gkv", [128, 1], F32)
            S.dma(gq[:], A["g_q"], writes=[bwA])
            S.dma(gkv[:], A["g_kv"], writes=[bwA])
            bcq = Buf()
            cf = [sb3(f"cf{m}", [128, 512], F32) for m in range(3)]
            sqb = [sb3(f"sqb{m}", [128, 512], BF16) for m in range(3)]
            rq = sb3("rq", [128, 512], F32)
            rk = sb3("rk", [128, 512], F32)
            bcf = [Buf() for _ in range(3)]
            bsqb = [Buf() for _ in range(3)]
            brq, brk, bt1, bt2 = Buf(), Buf(), Buf(), Buf()
            for c in range(4):
                cs = slice(c * 512, (c + 1) * 512)
                for m, col0 in enumerate((0, 128, 256)):
                    ib = nb()
                    for kc in range(8):
                        S.op("pe", lambda e: e.matmul(PS[ib][:, :], lhsT=wA[:, kc, col0:col0 + 128], rhs=hT[:, kc, cs], start=(kc == 0), stop=(kc == 7)), reads=[bwA, bhT[c]], writes=[PSb[ib]])
                    S.op("act", lambda e: e.copy(out=cf[m][:], in_=PS[ib][:, :]), reads=[PSb[ib]], writes=[bcf[m]])
                    S.op("act", lambda e: e.activation(out=sqb[m][:], in_=PS[ib][:, :], func=ACT.Square), reads=[PSb[ib]], writes=[bsqb[m]])
                iq = nb()
                S.op("pe", lambda e: e.matmul(PS[iq][:, :], lhsT=ones[:], rhs=sqb[0][:], start=True, stop=False), reads=[cst, bsqb[0]], writes=[PSb[iq]])
                S.op("pe", lambda e: e.matmul(PS[iq][:, :], lhsT=ones[:], rhs=sqb[1][:], start=False, stop=True), reads=[cst, bsqb[1]], writes=[PSb[iq]])
                ik = nb()
                S.op("pe", lambda e: e.matmul(PS[ik][:, :], lhsT=ones[:], rhs=sqb[2][:], start=True, stop=True), reads=[cst, bsqb[2]], writes=[PSb[ik]])
                S.op("act", lambda e: e.activation(out=rq[:], in_=PS[iq][:, :], func=ACT.Sqrt, scale=1.0 / 256, bias=EPS), reads=[PSb[iq]], writes=[brq])
                S.op("dve", lambda e: e.reciprocal(out=rq[:], in_=rq[:]), reads=[brq], writes=[brq])
                S.op("act", lambda e: e.activation(out=rk[:], in_=PS[ik][:, :], func=ACT.Sqrt, scale=1.0 / 128, bias=EPS), reads=[PSb[ik]], writes=[brk])
                S.op("dve", lambda e: e.reciprocal(out=rk[:], in_=rk[:]), reads=[brk], writes=[brk])
                for m in range(2):
                    S.op("dve", lambda e: e.scalar_tensor_tensor(out=cqn[:, m, cs], in0=cf[m][:], scalar=gq[:, m:m + 1], in1=rq[:], op0=ALU.mult, op1=ALU.mult), reads=[bcf[m], brq, bwA], writes=[bcq])
                S.op("dve", lambda e: e.scalar_tensor_tensor(out=ckvn[:, cs], in0=cf[2][:], scalar=gkv[:, 0:1], in1=rk[:], op0=ALU.mult, op1=ALU.mult), reads=[bcf[2], brk, bwA], writes=[bcq])
                i1 = nb()
                i2 = nb()
                for kc in range(8):
                    S.op("pe", lambda e: e.matmul(PS[i1][64:96, :], lhsT=wA[:, kc, 384:416], rhs=hT[:, kc, cs], start=(kc == 0), stop=(kc == 7), tile_position=(0, 64)), reads=[bwA, bhT[c]], writes=[PSb[i1]])
                for kc in range(8):
                    S.op("pe", lambda e: e.matmul(PS[i2][64:96, :], lhsT=wkrot[:, kc, :], rhs=hT[:, kc, cs], start=(kc == 0), stop=(kc == 7), tile_position=(0, 64)), reads=[bwA, bhT[c]], writes=[PSb[i2]])
                S.op("dve", lambda e: e.tensor_tensor(out=t1[RP, :], in0=PS[i1][64:96, :], in1=cos2[RP, cs], op=ALU.mult), reads=[PSb[i1], brope], writes=[bt1])
                S.op("dve", lambda e: e.tensor_tensor(out=t2[RP, :], in0=PS[i2][64:96, :], in1=sin2[RP, cs], op=ALU.mult), reads=[PSb[i2], brope], writes=[bt2])
                S.op("pool", lambda e: e.tensor_tensor(out=krope[RP, cs], in0=t1[RP, :], in1=t2[RP, :], op=ALU.add), reads=[bt1, bt2], writes=[bcq])

            S.barrier()
            s3.close()
            wuq = sb2("wuq", [128, 2, 768], BF16)
            wuqr = sb2("wuqr", [128, 2, 8, 32], BF16)
            wukv = sb2("wukv", [128, 1024], BF16)
            bwu = Buf()
            S.dma(wuq[:], A["w_uq"], writes=[bwu], eng="pool")
            S.dma(wukv[:], A["w_ukv"], writes=[bwu], eng="pool")
            wuq4 = wuq[:, :, :].rearrange("p m (h c) -> p m h c", h=8)
            for m in range(2):
                S.op("pool", lambda e: e.tensor_scalar(out=wuqr[:, m, :, 0:16], in0=wuq4[:, m, :, 80:96], scalar1=-1.0, scalar2=None, op0=ALU.mult), reads=[bwu], writes=[bwu])
                S.op("pool", lambda e: e.tensor_copy(out=wuqr[:, m, :, 16:32], in_=wuq4[:, m, :, 64:80]), reads=[bwu], writes=[bwu])
            VAm = sb2("VAm", [128, 16, 8, 128], BF16)
            bVA = Buf()
            S.op("pool", lambda e: e.memset(VAm[:], 1.0), writes=[bVA])
            wukv3 = wukv[:, :].rearrange("p (h c) -> p h c", h=8)
            for kt in range(16):
                ib = nb()
                S.op("pe", lambda e: e.matmul(PS[ib][:, :], lhsT=ckvn[:, kt * 128:(kt + 1) * 128], rhs=wukv3[:, :, 64:128], start=True, stop=True), reads=[bcq, bwu], writes=[PSb[ib]])
                S.op("act", lambda e: e.copy(out=VAm[:, kt, :, 0:64], in_=PS[ib][:, :].rearrange("p (h c) -> p h c", h=8)), reads=[PSb[ib]], writes=[bVA])
            qnt = [sb2(f"qnt{i}", [128, T], BF16) for i in range(2)]
            knt = [sb2(f"knt{i}", [128, T], BF16) for i in range(2)]
            negc1 = sb2("negc1", [128, 128], BF16)
            S.dma(negc1[:], A["negc"][:, 0:128], writes=[cst], eng="pool")
            bqk = [Buf() for _ in range(2)]
            PT = [sb2(f"PT{i}", [128, 512], BF16) for i in range(4)]
            bPT = [Buf() for _ in range(4)]
            rcm = sb2("rcm", [64, 512], F32)
            brcm = Buf()
            def proj(h, c):
                hk = h % 2
                qn, kn = qnt[hk], knt[hk]
                cs = slice(c * 512, (c + 1) * 512)
                if c == 0:
                    S.op("pool", lambda e: e.tensor_copy(out=kn[RP, :], in_=krope[RP, :]), reads=[bcq], writes=[bqk[hk]])
                for m in range(2):
                    S.op("pe", lambda e: e.matmul(PS[5][0:96, :], lhsT=wuq[:, m, h * 96:h * 96 + 96], rhs=cqn[:, m, cs], start=(m == 0), stop=(m == 1)), reads=[bwu, bcq], writes=[PSb[5]])
                S.op("act", lambda e: e.mul(out=qn[0:64, cs], in_=PS[5][0:64, :], mul=SC_M), reads=[PSb[5]], writes=[bqk[hk]])
                S.op("dve", lambda e: e.scalar_tensor_tensor(out=t1[RP, :], in0=PS[5][64:96, :], scalar=SC_M, in1=cos2[RP, cs], op0=ALU.mult, op1=ALU.mult), reads=[PSb[5], brope], writes=[bt1])
                S.op("pe", lambda e: e.matmul(PS[6][0:64, :], lhsT=wukv[:, h * 128:h * 128 + 64], rhs=ckvn[:, cs], start=True, stop=True), reads=[bwu, bcq], writes=[PSb[6]])
                for m in range(2):
                    S.op("pe", lambda e: e.matmul(PS[6][64:96, :], lhsT=wuqr[:, m, h, :], rhs=cqn[:, m, cs], start=(m == 0), stop=(m == 1), tile_position=(0, 64)), reads=[bwu, bcq], writes=[PSb[6]])
                S.op("dve", lambda e: e.tensor_copy(out=kn[0:64, cs], in_=PS[6][0:64, :]), reads=[PSb[6]], writes=[bqk[hk]])
                S.op("dve", lambda e: e.scalar_tensor_tensor(out=t2[RP, :], in0=PS[6][64:96, :], scalar=SC_M, in1=sin2[RP, cs], op0=ALU.mult, op1=ALU.mult), reads=[PSb[6], brope], writes=[bt2])
                S.op("pool", lambda e: e.tensor_tensor(out=qn[RP, cs], in0=t1[RP, :], in1=t2[RP, :], op=ALU.add), reads=[bt1, bt2], writes=[bqk[hk]])

            def attn(h):
                hk = h % 2
                qn, kn = qnt[hk], knt[hk]
                pairs = []
                for c in range(4):
                    cs = slice(c * 512, (c + 1) * 512)
                    ob = 3 + (c % 2)

                    def fin(c=c, cs=cs, ob=ob):
                        S.op("dve", lambda e: e.reciprocal(out=rcm[:], in_=PS[ob][64:128, :]), reads=[PSb[ob]], writes=[brcm])
                        S.op("dve", lambda e: e.tensor_tensor(out=omla[hk * 64:hk * 64 + 64, h // 2, cs], in0=PS[ob][0:64, :], in1=rcm[:], op=ALU.mult), reads=[PSb[ob], brcm], writes=[bomla])
                        if h + 1 < 8:
                            proj(h + 1, c)

                    nk = 4 * c + 4
                    for kt in range(nk):
                        ks = slice(kt * 128, (kt + 1) * 128)
                        lo = 128 * (kt - 4 * c) if kt >= 4 * c else 0
                        smm = [(kn[0:96, ks], qn[0:96, c * 512 + lo:(c + 1) * 512], [bqk[hk]], lo, 512)]
                        if kt >= 4 * c:
                            smm.append((ident[:], negc1[:], [bid, cst], lo, lo + 128))
                        pairs.append(dict(smm=smm, cols=(lo, 512), v=(VAm[:, kt, h, :], [bVA]), O=(ob, kt == 0, kt == nk - 1), fin=fin if kt == nk - 1 else None))
                attn_stream(S, pairs, PS, PSb, PT, bPT)

            for c in range(4):
                proj(0, c)
            for h in range(8):
                attn(h)
            if dbg:
                S.dma(A["d_omla"], omla[:], reads=[bomla], eng="pool")
        S.barrier()
        if os.environ.get('SKIP_NSA') is None:
          stage_nsa(nc, S, A, sq, PS, PSb, PB, PBb, ident, bid, s1, hT, bhT, onsa, bonsa, cst, negc, negw, selg, cmptbl, v12, dbg, nb, onesf)
        S.barrier()
        if os.environ.get('SKIP_MERGE') is None:
          stage_merge(nc, S, A, sq, PS, PSb, PB, PBb, ident, bid, hT, bhT, onsa, bonsa, omla, bomla, x1b, nb, junk, bjunk)


def stage_nsa(nc, S, A, sq, PS, PSb, PB, PBb, ident, bid, s1, hT, bhT, onsa, bonsa, cst, negc, negw, selg, cmptbl, v12, dbg, nb, onesf):
    with ExitStack() as s2:
        sb = lambda name, shape, dt: s2.enter_context(_sbt(nc, name, shape, dt))
        QA = [sb(f"QA{g}", [128, 16 * 512], BF16) for g in range(2)]
        bQA = [[Buf() for _ in range(16)] for g in range(2)]
        BR = ("sel", "win")
        KA = {(b, g): sb(f"KA{b}{g}", [128, T], BF16) for b in BR for g in range(2)}
        bKA = {k: Buf() for k in KA}
        VA = {(b, g): sb(f"VA{b}{g}", [128, 16, 128], BF16) for b in BR for g in range(2)}
        bVA = {k: Buf() for k in VA}
        gsig = sb("gsig", [24, T], BF16)
        bgs = Buf()
        KC = [sb(f"KC{g}", [128, 128], BF16) for g in range(2)]
        VC = [sb(f"VC{g}", [128, 64], BF16) for g in range(2)]
        bKC = [Buf() for _ in range(2)]
        bVC = [Buf() for _ in range(2)]
        for g in range(2):
            S.op("pool", lambda e: e.memset(QA[g][64:96, :], 0.0), writes=bQA[g])
            S.dma(QA[g][96:100, :], A["qaug"][g, 32:36, :], writes=bQA[g], eng="pool")
            S.op("pool", lambda e: e.memset(KC[g][:], 0.0), writes=[bKC[g]])
            S.dma(KC[g][64:100, :], A["kaug_cmp"], writes=[bKC[g]], eng="pool")
            S.op("pool", lambda e: e.memset(VC[g][:], 0.0), writes=[bVC[g]])
            for b in BR:
                S.dma(KA[(b, g)][64:100, :], A["kaug_" + b], writes=[bKA[(b, g)]], eng="pool")
                S.op("pool", lambda e: e.memset(VA[(b, g)][:], 1.0), writes=[bVA[(b, g)]])
        NSTOP = int(os.environ.get('NSA_STOP', '99'))
        if NSTOP <= 1:
            S.barrier()
            return
        with ExitStack() as s3:
            sb3 = lambda name, shape, dt: s3.enter_context(_sbt(nc, name, shape, dt))
            wN = sb3("wN", [128, 8, 1304], BF16)
            bwN = Buf()
            S.dma(wN[:], A["w_in"][:, :, 0:1304], writes=[bwN], eng="pool")
            kcT = {(kd, g): sb3(f"kcT{kd}{g}", [64, T], BF16) for kd in range(2) for g in range(2)}
            bkc = {k: Buf() for k in kcT}
            for c in range(4):
                cs = slice(c * 512, (c + 1) * 512)
                for n in range(4):
                    ib = nb()
                    for kc in range(8):
                        S.op("pe", lambda e: e.matmul(PS[ib][:, :], lhsT=wN[:, kc, n * 128:(n + 1) * 128], rhs=hT[:, kc, cs], start=(kc == 0), stop=(kc == 7)), reads=[bwN, bhT[c]], writes=[PSb[ib]])
                    g = n // 2
                    rr = (2 * n) % 4
                    QAv = QA[g][:, :].rearrange("p (i r t) -> p i r t", i=16, r=4)
                    S.op("act", lambda e: e.mul(out=QAv[0:64, 4 * c:4 * c + 4, rr, :], in_=PS[ib][0:64, :].rearrange("p (i t) -> p i t", i=4), mul=SC_N), reads=[PSb[ib]], writes=bQA[g][4 * c:4 * c + 4])
                    S.op("dve", lambda e: e.tensor_scalar(out=QAv[0:64, 4 * c:4 * c + 4, rr + 1, :], in0=PS[ib][64:128, :].rearrange("p (i t) -> p i t", i=4), scalar1=SC_N, scalar2=None, op0=ALU.mult), reads=[PSb[ib]], writes=bQA[g][4 * c:4 * c + 4])
                for kind in (0, 1, 2, 4):
                    ib = nb()
                    col0 = 512 + kind * 128
                    for kc in range(8):
                        S.op("pe", lambda e: e.matmul(PS[ib][:, :], lhsT=wN[:, kc, col0:col0 + 128], rhs=hT[:, kc, cs], start=(kc == 0), stop=(kc == 7)), reads=[bwN, bhT[c]], writes=[PSb[ib]])
                    for g in range(2):
                        if kind < 2:
                            dst, bd = kcT[(kind, g)][0:64, cs], bkc[(kind, g)]
                        else:
                            key = ("sel" if kind == 2 else "win", g)
                            dst, bd = KA[key][0:64, cs], bKA[key]
                        if g == 0:
                            S.op("act", lambda e: e.copy(out=dst, in_=PS[ib][0:64, :]), reads=[PSb[ib]], writes=[bd])
                        else:
                            S.op("dve", lambda e: e.tensor_copy(out=dst, in_=PS[ib][64:128, :]), reads=[PSb[ib]], writes=[bd])
                ib = nb()
                for kc in range(8):
                    S.op("pe", lambda e: e.matmul(PS[ib][0:24, :], lhsT=wN[:, kc, 1280:1304], rhs=hT[:, kc, cs], start=(kc == 0), stop=(kc == 7)), reads=[bwN, bhT[c]], writes=[PSb[ib]])
                S.op("act", lambda e: e.activation(out=gsig[:, cs], in_=PS[ib][0:24, :], func=ACT.Sigmoid), reads=[PSb[ib]], writes=[bgs])
            for kt in range(int(os.environ.get('NKT', '16')) if NSTOP > 2 else 0):
                ib = nb()
                ts = slice(kt * 128, (kt + 1) * 128)
                for q, col0 in enumerate((896, 1152)):
                    for kc in range(8):
                        S.op("pe", lambda e: e.matmul(PS[ib][:, q * 128:(q + 1) * 128], lhsT=hT[:, kc, ts], rhs=wN[:, kc, col0:col0 + 128], start=(kc == 0), stop=(kc == 7)), reads=[bwN, bhT[kt // 4]], writes=[PSb[ib]])
                for q, b in enumerate(BR):
                    for g in range(int(os.environ.get('VCOPY', '2'))):
                        if g == 0:
                            S.op("act", lambda e: e.copy(out=VA[(b, g)][:, kt, 0:64], in_=PS[ib][:, q * 128 + g * 64: q * 128 + g * 64 + 64]), reads=[PSb[ib]], writes=[bVA[(b, g)]])
                        else:
                            S.op("dve", lambda e: e.tensor_copy(out=VA[(b, g)][:, kt, 0:64], in_=PS[ib][:, q * 128 + g * 64: q * 128 + g * 64 + 64]), reads=[PSb[ib]], writes=[bVA[(b, g)]])
            for kd, (w1n, b1n, w2n, posn) in enumerate((("ck_w1", "ck_b1", "ck_w2", "posT_k"), ("cv_w1", "cv_b1", "cv_w2", "posT_v"))[:int(os.environ.get('NCMP', '2'))]):
                W1 = sb3(f"W1{kd}", [64, 32, 64], BF16)
                posT = sb3(f"posT{kd}", [64, 32], BF16)
                b1 = sb3(f"b1{kd}", [64, 1], F32)
                W2 = sb3(f"W2{kd}", [64, 64], BF16)
                bias = sb3(f"bias{kd}", [64, 1], F32)
                bcw = Buf()
                bbias = Buf()
                S.dma(W1[:], A[w1n], writes=[bcw], eng="pool")
                S.dma(posT[:], A[posn], writes=[bcw], eng="pool")
                S.dma(W2[:], A[w2n], writes=[bcw], eng="pool")
                S.dma(b1[:], A[b1n], writes=[bcw])
                ip = nb()
                for l in range(32):
                    S.op("pe", lambda e: e.matmul(PS[ip][0:64, 0:1], lhsT=W1[:, l, :], rhs=posT[:, l:l + 1], start=(l == 0), stop=(l == 31)), reads=[bcw], writes=[PSb[ip]])
                S.op("dve", lambda e: e.tensor_tensor(out=bias[:], in0=PS[ip][0:64, 0:1], in1=b1[:], op=ALU.add), reads=[PSb[ip], bcw], writes=[bbias])
                for g in range(2):
                    G = sb3(f"G{kd}{g}", [64, 128], BF16)
                    bG = Buf()
                    S.op("pool", lambda e: e.memset(G[:], 0.0), writes=[bG])
                    ia = nb()
                    for l in range(32):
                        S.op("pe", lambda e: e.matmul(PS[ia][0:64, 0:127], lhsT=W1[:, l, :], rhs=kcT[(kd, g)][0:64, l:l + 2017:16], start=(l == 0), stop=(l == 31)), reads=[bcw, bkc[(kd, g)]], writes=[PSb[ia]])
                    S.op("act", lambda e: e.activation(out=G[:, 0:127], in_=PS[ia][0:64, 0:127], func=ACT.Gelu_apprx_tanh, bias=bias[:, 0:1]), reads=[PSb[ia], bbias], writes=[bG])
                    io = nb()
                    if kd == 0:
                        S.op("pe", lambda e: e.matmul(PS[io][0:64, 0:127], lhsT=W2[:, :], rhs=G[:, 0:127], start=True, stop=True), reads=[bcw, bG], writes=[PSb[io]])
                        S.op("act", lambda e: e.copy(out=KC[g][0:64, 0:127], in_=PS[io][0:64, 0:127]), reads=[PSb[io]], writes=[bKC[g]])
                    else:
                        S.op("pe", lambda e: e.matmul(PS[io][0:127, 0:64], lhsT=G[:, 0:127], rhs=W2[:, :], start=True, stop=True), reads=[bcw, bG], writes=[PSb[io]])
                        S.op("act", lambda e: e.copy(out=VC[g][0:127, :], in_=PS[io][0:127, 0:64]), reads=[PSb[io]], writes=[bVC[g]])
            S.dma(A["gsd"], gsig[:], reads=[bgs], writes=[A["gsd_buf"]])
            S.barrier()
        f32t = lambda name, shape: sb(name, shape, F32)
        sc = f32t("sc", [128, 512])
        ex = f32t("ex", [128, 512])
        pp = f32t("pp", [128, 512])
        pbf = sb("pbf", [128, 512], BF16)
        PTc = sb("PTc", [128, 512], BF16)
        sm = f32t("sm", [128, 8])
        t1i = f32t("t1i", [128, 4, 32])
        imp = f32t("imp", [128, 32])
        score = f32t("score", [128, 32])
        sc2 = f32t("sc2", [128, 32])
        m8 = f32t("m8", [128, 16])
        Z = sb("Z", [128, 128], BF16)
        rcs = f32t("rcs", [64, 512])
        cfm = f32t("cfm", [64, 512])
        tb = f32t("tb", [64, 512])
        acc = f32t("acc", [64, 512])
        Ocs = [f32t(f"Ocs{q}", [64, 512]) for q in range(2)]
        bOcs = [Buf() for _ in range(2)]
        bsc, bex, bpp, bpbf, bPTc, bsm, bsel, bZ, brcs, bcfm, btb, bacc = [Buf() for _ in range(12)]
        PT = [sb(f"PTn{i}", [128, 512], BF16) for i in range(4)]
        bPT = [Buf() for _ in range(4)]
        S.op("pool", lambda e: e.memset(Z[:], 0.0), writes=[bZ])
        v3 = lambda t: t[:, :].rearrange("p (r j) -> p r j", r=4)
        tiles = [(g, i) for g in range(2) for i in range(int(os.environ.get('NSA_NI', '16')))]
        SROT = (0, 1)

        def step1(n):
            g, i = tiles[n]
            nj = 8 * i + 8
            for r in range(4):
                S.op("pe", lambda e: e.matmul(PS[6][:, r * 128:r * 128 + nj], lhsT=QA[g][0:100, (4 * i + r) * 128:(4 * i + r + 1) * 128], rhs=KC[g][0:100, 0:nj], start=True, stop=True), reads=[bQA[g][i], bKC[g]], writes=[PSb[6]])

        def step3(n):
            g, i = tiles[n]
            nj = 8 * i + 8
            S.op("dve", lambda e: e.tensor_tensor(out=v3(sc)[:, :, 0:nj], in0=v3(PS[6])[:, :, 0:nj], in1=cmptbl[:, 120 - 8 * i:120 - 8 * i + nj].unsqueeze(1).to_broadcast([128, 4, nj]), op=ALU.add), reads=[PSb[6], cst], writes=[bsc])
            S.op("act", lambda e: e.activation(out=v3(ex)[:, :, 0:nj], in_=v3(sc)[:, :, 0:nj], func=ACT.Exp), reads=[bsc], writes=[bex])
            S.op("dve", lambda e: e.tensor_reduce(out=sm[:, 0:4], in_=v3(ex)[:, :, 0:nj], axis=AX.X, op=ALU.add), reads=[bex], writes=[bsm])
            S.op("dve", lambda e: e.tensor_scalar(out=sm[:, 0:4], in0=sm[:, 0:4], scalar1=1e-30, scalar2=None, op0=ALU.max), reads=[bsm], writes=[bsm])
            S.op("dve", lambda e: e.reciprocal(out=sm[:, 4:8], in_=sm[:, 0:4]), reads=[bsm], writes=[bsm])
            S.op("dve", lambda e: e.tensor_tensor(out=v3(pp)[:, :, 0:nj], in0=v3(ex)[:, :, 0:nj], in1=sm[:, 4:8].unsqueeze(2).to_broadcast([128, 4, nj]), op=ALU.mult), reads=[bex, bsm], writes=[bpp])
            S.op("pool", lambda e: e.tensor_copy(out=v3(pbf)[:, :, 0:nj], in_=v3(pp)[:, :, 0:nj]), reads=[bpp], writes=[bpbf])

        def step5(n):
            g, i = tiles[n]
            nj = 8 * i + 8
            for r in range(4):
                S.op("pe", lambda e: e.transpose(out=PB[0:nj, r * 128:(r + 1) * 128], in_=pbf[:, r * 128:r * 128 + nj], identity=ident[:]), reads=[bpbf, bid], writes=[PBb])
            S.op("act", lambda e: e.copy(out=PTc[0:nj, :], in_=PB[0:nj, 0:512]), reads=[PBb], writes=[bPTc])
            S.op("pe", lambda e: e.matmul(PS[6][0:64, :], lhsT=VC[g][0:nj, 0:64], rhs=PTc[0:nj, :], start=True, stop=True), reads=[bVC[g], bPTc], writes=[PSb[6]])
            S.op("act", lambda e: e.copy(out=Ocs[n % 2][:], in_=PS[6][0:64, :]), reads=[PSb[6]], writes=[bOcs[n % 2]])

        def step6(n):
            g, i = tiles[n]
            if i < 8:
                return
            nj = 8 * i + 8
            QAv = QA[g][:, :].rearrange("p (i r t) -> p i r t", i=16, r=4)
            nsb = nj // 4
            S.op("dve", lambda e: e.tensor_reduce(out=t1i[:, :, 0:nsb], in_=v3(pp)[:, :, 0:nj].rearrange("p r (s q) -> p r s q", q=4), axis=AX.X, op=ALU.add), reads=[bpp], writes=[bsel])
            S.op("dve", lambda e: e.tensor_reduce(out=imp[:, 0:nsb], in_=t1i[:, :, 0:nsb].rearrange("p r s -> p s r"), axis=AX.X, op=ALU.add), reads=[bsel], writes=[bsel])
            S.op("pool", lambda e: e.memset(score[:], -BIG), reads=[bsel], writes=[bsel])
            S.op("dve", lambda e: e.tensor_copy(out=score[:, 0:2 * i], in_=imp[:, 0:2 * i]), reads=[bsel], writes=[bsel])
            S.op("dve", lambda e: e.tensor_scalar(out=score[:, 2 * i - 1:2 * i], in0=imp[:, 2 * i - 1:2 * i], scalar1=v12[:, 1:2], scalar2=None, op0=ALU.add), reads=[bsel, cst], writes=[bsel])
            S.op("pool", lambda e: e.memset(score[:, 0:1], BIG), reads=[bsel], writes=[bsel])
            S.op("pool", lambda e: e.memset(score[:, 2 * i:2 * i + 1], BIG), reads=[bsel], writes=[bsel])
            S.op("dve", lambda e: e.tensor_copy(out=score[:, 2 * i + 1:2 * i + 2], in_=v12[:, 0:1]), reads=[bsel, cst], writes=[bsel])
            S.op("dve", lambda e: e.max(out=m8[:, 0:8], in_=score[:]), reads=[bsel], writes=[bsel])
            S.op("dve", lambda e: e.match_replace(out=sc2[:], in_to_replace=m8[:, 0:8], in_values=score[:], imm_value=-3.0e9), reads=[bsel], writes=[bsel])
            S.op("dve", lambda e: e.max(out=m8[:, 8:16], in_=sc2[:]), reads=[bsel], writes=[bsel])
            S.op("dve", lambda e: e.tensor_scalar(out=Z[:, 64:96], in0=score[:], scalar1=m8[:, 15:16], scalar2=NEG, op0=ALU.is_lt, op1=ALU.mult), reads=[bsel], writes=[bZ])
            S.op("pe", lambda e: e.transpose(out=PB[:, 512:640], in_=Z[:, :], identity=ident[:]), reads=[bZ, bid], writes=[PBb])
            S.op("act", lambda e: e.copy(out=QAv[64:96, i, :, :], in_=PB[64:96, 512:640].unsqueeze(1).to_broadcast([32, 4, 128])), reads=[PBb], writes=[bQA[g][i]])

        def stream(n, br):
            g, i = tiles[n]
            rhsQ = QA[g][0:100, i * 512:(i + 1) * 512]
            ob = (2 if br == "sel" else 4) + (n % 2)
            k0 = max(0, i - 4) if br == "win" else 0
            pairs = []
            for kt in range(k0, i + 1):
                ks = slice(kt * 128, (kt + 1) * 128)
                smm = [(KA[(br, g)][0:100, ks], rhsQ, [bKA[(br, g)], bQA[g][i]], 0, 512)]
                if kt == i:
                    smm.append((ident[:], negc[:], [bid, cst], 0, 512))
                elif br == "win" and kt == i - 4:
                    smm.append((ident[:], negw[:], [bid, cst], 0, 512))
                pairs.append(dict(smm=smm, v=(VA[(br, g)][:, kt, :], [bVA[(br, g)]]), O=(ob, kt == k0, kt == i), fin=None))
            attn_stream(S, pairs, PS, PSb, PT, bPT, srot=SROT, skew=1)

        gsb = [[sb(f"gsb{q}{b}", [64, 512], BF16) for b in range(3)] for q in range(2)]
        bgsb = [[Buf() for b in range(3)] for q in range(2)]
        rcw = [f32t(f"rcw{q}", [64, 512]) for q in range(2)]
        rcq = [f32t(f"rcq{q}", [64, 512]) for q in range(2)]
        brcw = [Buf() for _ in range(2)]
        brcq = [Buf() for _ in range(2)]
        tb2 = f32t("tb2", [64, 512])
        btb2 = Buf()

        def recip(n, br):
            ob = (2 if br == "sel" else 4) + (n % 2)
            dst, bd = (rcq, brcq) if br == "sel" else (rcw, brcw)
            S.op("dve", lambda e: e.reciprocal(out=dst[n % 2][:], in_=PS[ob][64:128, :]), reads=[PSb[ob]], writes=[bd[n % 2]])

        def gates(n):
            g, i = tiles[n]
            for b in range(3):
                r0_ = b * 8 + 4 * g
                src = A["gsd"][r0_:r0_ + 4, i * 128:(i + 1) * 128].unsqueeze(0).to_broadcast([64, 4, 128])
                S.dma(gsb[n % 2][b][:, :].rearrange("p (r t) -> p r t", r=4), src, reads=[A["gsd_buf"]], writes=[bgsb[n % 2][b]])

        def combine(n):
            g, i = tiles[n]
            q = n % 2
            ts = slice(i * 128, (i + 1) * 128)
            S.op("pool", lambda e: e.tensor_tensor(out=acc[:], in0=Ocs[q][:], in1=gsb[q][0][:], op=ALU.mult), reads=[bOcs[q], bgsb[q][0]], writes=[bacc])
            S.op("pool", lambda e: e.tensor_tensor(out=rcq[q][:], in0=rcq[q][:], in1=gsb[q][1][:], op=ALU.mult), reads=[bgsb[q][1]], writes=[brcq[q]])
            S.op("pool", lambda e: e.tensor_tensor(out=rcw[q][:], in0=rcw[q][:], in1=gsb[q][2][:], op=ALU.mult), reads=[bgsb[q][2]], writes=[brcw[q]])
            S.op("dve", lambda e: e.tensor_tensor(out=tb[:], in0=PS[2 + q][0:64, :], in1=rcq[q][:], op=ALU.mult), reads=[PSb[2 + q], brcq[q]], writes=[btb])
            S.op("dve", lambda e: e.tensor_tensor(out=tb2[:], in0=PS[4 + q][0:64, :], in1=rcw[q][:], op=ALU.mult), reads=[PSb[4 + q], brcw[q]], writes=[btb2])
            S.op("pool", lambda e: e.tensor_tensor(out=acc[:], in0=acc[:], in1=tb[:], op=ALU.add), reads=[btb], writes=[bacc])
            S.op("pool", lambda e: e.tensor_tensor(out=acc[:], in0=acc[:], in1=tb2[:], op=ALU.add), reads=[btb2], writes=[bacc])
            a3 = acc[:, :].rearrange("p (r t) -> p r t", r=4)
            S.op("pool", lambda e: e.tensor_copy(out=onsa[0:64, 2 * g:2 * g + 2, ts], in_=a3[:, 0::2, :]), reads=[bacc], writes=[bonsa])
            S.op("dve", lambda e: e.tensor_copy(out=onsa[64:128, 2 * g:2 * g + 2, ts], in_=a3[:, 1::2, :]), reads=[bacc], writes=[bonsa])

        NT = len(tiles)
        for n in range(NT + 1):
            cur = n if n < NT else None
            prev = n - 1 if n >= 1 else None
            if cur is not None:
                gates(cur)
            if cur is not None:
                step1(cur)
            if prev is not None:
                stream(prev, "win")
            if cur is not None:
                step3(cur)
            if prev is not None:
                stream(prev, "sel")
                recip(prev, "win")
            if cur is not None:
                step5(cur)
                step6(cur)
            if prev is not None:
                recip(prev, "sel")
                combine(prev)
        if dbg:
            S.dma(A["d_onsa"], onsa[:], reads=[bonsa], eng="pool")
        S.barrier()


def stage_merge(nc, S, A, sq, PS, PSb, PB, PBb, ident, bid, hT, bhT, onsa, bonsa, omla, bomla, x1b, nb, junk, bjunk):
    r0 = sq * T
    with ExitStack() as s2:
        sb = lambda name, shape, dt: s2.enter_context(_sbt(nc, name, shape, dt))
        wbn = sb("wbn", [128, 4, D], BF16)
        wbm = sb("wbm", [128, 4, D], BF16)
        wo = sb("wo", [128, 8, D], BF16)
        wmg = sb("wmg", [128, 8, 2048], BF16)
        gpo = sb("gpo", [128, D], F32)
        bw = Buf()
        S.dma(wmg[:], A["w_in"][:, :, 1720:3768], writes=[bw], eng="pool")
        S.dma(wbn[:], A["w_br_nsa"], writes=[bw], eng="pool")
        S.dma(wbm[:], A["w_br_mla"], writes=[bw], eng="pool")
        S.dma(wo[:], A["w_o"], writes=[bw], eng="pool")
        S.dma(gpo[:], A["g_post_mix"][0:1, :].to_broadcast([128, D]), writes=[bw])
        R = 4
        xt = [sb(f"xtm{i}", [128, D], F32) for i in range(R)]
        s0 = [sb(f"s0{i}", [128, 512], F32) for i in range(2)]
        m0 = [sb(f"m0{i}", [128, 512], F32) for i in range(2)]
        mgb = [sb(f"mgb{i}", [128, D], BF16) for i in range(R)]
        mgT = [sb(f"mgT{i}", [128, 8, 128], BF16) for i in range(R)]
        tmp = [sb(f"tmpm{i}", [128, D], F32) for i in range(R)]
        ssm = sb("ssm", [128, 16], F32)
        bxt, bmgb, bmgT, btmp, bssm = [[Buf() for _ in range(R)] for _ in range(5)]
        bs0 = [Buf() for _ in range(2)]
        bm0 = [Buf() for _ in range(2)]
        nb5c = [0]

        def nb5():
            nb5c[0] = (nb5c[0] + 1) % 5
            return nb5c[0]

        def partA(i):
            k = i % R
            ts = slice(i * 128, (i + 1) * 128)
            S.dma(xt[k][:], A["x"][r0 + 128 * i: r0 + 128 * (i + 1), :], writes=[bxt[k]])
            for hf in range(2):
                cs = slice(hf * 512, (hf + 1) * 512)
                for br, (wb_, osrc, bo) in enumerate(((wbn, onsa, bonsa), (wbm, omla, bomla))):
                    ig = nb5()
                    for kc in range(8):
                        S.op("pe", lambda e: e.matmul(PS[ig][:, :], lhsT=hT[:, kc, ts], rhs=wmg[:, kc, br * 1024 + hf * 512: br * 1024 + (hf + 1) * 512], start=(kc == 0), stop=(kc == 7)), reads=[bhT[i // 4], bw], writes=[PSb[ig]])
                    ibr = nb5()
                    for c in range(4):
                        S.op("pe", lambda e: e.matmul(PS[ibr][:, :], lhsT=osrc[:, c, ts], rhs=wb_[:, c, cs], start=(c == 0), stop=(c == 3)), reads=[bo, bw], writes=[PSb[ibr]])
                    S.op("act", lambda e: e.activation(out=s0[br][:], in_=PS[ig][:, :], func=ACT.Sigmoid), reads=[PSb[ig]], writes=[bs0[br]])
                    S.op("dve", lambda e: e.tensor_tensor(out=m0[br][:], in0=PS[ibr][:, :], in1=s0[br][:], op=ALU.mult), reads=[PSb[ibr], bs0[br]], writes=[bm0[br]])
                S.op("pool", lambda e: e.tensor_tensor(out=mgb[k][:, cs], in0=m0[0][:], in1=m0[1][:], op=ALU.add), reads=[bm0[0], bm0[1]], writes=[bmgb[k]])

        iy = [5, 6]

        def partB1(i):
            k = i % R
            for kc in range(8):
                S.op("pe", lambda e: e.transpose(out=PB[:, kc * 128:(kc + 1) * 128], in_=mgb[k][:, kc * 128:(kc + 1) * 128], identity=ident[:]), reads=[bmgb[k], bid], writes=[PBb])
            S.op("act", lambda e: e.copy(out=mgT[k][:], in_=PB[:, :].rearrange("p (k t) -> p k t", k=8)), reads=[PBb], writes=[bmgT[k]])
            for hf in range(2):
                for kc in range(8):
                    S.op("pe", lambda e: e.matmul(PS[iy[hf]][:, :], lhsT=mgT[k][:, kc, :], rhs=wo[:, kc, hf * 512:(hf + 1) * 512], start=(kc == 0), stop=(kc == 7)), reads=[bmgT[k], bw], writes=[PSb[iy[hf]]])
                S.op("act", lambda e: e.activation(out=junk[:, 0:512], in_=PS[iy[hf]][:, :], func=ACT.Square, accum_out=ssm[:, 4 * k + hf:4 * k + hf + 1]), reads=[PSb[iy[hf]]], writes=[bjunk, bssm[k]])

        def partB2(i):
            k = i % R
            S.op("dve", lambda e: e.tensor_tensor(out=ssm[:, 4 * k + 2:4 * k + 3], in0=ssm[:, 4 * k:4 * k + 1], in1=ssm[:, 4 * k + 1:4 * k + 2], op=ALU.add), reads=[bssm[k]], writes=[bssm[k]])
            rstd_from_ss(S, ssm[:, 4 * k + 2:4 * k + 3], ssm[:, 4 * k + 3:4 * k + 4], D, [bssm[k]], [bssm[k]])
            for hf in range(2):
                S.op("dve", lambda e: e.scalar_tensor_tensor(out=tmp[k][:, hf * 512:(hf + 1) * 512], in0=PS[iy[hf]][:, :], scalar=ssm[:, 4 * k + 3:4 * k + 4], in1=gpo[:, hf * 512:(hf + 1) * 512], op0=ALU.mult, op1=ALU.mult), reads=[PSb[iy[hf]], bssm[k], bw], writes=[btmp[k]])
            S.op("pool", lambda e: e.tensor_tensor(out=tmp[k][:], in0=tmp[k][:], in1=xt[k][:], op=ALU.add), reads=[bxt[k]], writes=[btmp[k]])
            S.dma(A["x1s"][r0 + 128 * i: r0 + 128 * (i + 1), :], tmp[k][:], reads=[btmp[k]], writes=[x1b[16 * sq + i]], eng="pool")

        for t in range(16 + 2):
            if 0 <= t - 2 < 16:
                partB2(t - 2)
            if 0 <= t - 1 < 16:
                partB1(t - 1)
            if t < 16:
                partA(t)


def kernel(**inp):
    inp = {k: np.asarray(v) for k, v in inp.items()}
    nc = build()
    consts = make_consts()
    w = host_weights(inp)
    in_maps = []
    for c in range(NCORES):
        m = {"x": np.ascontiguousarray(inp["x"][2 * c:2 * c + 2].reshape(2 * T, D)),
             "p": np.ascontiguousarray(inp["p"][0, 2 * c:2 * c + 2].reshape(2 * T, 256)),
             "pos": np.ascontiguousarray(inp["positions"][2 * c:2 * c + 2].astype(np.int32))}
        m.update(w)
        for k, v in consts.items():
            m["c_" + k] = v
        in_maps.append(m)
    res = run_bass_kernel_spmd(nc, in_maps, core_ids=list(range(NCORES)))
    out = np.stack([r["out"].reshape(2, T, D) for r in res.results], 0).reshape(16, T, D)
    return out.astype(np.float32)
```
